# Optimizing a Trainium2 kernel written in Bass

```python
import math
import jax, jax.numpy as jnp
from jax import lax
import numpy as np

D_MODEL = 1024
BATCH = 32
SEQ = 256
DEPTH = 4
DEC_BATCH = 4
DEC_SEQ = 2048
PAST_LEN = 512

GRID_W = 64
HEAD_DIM = 64
AXIS_DIM = HEAD_DIM // 2
ROPE_THETA = 10000.0
QBLK = 128
GQA_HEADS = 8
GQA_KV_HEADS = 2
DIFF_HEADS = 4
RWKV_HEADS = D_MODEL // HEAD_DIM
DECAY_LORA = 64
AAA_LORA = 64
GATE_LORA = 128
D_FF = 4 * D_MODEL
N_ATTN_LAYERS = (DEPTH + 1) // 2
N_RWKV_LAYERS = DEPTH // 2
GQA_Q = GQA_HEADS * HEAD_DIM
GQA_KV = GQA_KV_HEADS * HEAD_DIM
DIFF_QK = DIFF_HEADS * 2 * HEAD_DIM
DIFF_V = DIFF_HEADS * 2 * HEAD_DIM
ATTN_IN = GQA_Q + 2 * GQA_KV + 2 * DIFF_QK + DIFF_V
ATTN_SPLITS = (GQA_Q, GQA_Q + GQA_KV, GQA_Q + 2 * GQA_KV, GQA_Q + 2 * GQA_KV + DIFF_QK, GQA_Q + 2 * GQA_KV + 2 * DIFF_QK)
MIX_WIDTH = GQA_Q + DIFF_V
NORM_EPS = 1e-6
LNX_EPS = 64e-5

kernel_name = 'hybrid_dit_gqa_diffattn_birwkv7_step'

F32 = jnp.float32


def rms_norm(x, g):
    xf = x.astype(F32)
    y = xf * lax.rsqrt(jnp.mean(xf * xf, axis=-1, keepdims=True) + NORM_EPS)
    return y.astype(x.dtype) * g


def adaln(cond, w, b):
    m = jax.nn.silu(cond) @ w + b
    return jnp.split(m[:, None, :], 6, axis=-1)


def axial_rope(rows):
    row = jnp.repeat(jnp.arange(rows), GRID_W).astype(F32)
    col = jnp.tile(jnp.arange(GRID_W), rows).astype(F32)
    inv = 1.0 / (ROPE_THETA ** (jnp.arange(0, AXIS_DIM, 2, dtype=F32) / AXIS_DIM))
    ar = row[:, None] * inv[None, :]
    ac = col[:, None] * inv[None, :]
    ang = jnp.concatenate([ar, ar, ac, ac], axis=-1)
    return jnp.cos(ang), jnp.sin(ang)


def apply_rope(x, cos, sin):
    bshape = (cos.shape[0],) + (1,) * (x.ndim - 3) + (HEAD_DIM,)
    xh = x.reshape(x.shape[:-1] + (2, 2, AXIS_DIM // 2))
    rot = jnp.stack([-xh[..., 1, :], xh[..., 0, :]], axis=-2).reshape(x.shape)
    return (x.astype(F32) * cos.reshape(bshape) + rot.astype(F32) * sin.reshape(bshape)).astype(x.dtype)


def sweep_query_blocks(fn, q):
    B, T = q.shape[:2]
    nb = T // QBLK
    qb = jnp.moveaxis(q.reshape((B, nb, QBLK) + q.shape[2:]), 1, 0)
    out = lax.map(fn, qb)
    return jnp.moveaxis(out, 0, 1).reshape((B, T) + out.shape[3:])


def gqa_attention(q, k, v):
    B, T = q.shape[:2]
    grp = GQA_HEADS // GQA_KV_HEADS
    qg = q.reshape(B, T, GQA_KV_HEADS, grp, HEAD_DIM)
    scale = HEAD_DIM ** -0.5

    def block(qb):
        s = jnp.einsum('bqhgd,bkhd->bhgqk', qb, k).astype(F32) * scale
        p = jax.nn.softmax(s, axis=-1).astype(v.dtype)
        return jnp.einsum('bhgqk,bkhd->bqhgd', p, v)

    return sweep_query_blocks(block, qg).reshape(B, T, GQA_Q)


def diff_attention(q, k, v, lam):
    scale = HEAD_DIM ** -0.5

    def block(qb):
        s = jnp.einsum('bqhcd,bkhcd->bhcqk', qb, k).astype(F32) * scale
        p = jax.nn.softmax(s, axis=-1)
        pd = (p[:, :, 0] - lam * p[:, :, 1]).astype(v.dtype)
        return jnp.einsum('bhqk,bkhe->bqhe', pd, v)

    return sweep_query_blocks(block, q)


def attn_mixer(h, p, lam_init, rope, ctx):
    w_in, w_out, qk_gain, lam_vec, subln_g = p
    B, T, _ = h.shape
    qa, ka, va, qb, kb, vb = jnp.split(h @ w_in, ATTN_SPLITS, axis=-1)
    qa = rms_norm(qa.reshape(B, T, GQA_HEADS, HEAD_DIM), qk_gain[0])
    ka = rms_norm(ka.reshape(B, T, GQA_KV_HEADS, HEAD_DIM), qk_gain[1])
    va = va.reshape(B, T, GQA_KV_HEADS, HEAD_DIM)
    qb = qb.reshape(B, T, DIFF_HEADS, 2, HEAD_DIM)
    kb = kb.reshape(B, T, DIFF_HEADS, 2, HEAD_DIM)
    vb = vb.reshape(B, T, DIFF_HEADS, 2 * HEAD_DIM)
    side = (ka, va, kb, vb)
    if rope is not None:
        cos, sin = rope
        qa, ka, qb, kb = (apply_rope(t, cos, sin) for t in (qa, ka, qb, kb))
    if ctx is not None:
        ka, va, kb, vb = (jnp.concatenate([cc, t], axis=1) for cc, t in zip(ctx, (ka, va, kb, vb)))
    oa = gqa_attention(qa, ka, va)
    lf = lam_vec.astype(F32)
    lam = jnp.exp(jnp.sum(lf[0] * lf[1])) - jnp.exp(jnp.sum(lf[2] * lf[3])) + lam_init
    ob = rms_norm(diff_attention(qb, kb, vb, lam), subln_g) * (1.0 - lam_init)
    out = jnp.concatenate([oa, ob.reshape(B, T, DIFF_V)], axis=-1) @ w_out
    return out, side


def token_shift_centred(h):
    hp = jnp.pad(h[:, :-1], ((0, 0), (1, 0), (0, 0)))
    hn = jnp.pad(h[:, 1:], ((0, 0), (0, 1), (0, 0)))
    return 0.5 * (hp + hn) - h


def wkv_scan(r, decay, k, v, kk, a, s0, reverse):
    seq = tuple(jnp.moveaxis(t.astype(F32), 1, 0) for t in (r, decay, k, v, kk, kk * a))

    def step(S, inp):
        r_t, w_t, k_t, v_t, kk_t, kka_t = inp
        sk = jnp.einsum('bhvk,bhk->bhv', S, kk_t)
        S = S * w_t[:, :, None, :] - sk[..., None] * kka_t[:, :, None, :] + v_t[..., None] * k_t[:, :, None, :]
        return S, jnp.einsum('bhvk,bhk->bhv', S, r_t)

    sf, ys = lax.scan(step, s0.astype(F32), seq, reverse=reverse)
    return jnp.moveaxis(ys, 0, 1), sf


def rwkv_mixer(h, p, s0):
    mu, w_rkv, w_o, w0, w1, w2, a0, a1, a2, g1, g2, kvec, lnx = p
    B, T, D = h.shape
    hs = (B, T, RWKV_HEADS, HEAD_DIM)
    xx = token_shift_centred(h)
    xr, xw, xk, xv, xa, xg = (h + xx * mu[n] for n in range(6))
    r = (xr @ w_rkv[0]).reshape(hs)
    k = (xk @ w_rkv[1]).reshape(hs)
    v = (xv @ w_rkv[2]).reshape(hs)
    g = jax.nn.sigmoid(xg @ g1) @ g2
    k_k, k_a, r_k = (kvec[n].reshape(RWKV_HEADS, HEAD_DIM) for n in range(3))
    kk = (k * k_k).astype(F32)
    kk = kk * lax.rsqrt(jnp.sum(kk * kk, axis=-1, keepdims=True) + 1e-12)
    ys, bonuses, finals = [], [], []
    for d in range(2):
        wlog = -jax.nn.softplus(-(w0[d] + jnp.tanh(xw @ w1[d]) @ w2[d])) - 0.5
        decay = jnp.exp(-jnp.exp(wlog.astype(F32))).reshape(hs)
        a = jax.nn.sigmoid(a0[d] + (xa @ a1[d]) @ a2[d]).reshape(hs)
        kd = k * (1.0 + (a - 1.0) * k_a)
        y, sf = wkv_scan(r, decay, kd, v, kk, a, s0[:, d], reverse=(d == 1))
        ys.append(y)
        bonuses.append(jnp.sum(r * kd * r_k, axis=-1, keepdims=True) * v)
        finals.append(sf)
    y = ys[0] + ys[1]
    mean = jnp.mean(y, axis=-1, keepdims=True)
    var = jnp.mean(jnp.square(y - mean), axis=-1, keepdims=True)
    y = (y - mean) * lax.rsqrt(var + LNX_EPS)
    y = y.astype(h.dtype) * lnx[0].reshape(RWKV_HEADS, HEAD_DIM) + lnx[1].reshape(RWKV_HEADS, HEAD_DIM)
    y = y + bonuses[0] + bonuses[1]
    out = (y.reshape(B, T, D) * g) @ w_o
    return out, jnp.stack(finals, axis=1).astype(h.dtype)


def trunk_layer(x, cond, w_ada, b_ada, gains, w1, w2, mix_fn):
    sh1, sc1, gt1, sh2, sc2, gt2 = adaln(cond, w_ada, b_ada)
    h = rms_norm(x, gains[0]) * (1.0 + sc1) + sh1
    m, side = mix_fn(h)
    x = x + gt1 * rms_norm(m, gains[1])
    h = rms_norm(x, gains[2]) * (1.0 + sc2) + sh2
    f = jnp.square(jax.nn.relu(h @ w1)) @ w2
    x = x + gt2 * rms_norm(f, gains[3])
    return x, side


def setup_inputs(seed: int = 0) -> dict:
    key = jax.random.key(seed)
    ks = iter(jax.random.split(key, 48))
    D = D_MODEL
    LA, LR = N_ATTN_LAYERS, N_RWKV_LAYERS

    def nrm(shape, scale):
        return scale * jax.random.normal(next(ks), shape, F32)

    def unif(shape, lo, hi):
        return jax.random.uniform(next(ks), shape, F32, lo, hi)

    kv_off = jnp.array([0.85, 1.0, 0.0], F32)[None, :, None]
    kv_sc = jnp.array([0.02, 0.02, 0.1], F32)[None, :, None]
    ln_off = jnp.array([1.0, 0.0], F32)[None, :, None]
    ln_sc = jnp.array([0.02, 0.01], F32)[None, :, None]
    return {
        'x_prompt': nrm((BATCH, SEQ, D), 1.0),
        'x_sample': nrm((DEC_BATCH, DEC_SEQ, D), 1.0),
        'c': nrm((DEC_BATCH, D), 1.0),
        'cache_k_gqa': nrm((DEC_BATCH, LA, PAST_LEN, GQA_KV_HEADS, HEAD_DIM), 1.0),
        'cache_v_gqa': nrm((DEC_BATCH, LA, PAST_LEN, GQA_KV_HEADS, HEAD_DIM), 1.0),
        'cache_k_diff': nrm((DEC_BATCH, LA, PAST_LEN, DIFF_HEADS, 2, HEAD_DIM), 1.0),
        'cache_v_diff': nrm((DEC_BATCH, LA, PAST_LEN, DIFF_HEADS, 2 * HEAD_DIM), 1.0),
        'state_rwkv': nrm((DEC_BATCH, LR, 2, RWKV_HEADS, HEAD_DIM, HEAD_DIM), 0.5),
        'c_ctx': nrm((D,), 1.0),
        'w_ada': nrm((DEPTH, D, 6 * D), 0.5 * D ** -0.5),
        'b_ada': nrm((DEPTH, 6 * D), 0.02),
        'norm_gains': 1.0 + nrm((DEPTH, 4, D), 0.02),
        'attn_w_in': nrm((LA, D, ATTN_IN), D ** -0.5),
        'attn_w_out': nrm((LA, MIX_WIDTH, D), MIX_WIDTH ** -0.5),
        'attn_qk_gain': 1.0 + nrm((LA, 2, HEAD_DIM), 0.02),
        'diff_lambda': nrm((LA, 4, HEAD_DIM), 0.1),
        'diff_subln': 1.0 + nrm((LA, 2 * HEAD_DIM), 0.02),
        'rwkv_mu': unif((LR, 6, D), 0.0, 1.0),
        'rwkv_w_rkv': nrm((LR, 3, D, D), D ** -0.5),
        'rwkv_w_o': nrm((LR, D, D), D ** -0.5),
        'rwkv_w0': unif((LR, 2, D), -5.0, 0.0),
        'rwkv_w1': nrm((LR, 2, D, DECAY_LORA), D ** -0.5),
        'rwkv_w2': nrm((LR, 2, DECAY_LORA, D), 0.1 * DECAY_LORA ** -0.5),
        'rwkv_a0': nrm((LR, 2, D), 0.1),
        'rwkv_a1': nrm((LR, 2, D, AAA_LORA), D ** -0.5),
        'rwkv_a2': nrm((LR, 2, AAA_LORA, D), 0.5 * AAA_LORA ** -0.5),
        'rwkv_g1': nrm((LR, D, GATE_LORA), D ** -0.5),
        'rwkv_g2': nrm((LR, GATE_LORA, D), GATE_LORA ** -0.5),
        'rwkv_kvec': kv_off + kv_sc * jax.random.normal(next(ks), (LR, 3, D), F32),
        'rwkv_lnx': ln_off + ln_sc * jax.random.normal(next(ks), (LR, 2, D), F32),
        'mlp_w1': nrm((DEPTH, D, D_FF), D ** -0.5),
        'mlp_w2': nrm((DEPTH, D_FF, D), D_FF ** -0.5),
    }


def reference(x_prompt, x_sample, c, cache_k_gqa, cache_v_gqa, cache_k_diff, cache_v_diff, state_rwkv,
              c_ctx, w_ada, b_ada, norm_gains, attn_w_in, attn_w_out, attn_qk_gain, diff_lambda, diff_subln,
              rwkv_mu, rwkv_w_rkv, rwkv_w_o, rwkv_w0, rwkv_w1, rwkv_w2, rwkv_a0, rwkv_a1, rwkv_a2,
              rwkv_g1, rwkv_g2, rwkv_kvec, rwkv_lnx, mlp_w1, mlp_w2):
    rows = x_sample.shape[1] // GRID_W
    rope = axial_rope(rows)
    ctx_cond = c_ctx[None, :]
    zero_state = jnp.zeros((x_prompt.shape[0], 2, RWKV_HEADS, HEAD_DIM, HEAD_DIM), x_prompt.dtype)
    xp, xs = x_prompt, x_sample
    kg, vg, kd, vd, st = [], [], [], [], []
    for i in range(DEPTH):
        j = i // 2
        lp = (w_ada[i], b_ada[i], norm_gains[i], mlp_w1[i], mlp_w2[i])
        if i % 2 == 0:
            lam_init = 0.8 - 0.6 * math.exp(-0.3 * i)
            ap = (attn_w_in[j], attn_w_out[j], attn_qk_gain[j], diff_lambda[j], diff_subln[j])
            xp, side = trunk_layer(xp, ctx_cond, *lp, lambda h: attn_mixer(h, ap, lam_init, None, None))
            kg.append(side[0]); vg.append(side[1]); kd.append(side[2]); vd.append(side[3])
            cached = (cache_k_gqa[:, j], cache_v_gqa[:, j], cache_k_diff[:, j], cache_v_diff[:, j])
            xs, _ = trunk_layer(xs, c, *lp, lambda h: attn_mixer(h, ap, lam_init, rope, cached))
        else:
            rp = (rwkv_mu[j], rwkv_w_rkv[j], rwkv_w_o[j], rwkv_w0[j], rwkv_w1[j], rwkv_w2[j],
                  rwkv_a0[j], rwkv_a1[j], rwkv_a2[j], rwkv_g1[j], rwkv_g2[j], rwkv_kvec[j], rwkv_lnx[j])
            xp, sf = trunk_layer(xp, ctx_cond, *lp, lambda h: rwkv_mixer(h, rp, zero_state))
            st.append(sf)
            s0 = state_rwkv[:, j]
            xs, _ = trunk_layer(xs, c, *lp, lambda h: rwkv_mixer(h, rp, s0))
    new_k_gqa = jnp.stack(kg, axis=1)
    new_v_gqa = jnp.stack(vg, axis=1)
    new_k_diff = jnp.stack(kd, axis=1)
    new_v_diff = jnp.stack(vd, axis=1)
    new_state_rwkv = jnp.stack(st, axis=1)
    return (xp, xs, new_k_gqa, new_v_gqa, new_k_diff, new_v_diff, new_state_rwkv)
```

```python
from contextlib import ExitStack
import numpy as np
import concourse.bass as bass
import concourse.mybir as mybir
from concourse.bass_utils import run_bass_kernel_spmd

F32 = mybir.dt.float32
BF16 = mybir.dt.bfloat16
AF = mybir.ActivationFunctionType
ALU = mybir.AluOpType
AX = mybir.AxisListType

ENG_NAMES = ['pe', 'act', 'dve', 'pool', 'sp']


class _Op:
    __slots__ = ('eng', 'fn', 'deps', 'flag', 'num', 'sk', 'inc', 'is_dma', 'prev_same_sem')

    def __init__(self, eng, fn, deps, is_dma):
        self.eng = eng
        self.fn = fn
        self.deps = deps
        self.flag = is_dma
        self.num = 0
        self.sk = None
        self.inc = 16 if is_dma else 1
        self.is_dma = is_dma
        self.prev_same_sem = None


class Prog:
    def __init__(self, nc, ndma=12):
        self.nc = nc
        self.all = []
        self.last_w = {}
        self.readers = {}
        self.ndma = ndma
        self.dma_rr = {e: 0 for e in ENG_NAMES}
        self.dma_last = {}
        self.finals = []
        self.bar = None
        self.bar_tile = None

    def barrier(self):
        deps = {}
        for o in self.last_w.values():
            deps[id(o)] = o
        for r in self.readers.values():
            for v in r.values():
                for o in (v if isinstance(v, list) else [v]):
                    deps[id(o)] = o
        if self.bar is not None:
            deps[id(self.bar)] = self.bar
        bt = self.bar_tile
        o = _Op('dve', (lambda e: e.memset(bt, 0.0)), list(deps.values()), False)
        self.all.append(o)
        self.last_w = {}
        self.readers = {}
        self.bar = o

    def _collect(self, eng, reads, writes):
        deps = []
        for k in reads:
            o = self.last_w.get(k)
            if o is not None:
                deps.append(o)
        for k in writes:
            o = self.last_w.get(k)
            if o is not None:
                deps.append(o)
            r = self.readers.get(k)
            if r:
                for v in r.values():
                    if isinstance(v, list):
                        deps.extend(v)
                    else:
                        deps.append(v)
        if self.bar is not None:
            deps.append(self.bar)
        out = []
        seen = set()
        for d in deps:
            if id(d) in seen:
                continue
            seen.add(id(d))
            if d.eng == 'pe' and eng == 'pe' and not d.is_dma:
                continue
            out.append(d)
        return out

    def _commit(self, op, reads, writes):
        for k in writes:
            self.last_w[k] = op
            self.readers[k] = {}
        for k in reads:
            r = self.readers.setdefault(k, {})
            if op.is_dma:
                r.setdefault('dma', []).append(op)
            else:
                r[op.eng] = op

    def op(self, eng, fn, reads=(), writes=()):
        deps = self._collect(eng, reads, writes)
        o = _Op(eng, fn, deps, False)
        self.all.append(o)
        self._commit(o, reads, writes)
        return o

    def dma(self, q, out, in_, reads=(), writes=(), final=False):
        deps = self._collect(q, reads, writes)
        o = _Op(q, (lambda e, out=out, in_=in_: e.dma_start(out=out, in_=in_)), deps, True)
        i = self.dma_rr[q]
        self.dma_rr[q] += 1
        o.sk = ('d', q, i % self.ndma)
        o.prev_same_sem = self.dma_last.get(o.sk)
        self.dma_last[o.sk] = o
        self.all.append(o)
        self._commit(o, reads, writes)
        if final:
            self.finals.append(o)
        return o

    def emit(self):
        nc = self.nc
        for o in self.all:
            for d in o.deps:
                d.flag = True
        cnt = {}
        for o in self.all:
            if o.is_dma:
                k = cnt.get(o.sk, 0) + 1
                cnt[o.sk] = k
                o.num = 16 * k
            elif o.flag:
                o.sk = ('c', o.eng)
                k = cnt.get(o.sk, 0) + 1
                cnt[o.sk] = k
                o.num = k
        waited = {e: {} for e in ENG_NAMES}
        streams = {e: [] for e in ENG_NAMES}
        for o in self.all:
            need = {}
            for d in o.deps:
                if waited[o.eng].get(d.sk, 0) >= d.num:
                    continue
                need[d.sk] = max(need.get(d.sk, 0), d.num)
            if o.is_dma and o.prev_same_sem is not None:
                p = o.prev_same_sem
                if waited[o.eng].get(p.sk, 0) < p.num:
                    need[p.sk] = max(need.get(p.sk, 0), p.num)
            for sk, v in need.items():
                waited[o.eng][sk] = v
            streams[o.eng].append((list(need.items()), o))
        fin = {}
        for o in self.finals:
            fin[o.sk] = max(fin.get(o.sk, 0), o.num)
        self.n_ops = {e: len(streams[e]) for e in ENG_NAMES}
        self.streams = streams
        self.fin = fin
        with ExitStack() as st:
            sems = {}
            for sk in cnt:
                sems[sk] = st.enter_context(nc.semaphore("s_" + "_".join(str(x) for x in sk)))
            block = st.enter_context(nc.Block())

            def mk(name):
                def f(eng):
                    for waits, o in streams[name]:
                        for (wk, v) in waits:
                            eng.wait_ge(sems[wk], v)
                        ins = o.fn(eng)
                        if o.flag:
                            ins.then_inc(sems[o.sk], o.inc)
                    if name == 'sp':
                        for sk, v in fin.items():
                            eng.wait_ge(sems[sk], v)
                return f

            block.tensor(mk('pe'))
            block.scalar(mk('act'))
            block.vector(mk('dve'))
            block.gpsimd(mk('pool'))
            block.sync(mk('sp'))


D = 1024
T = 2048
NKT = 20
EPS = 1e-6
VW = 652
LAM_INIT = {0: 0.8 - 0.6 * float(np.exp(-0.3 * 0)), 2: 0.8 - 0.6 * float(np.exp(-0.3 * 2))}


class Arena:
    def __init__(self, ap, nwords):
        self.ap = ap
        self.n = nwords
        self.top = 0

    def mark(self):
        return self.top

    def release(self, m):
        self.top = m

    def f32(self, n):
        a = self.top
        self.top += (n + 7) // 8 * 8
        assert self.top <= self.n, ("arena overflow", self.top, self.n)
        return self.ap[:, a:a + n]

    def bf16(self, n):
        w = (n + 1) // 2
        a = self.top
        self.top += (w + 7) // 8 * 8
        assert self.top <= self.n, ("arena overflow", self.top, self.n)
        return self.ap[:, a:a + w].bitcast(BF16)[:, 0:n]


def build_program(n_layers=4, do_rwkv=True, stage=9, rstage=9, rstop=999):
    nc = bass.Bass("TRN2", target_bir_lowering=False)

    def din(name, shape):
        return nc.dram_tensor(name, list(shape), F32, kind="ExternalInput").ap()

    def dout(name, shape):
        return nc.dram_tensor(name, list(shape), F32, kind="ExternalOutput").ap()

    xT_d = din("xT", [D, T])
    cond_d = din("cond", [128, 8])
    bada_d = din("b_ada", [4, 128, 48])
    gains_d = din("gains", [4, 128, 32])
    wada_d = din("w_ada", [4, D, 6 * D])
    win_d = din("w_in", [2, D, 2304])
    wout_d = din("w_out", [2, D, D])
    gqk_d = din("gqk", [2, 128, 128])
    lamv_d = din("lamv", [2, 128, 256])
    subln_d = din("subln", [2, 128, 128])
    kTc_d = din("kTc", [2, 640, 512])
    vc_d = din("vc", [2, 512, 640])
    maskb_d = din("maskb", [128, 160])
    cos_d = din("cos", [128, 16 * 64])
    sin_d = din("sin", [128, 16 * 64])
    w1_d = din("mlp_w1", [4, D, 4 * D])
    w2_d = din("mlp_w2", [4, 4 * D, D])
    ident_d = din("ident", [128, 128])

    wrkv_d = din("wrkv", [2, 3, D, D])
    wor_d = din("wo_r", [2, D, D])
    w1c_d = din("w1c", [2, D, 128])
    a1c_d = din("a1c", [2, D, 128])
    g1_d = din("g1", [2, D, 128])
    w2x_d = din("w2x", [2, 2, 128, D])
    a2x_d = din("a2x", [2, 2, 128, D])
    g2_d = din("g2", [2, 128, D])
    rmu_d = din("rmu", [2, 128, 48])
    rvecs_d = din("rvecs", [2, 128, 72])
    st0_d = din("st0", [2, 2, 8, 128, 128])
    keepf_d = din("keepf", [128, 1])
    shiftm_d = din("shiftm", [2, 128, T])
    bones_d = din("bones", [128, 128])
    rmask_d = din("rmask", [2, 128, 640])
    bd16_d = din("bd16", [128, 256])

    yT_d = dout("yT", [D, T])
    okv_d = dout("okv", [2, T, 1280])
    ostate_d = dout("ostate", [2, 8, 2, 8, 128, 128])

    def dscr(name, shape):
        return nc.dram_tensor(name, list(shape), F32, kind="Internal").ap()

    SCR = {n: dscr("scr_" + n, [D, T]) for n in ("R", "KA", "KD0", "KD1", "B0", "B1", "LW0", "LW1", "G", "BON", "YF", "YB", "YG")}
    Vt_s = dscr("scr_Vt", [T, D])

    P = Prog(nc)
    NW = 53000
    with ExitStack() as st:
        arena_t = st.enter_context(nc.sbuf_tensor("arena", [128, NW], F32))
        A = Arena(arena_t, NW)
        psbig = st.enter_context(nc.psum_tensor("psbig", [128, 4096], F32))
        ps = [psbig[:, i * 512:(i + 1) * 512] for i in range(8)]
        psb = [p.bitcast(BF16) for p in ps]

        uid = [0]

        def K(s):
            uid[0] += 1
            return f"{s}#{uid[0]}"

        xT = A.f32(8 * T).rearrange("p (c t) -> p c t", c=8)
        ones_bf = A.bf16(128)
        ident_bf = A.bf16(128)
        cos_t = A.f32(1024).rearrange("p (s d) -> p s d", s=16)
        sin_t = A.f32(1024).rearrange("p (s d) -> p s d", s=16)
        maskb = A.f32(160).rearrange("p (i k) -> p i k", i=8)
        mods = A.f32(4 * 48).rearrange("p (l m) -> p l m", l=4)
        gains = A.f32(4 * 32).rearrange("p (l m) -> p l m", l=4)
        gsv = A.f32(4 * 32).rearrange("p (l m) -> p l m", l=4)
        condT = A.f32(8)
        silu_bf = A.bf16(8)
        bada = A.f32(4 * 48).rearrange("p (l m) -> p l m", l=4)
        epsb = A.f32(1)
        P.bar_tile = A.f32(1)

        P.op('dve', lambda e: e.memset(ones_bf, 1.0), writes=['ones'])
        P.op('dve', lambda e: e.memset(epsb, EPS), writes=['epsb'])
        P.dma('pool', ident_bf, ident_d, writes=['identbf'])
        P.dma('sp', cos_t, cos_d.rearrange("p (s d) -> p s d", s=16), writes=['cos'])
        P.dma('sp', sin_t, sin_d.rearrange("p (s d) -> p s d", s=16), writes=['sin'])
        P.dma('sp', maskb, maskb_d.rearrange("p (i k) -> p i k", i=8), writes=['maskb'])
        P.dma('sp', condT, cond_d, writes=['cond'])
        P.dma('sp', bada, bada_d.rearrange("l p m -> p l m"), writes=['bada'])
        P.dma('sp', gains, gains_d.rearrange("l p m -> p l m"), writes=['gains'])
        for c in range(8):
            P.dma('sp', xT[:, c, :], xT_d[c * 128:(c + 1) * 128, :], writes=[f'x:{c}:{tb}' for tb in range(4)])

        P.op('act', lambda e: e.activation(silu_bf, condT, AF.Silu), reads=['cond'], writes=['silu'])
        m0 = A.mark()
        wab = [A.bf16(8 * 512).rearrange("p (k n) -> p k n", k=8) for _ in range(2)]
        bi = 0
        for l in range(n_layers):
            for blk in range(12):
                buf = wab[bi % 2]
                key = f'wab:{bi % 2}'
                bi += 1
                P.dma('pool', buf, wada_d[l, :, blk * 512:(blk + 1) * 512].rearrange("(k p) n -> p k n", p=128),
                      writes=[key])
                for jj in range(4):
                    j = blk * 4 + jj
                    for k in range(8):
                        P.op('pe', lambda e, buf=buf, jj=jj, k=k, j=j: e.matmul(
                            ps[7][:, j:j + 1], buf[:, k, jj * 128:(jj + 1) * 128], silu_bf[:, k:k + 1],
                            start=(k == 0), stop=(k == 7)), reads=[key, 'silu'], writes=['ps:7'])
            P.op('dve', lambda e, l=l: e.tensor_tensor(mods[:, l, :], ps[7][:, 0:48], bada[:, l, :], ALU.add),
                 reads=['ps:7', 'bada'], writes=[f'mods:{l}'])
            P.op('dve', lambda e, l=l: e.scalar_tensor_tensor(gsv[:, l, 0:8], mods[:, l, 8:16], 1.0, gains[:, l, 0:8],
                                                              ALU.add, ALU.mult), reads=[f'mods:{l}', 'gains'], writes=[f'gsv:{l}:0'])
            P.op('dve', lambda e, l=l: e.tensor_tensor(gsv[:, l, 8:16], mods[:, l, 16:24], gains[:, l, 8:16], ALU.mult),
                 reads=[f'mods:{l}', 'gains'], writes=[f'gsv:{l}:1'])
            P.op('dve', lambda e, l=l: e.scalar_tensor_tensor(gsv[:, l, 16:24], mods[:, l, 32:40], 1.0, gains[:, l, 16:24],
                                                              ALU.add, ALU.mult), reads=[f'mods:{l}', 'gains'], writes=[f'gsv:{l}:2'])
            P.op('dve', lambda e, l=l: e.tensor_tensor(gsv[:, l, 24:32], mods[:, l, 40:48], gains[:, l, 24:32], ALU.mult),
                 reads=[f'mods:{l}', 'gains'], writes=[f'gsv:{l}:3'])
        A.release(m0)
        P.barrier()

        def rstd_from_ps(pst, n, dst, scale, keyr, keyw):
            P.op('act', lambda e: e.activation(dst, pst, AF.Sqrt, bias=epsb[:, 0:1], scale=scale),
                 reads=keyr + ['epsb'], writes=[keyw])
            P.op('dve', lambda e: e.reciprocal(dst, dst), reads=[keyw], writes=[keyw])

        def norm_block(l, which, t0, n, hT, hkey, sqb, rstd, tmp):
            gi, shi = (0, 0) if which == 1 else (16, 24)
            tb = t0 // 512
            for c in range(8):
                P.op('act', lambda e, c=c: e.activation(sqb[:, 0:n], xT[:, c, t0:t0 + n], AF.Square),
                     reads=[f'x:{c}:{tb}'], writes=['sqb'])
                P.op('pe', lambda e, c=c: e.matmul(ps[2][:, 0:n], ones_bf, sqb[:, 0:n], start=(c == 0), stop=(c == 7)),
                     reads=['sqb', 'ones'], writes=['ps:2'])
            rstd_from_ps(ps[2][:, 0:n], n, rstd[:, 0:n], 1.0 / D, ['ps:2'], 'rstd')
            for c in range(8):
                P.op('dve', lambda e, c=c: e.scalar_tensor_tensor(tmp[:, 0:n], xT[:, c, t0:t0 + n], gsv[:, l, gi + c:gi + c + 1],
                                                                  rstd[:, 0:n], ALU.mult, ALU.mult),
                     reads=[f'x:{c}:{tb}', 'rstd', f'gsv:{l}:{gi // 8}'], writes=['ntmp'])
                P.op('act', lambda e, c=c: e.activation(hT[:, c, 0:n], tmp[:, 0:n], AF.Identity,
                                                        bias=mods[:, l, shi + c:shi + c + 1], scale=1.0),
                     reads=['ntmp', f'mods:{l}'], writes=[f'{hkey}:{c}'])

        def residual_update(l, gidx, t0, n, mT, mkey, rstd, tmp):
            tb = t0 // 512
            rstd_from_ps(ps[2][:, 0:n], n, rstd[:, 0:n], 1.0 / D, ['ps:2'], 'rstd')
            for c in range(8):
                P.op('dve', lambda e, c=c: e.scalar_tensor_tensor(tmp[:, 0:n], mT[:, c, 0:n], gsv[:, l, gidx + c:gidx + c + 1],
                                                                  rstd[:, 0:n], ALU.mult, ALU.mult),
                     reads=[f'{mkey}:{c}', 'rstd', f'gsv:{l}:{gidx // 8}'], writes=['ntmp'])
                P.op('pool', lambda e, c=c: e.tensor_tensor(xT[:, c, t0:t0 + n], xT[:, c, t0:t0 + n], tmp[:, 0:n], ALU.add),
                     reads=['ntmp', f'x:{c}:{tb}'], writes=[f'x:{c}:{tb}'])

        def rope_tm(src, nh, st_idx, dst_bf, t1, t2, kin, kout):
            s5 = src.rearrange("p (h a b d) -> p h a b d", h=nh, a=2, b=2)
            t5 = t1.rearrange("p (h a b d) -> p h a b d", h=nh, a=2, b=2)
            sn = sin_t[:, st_idx, :].rearrange("p (a b d) -> p a b d", a=2, b=2)
            P.op('dve', lambda e: e.tensor_tensor(t5[:, :, :, 0, :], s5[:, :, :, 1, :],
                                                  sn[:, :, 0, :].unsqueeze(1).broadcast_to([128, nh, 2, 16]), ALU.mult),
                 reads=kin + ['sin'], writes=[kout + 'a'])
            P.op('pool', lambda e: e.tensor_tensor(t5[:, :, :, 1, :], s5[:, :, :, 0, :],
                                                   sn[:, :, 1, :].unsqueeze(1).broadcast_to([128, nh, 2, 16]), ALU.mult),
                 reads=kin + ['sin'], writes=[kout + 'b'])
            s3 = src.rearrange("p (h d) -> p h d", h=nh)
            P.op('dve', lambda e: e.tensor_tensor(t2.rearrange("p (h d) -> p h d", h=nh), s3,
                                                  cos_t[:, st_idx, :].unsqueeze(1).broadcast_to([128, nh, 64]), ALU.mult),
                 reads=kin + ['cos'], writes=[kout + 'c'])
            P.op('dve', lambda e: e.tensor_tensor(dst_bf, t1, t2, ALU.add),
                 reads=[kout + 'a', kout + 'b', kout + 'c'], writes=[kout])

        def head_rmsnorm(src, nh, gain_bc, ss, tmp, kin, kout):
            s3 = src.rearrange("p (h d) -> p h d", h=nh)
            t3 = tmp.rearrange("p (h d) -> p h d", h=nh)
            P.op('dve', lambda e: e.tensor_tensor(tmp, src, src, ALU.mult), reads=kin, writes=['hn_tmp'])
            P.op('dve', lambda e: e.tensor_reduce(ss[:, 0:nh], t3, AX.X, ALU.add), reads=['hn_tmp'], writes=['hn_ss'])
            P.op('act', lambda e: e.activation(ss[:, 0:nh], ss[:, 0:nh], AF.Sqrt, bias=epsb[:, 0:1], scale=1.0 / 64),
                 reads=['hn_ss', 'epsb'], writes=['hn_ss'])
            P.op('dve', lambda e: e.reciprocal(ss[:, 0:nh], ss[:, 0:nh]), reads=['hn_ss'], writes=['hn_ss'])
            P.op('dve', lambda e: e.tensor_tensor(s3, s3, ss[:, 0:nh].unsqueeze(2).broadcast_to([128, nh, 64]), ALU.mult),
                 reads=kin + ['hn_ss'], writes=[kout])
            P.op('dve', lambda e: e.tensor_tensor(s3, s3, gain_bc.unsqueeze(1).broadcast_to([128, nh, 64]), ALU.mult),
                 reads=[kout, 'gqk'], writes=[kout])

        def attn_layer(l):
            j = l // 2
            lam_init = LAM_INIT[l]
            m_layer = A.mark()
            KT = A.bf16(5 * 2560).rearrange("p (c t) -> p c t", c=5)
            Vp = A.bf16(NKT * VW).rearrange("p (k w) -> p k w", k=NKT)
            gqk = A.f32(128)
            lamv = A.f32(256)
            sublnS = A.f32(128)
            lam_t = A.f32(4)
            rstd = A.f32(512)
            ntmp = A.f32(512)
            sqb = A.bf16(512)
            P.dma('sp', gqk, gqk_d[j], writes=['gqk'])
            P.dma('sp', lamv, lamv_d[j], writes=['lamv'])
            P.dma('sp', sublnS, subln_d[j], writes=['subln'])
            l4 = lamv.rearrange("p (a d) -> p a d", a=4)
            P.op('dve', lambda e: e.tensor_tensor(l4[:, 0, :], l4[:, 0, :], l4[:, 1, :], ALU.mult), reads=['lamv'], writes=['lamv'])
            P.op('dve', lambda e: e.tensor_tensor(l4[:, 2, :], l4[:, 2, :], l4[:, 3, :], ALU.mult), reads=['lamv'], writes=['lamv'])
            P.op('dve', lambda e: e.tensor_reduce(lam_t[:, 0:1], l4[:, 0, :], AX.X, ALU.add), reads=['lamv'], writes=['lam'])
            P.op('dve', lambda e: e.tensor_reduce(lam_t[:, 1:2], l4[:, 2, :], AX.X, ALU.add), reads=['lamv'], writes=['lam'])
            P.op('act', lambda e: e.activation(lam_t[:, 0:2], lam_t[:, 0:2], AF.Exp), reads=['lam'], writes=['lam'])
            P.op('dve', lambda e: e.tensor_tensor(lam_t[:, 2:3], lam_t[:, 0:1], lam_t[:, 1:2], ALU.subtract), reads=['lam'], writes=['lam'])
            P.op('dve', lambda e: e.tensor_scalar(lam_t[:, 2:3], lam_t[:, 2:3], lam_init, None, ALU.add), reads=['lam'], writes=['lam'])
            P.op('dve', lambda e: e.tensor_scalar(sublnS, sublnS, 1.0 - lam_init, None, ALU.mult), reads=['subln'], writes=['subln'])
            P.op('pool', lambda e: e.memset(Vp, 1.0), writes=[f'Vp:{k}' for k in range(NKT)])
            for c in range(5):
                P.dma('pool', KT[:, c, 0:512], kTc_d[j, c * 128:(c + 1) * 128, :], writes=[f'KT:{c}:c'])
            for k in range(4):
                P.dma('pool', Vp[:, k, 0:132].rearrange("p (h w) -> p h w", h=2)[:, :, 0:64],
                      vc_d[j, k * 128:(k + 1) * 128, 0:128].rearrange("p (h d) -> p h d", h=2),
                      reads=[f'Vp:{k}'], writes=[f'Vp:{k}'])
                P.dma('pool', Vp[:, k, 132:652].rearrange("p (h w) -> p h w", h=4)[:, :, 0:128],
                      vc_d[j, k * 128:(k + 1) * 128, 128:640].rearrange("p (h d) -> p h d", h=4),
                      reads=[f'Vp:{k}'], writes=[f'Vp:{k}'])

            m1 = A.mark()
            Wkv = A.bf16(8 * 1280).rearrange("p (k n) -> p k n", k=8)
            hT = A.bf16(8 * 512).rearrange("p (c t) -> p c t", c=8)
            kv32 = A.f32(1280)
            t1 = A.f32(640)
            t2 = A.f32(640)
            kb16 = A.bf16(640)
            ss = A.f32(16)
            P.dma('pool', Wkv, win_d[j, :, 1024:2304].rearrange("(k p) n -> p k n", p=128), writes=['Wkv'])
            for tb in range(4):
                norm_block(l, 1, tb * 512, 512, hT, 'hT', sqb, rstd, ntmp)
                for s in range(4):
                    sti = tb * 4 + s
                    tok0 = sti * 128
                    for cb, (c0, cn) in enumerate([(0, 512), (512, 512), (1024, 256)]):
                        for k in range(8):
                            P.op('pe', lambda e, k=k, s=s, c0=c0, cn=cn, cb=cb: e.matmul(
                                ps[cb % 2][:, 0:cn], hT[:, k, s * 128:(s + 1) * 128], Wkv[:, k, c0:c0 + cn],
                                start=(k == 0), stop=(k == 7)), reads=[f'hT:{k}', 'Wkv'], writes=[f'ps:{cb % 2}'])
                        eng = 'act' if cb != 1 else 'dve'
                        if eng == 'act':
                            P.op('act', lambda e, c0=c0, cn=cn, cb=cb: e.activation(kv32[:, c0:c0 + cn], ps[cb % 2][:, 0:cn], AF.Copy),
                                 reads=[f'ps:{cb % 2}'], writes=[f'kv32:{cb}'])
                        else:
                            P.op('dve', lambda e, c0=c0, cn=cn, cb=cb: e.tensor_copy(kv32[:, c0:c0 + cn], ps[cb % 2][:, 0:cn]),
                                 reads=[f'ps:{cb % 2}'], writes=[f'kv32:{cb}'])
                    head_rmsnorm(kv32[:, 0:128], 2, gqk[:, 64:128], ss, t1[:, 0:128], ['kv32:0'], 'kv32:0')
                    P.dma('sp', okv_d[j, tok0:tok0 + 128, :], kv32, reads=['kv32:0', 'kv32:1', 'kv32:2'], writes=[K('okv')], final=True)
                    rope_tm(kv32[:, 0:640], 10, sti, kb16, t1, t2, ['kv32:0', 'kv32:1'], 'kb16')
                    for c in range(5):
                        P.op('pe', lambda e, c=c: e.transpose(psb[3][:, c * 128:(c + 1) * 128], kb16[:, c * 128:(c + 1) * 128], ident_bf),
                             reads=['kb16', 'identbf'], writes=['ps:3'])
                    P.op('act', lambda e, tok0=tok0: e.activation(KT[:, :, 512 + tok0:512 + tok0 + 128],
                                                                 psb[3][:, 0:640].rearrange("p (c t) -> p c t", c=5), AF.Copy),
                         reads=['ps:3'], writes=[f'KT:{c}:{sti}' for c in range(5)])
                    kt = 4 + sti
                    P.op('pool', lambda e, kt=kt: e.tensor_copy(Vp[:, kt, 0:132].rearrange("p (h w) -> p h w", h=2)[:, :, 0:64],
                                                               kv32[:, 640:768].rearrange("p (h d) -> p h d", h=2)),
                         reads=['kv32:1', 'kv32:2', f'Vp:{kt}'], writes=[f'Vp:{kt}'])
                    P.op('pool', lambda e, kt=kt: e.tensor_copy(Vp[:, kt, 132:652].rearrange("p (h w) -> p h w", h=4)[:, :, 0:128],
                                                               kv32[:, 768:1280].rearrange("p (h d) -> p h d", h=4)),
                         reads=['kv32:1', 'kv32:2', f'Vp:{kt}'], writes=[f'Vp:{kt}'])
            A.release(m1)
            P.barrier()
            if stage < 3:
                A.release(m_layer)
                return

            Wq = A.bf16(8 * 1024).rearrange("p (k n) -> p k n", k=8)
            Wo = A.bf16(8 * 1024).rearrange("p (k n) -> p k n", k=8)
            hTq = A.bf16(8 * 256).rearrange("p (c t) -> p c t", c=8)
            QT = A.bf16(8 * 256).rearrange("p (c t) -> p c t", c=8)
            catT = A.bf16(8 * 256).rearrange("p (c t) -> p c t", c=8)
            q32 = A.f32(1024)
            qt1 = A.f32(1024)
            qt2 = A.f32(1024)
            qb16 = A.bf16(1024)
            cat = [A.bf16(1024) for _ in range(2)]
            Pt = [A.bf16(512) for _ in range(2)]
            mT = A.f32(8 * 256).rearrange("p (c t) -> p c t", c=8)
            ss = A.f32(16)
            rz = A.f32(8)
            od = A.f32(128)
            od2 = A.f32(128)
            P.dma('pool', Wq, win_d[j, :, 0:1024].rearrange("(k p) n -> p k n", p=128), writes=['Wq'])
            P.dma('pool', Wo, wout_d[j].rearrange("(k p) n -> p k n", p=128), writes=['Wo'])
            all_kt_keys = [[f'KT:{c}:c'] + [f'KT:{c}:{s}' for s in range(16)] for c in range(5)]
            pti = 0
            for qi in range(8):
                t0 = qi * 256
                tb = t0 // 512
                norm_block(l, 1, t0, 256, hTq, 'hTq', sqb, rstd, ntmp)
                for s in range(2):
                    sti = qi * 2 + s
                    for cb in range(2):
                        for k in range(8):
                            P.op('pe', lambda e, k=k, s=s, cb=cb: e.matmul(
                                ps[cb][:, 0:512], hTq[:, k, s * 128:(s + 1) * 128], Wq[:, k, cb * 512:(cb + 1) * 512],
                                start=(k == 0), stop=(k == 7)), reads=[f'hTq:{k}', 'Wq'], writes=[f'ps:{cb}'])
                        P.op('act', lambda e, cb=cb: e.activation(q32[:, cb * 512:(cb + 1) * 512], ps[cb][:, 0:512], AF.Copy),
                             reads=[f'ps:{cb}'], writes=[f'q32:{cb}'])
                    head_rmsnorm(q32[:, 0:512], 8, gqk[:, 0:64], ss, qt1[:, 0:512], ['q32:0'], 'q32:0')
                    rope_tm(q32, 16, sti, qb16, qt1, qt2, ['q32:0', 'q32:1'], 'qb16')
                    for c in range(8):
                        P.op('pe', lambda e, c=c: e.transpose(psb[3][:, c * 128:(c + 1) * 128], qb16[:, c * 128:(c + 1) * 128], ident_bf),
                             reads=['qb16', 'identbf'], writes=['ps:3'])
                    P.op('act', lambda e, s=s: e.activation(QT[:, :, s * 128:(s + 1) * 128],
                                                           psb[3][:, 0:1024].rearrange("p (c t) -> p c t", c=8), AF.Copy),
                         reads=['ps:3'], writes=[f'QT:{s}'])
                for hc in range(8):
                    gqa = hc < 4
                    kc = 0 if gqa else 1 + (hc - 4)
                    W = 66 if gqa else 130
                    accs = [ps[4], ps[4]] if gqa else [ps[4], ps[5]]
                    acck = ['ps:4', 'ps:4'] if gqa else ['ps:4', 'ps:5']
                    for kt in range(NKT):
                        sb = 0 if kt % 2 == 0 else 6
                        ktkeys = all_kt_keys[kc]
                        for half in range(2):
                            P.op('pe', lambda e, half=half, kt=kt, sb=sb, kc=kc, hc=hc: e.matmul(
                                ps[sb + half][:, 0:256], KT[half * 64:(half + 1) * 64, kc, kt * 128:(kt + 1) * 128],
                                QT[half * 64:(half + 1) * 64, hc, :], start=True, stop=True),
                                reads=ktkeys + ['QT:0', 'QT:1'], writes=[f'ps:{sb + half}'])
                        pt = Pt[pti % 2]
                        pk = f"Pt:{pti % 2}"
                        pti += 1
                        P.op('act', lambda e, pt=pt, sb=sb, kt=kt, qi=qi: e.activation(
                            pt.rearrange("p (b n) -> p b n", b=2),
                            psbig[:, sb * 512:(sb + 2) * 512].rearrange("p (b n) -> p b n", b=2)[:, :, 0:256], AF.Exp,
                            bias=maskb[:, qi, kt:kt + 1], scale=0.125),
                             reads=[f'ps:{sb}', f'ps:{sb + 1}', 'maskb'], writes=[pk])
                        for half in range(2):
                            voff = half * 66 if gqa else 132 + (hc - 4) * 130
                            for s in range(2):
                                a0 = (half * 2 + s) * W if gqa else s * W
                                P.op('pe', lambda e, pt=pt, half=half, s=s, a0=a0, voff=voff, W=W, kt=kt, accs=accs, gqa=gqa: e.matmul(
                                    accs[half][:, a0:a0 + W], pt[:, half * 256 + s * 128: half * 256 + (s + 1) * 128],
                                    Vp[:, kt, voff:voff + W], start=(kt == 0 and s == 0 and (half == 0 or not gqa)), stop=(kt == NKT - 1)),
                                    reads=[pk, f'Vp:{kt}'], writes=[acck[half]])
                    if gqa:
                        a3 = ps[4][:, 0:264].rearrange("p (g w) -> p g w", g=4)
                        P.op('dve', lambda e, a3=a3: e.reciprocal(rz[:, 0:4], a3[:, :, 64]), reads=['ps:4'], writes=['rz'])
                        for s in range(2):
                            src = ps[4][:, 0:264].rearrange("p (h s w) -> p h s w", h=2, s=2)[:, :, s, 0:64]
                            rzv = rz[:, 0:4].rearrange("p (h s) -> p h s", h=2)[:, :, s]
                            P.op('dve', lambda e, s=s, src=src, rzv=rzv, hc=hc: e.tensor_tensor(
                                cat[s][:, hc * 128:(hc + 1) * 128].rearrange("p (h d) -> p h d", h=2), src,
                                rzv.unsqueeze(2).broadcast_to([128, 2, 64]), ALU.mult),
                                reads=['ps:4', 'rz'], writes=[f'cat:{s}:{hc}'])
                    else:
                        for s in range(2):
                            P.op('dve', lambda e, s=s: e.reciprocal(rz[:, 0:1], ps[4][:, s * 130 + 128:s * 130 + 129]), reads=['ps:4'], writes=['rz'])
                            P.op('dve', lambda e, s=s: e.reciprocal(rz[:, 1:2], ps[5][:, s * 130 + 128:s * 130 + 129]), reads=['ps:5'], writes=['rz'])
                            P.op('dve', lambda e: e.tensor_tensor(rz[:, 1:2], rz[:, 1:2], lam_t[:, 2:3], ALU.mult), reads=['rz', 'lam'], writes=['rz'])
                            P.op('dve', lambda e, s=s: e.tensor_scalar(od2, ps[5][:, s * 130:s * 130 + 128], rz[:, 1:2], None, ALU.mult),
                                 reads=['ps:5', 'rz'], writes=['od2'])
                            P.op('dve', lambda e, s=s: e.scalar_tensor_tensor(od, ps[4][:, s * 130:s * 130 + 128], rz[:, 0:1], od2,
                                                                              ALU.mult, ALU.subtract), reads=['ps:4', 'rz', 'od2'], writes=['od'])
                            P.op('dve', lambda e: e.tensor_tensor(od2, od, od, ALU.mult), reads=['od'], writes=['od2'])
                            P.op('dve', lambda e: e.tensor_reduce(rz[:, 2:3], od2, AX.X, ALU.add), reads=['od2'], writes=['rz2'])
                            P.op('act', lambda e: e.activation(rz[:, 2:3], rz[:, 2:3], AF.Sqrt, bias=epsb[:, 0:1], scale=1.0 / 128),
                                 reads=['rz2', 'epsb'], writes=['rz2'])
                            P.op('dve', lambda e: e.reciprocal(rz[:, 2:3], rz[:, 2:3]), reads=['rz2'], writes=['rz2'])
                            P.op('dve', lambda e, s=s, hc=hc: e.scalar_tensor_tensor(cat[s][:, hc * 128:(hc + 1) * 128], od, rz[:, 2:3], sublnS,
                                                                                     ALU.mult, ALU.mult),
                                 reads=['od', 'rz2', 'subln'], writes=[f'cat:{s}:{hc}'])
                for s in range(2):
                    for c in range(8):
                        P.op('pe', lambda e, c=c, s=s: e.transpose(psb[3][:, c * 128:(c + 1) * 128], cat[s][:, c * 128:(c + 1) * 128], ident_bf),
                             reads=[f'cat:{s}:{c}', 'identbf'], writes=['ps:3'])
                    P.op('act', lambda e, s=s: e.activation(catT[:, :, s * 128:(s + 1) * 128],
                                                           psb[3][:, 0:1024].rearrange("p (c t) -> p c t", c=8), AF.Copy),
                         reads=['ps:3'], writes=[f'catT:{s}'])
                for c in range(8):
                    b = c % 2
                    for k in range(8):
                        P.op('pe', lambda e, c=c, k=k, b=b: e.matmul(ps[b][:, 0:256], Wo[:, k, c * 128:(c + 1) * 128], catT[:, k, :],
                                                                   start=(k == 0), stop=(k == 7)),
                             reads=['Wo', 'catT:0', 'catT:1'], writes=[f'ps:{b}'])
                    P.op('act', lambda e, c=c, b=b: e.activation(mT[:, c, :], ps[b][:, 0:256], AF.Copy), reads=[f'ps:{b}'], writes=[f'mT:{c}'])
                    P.op('act', lambda e, c=c, b=b: e.activation(sqb[:, 0:256], ps[b][:, 0:256], AF.Square), reads=[f'ps:{b}'], writes=['sqb'])
                    P.op('pe', lambda e, c=c: e.matmul(ps[2][:, 0:256], ones_bf, sqb[:, 0:256], start=(c == 0), stop=(c == 7)),
                         reads=['sqb', 'ones'], writes=['ps:2'])
                residual_update(l, 8, t0, 256, mT, 'mT', rstd, ntmp)
            A.release(m_layer)
            P.barrier()

        def mlp(l):
            m = A.mark()
            hT = A.bf16(8 * 512).rearrange("p (c t) -> p c t", c=8)
            hid = A.bf16(32 * 512).rearrange("p (j t) -> p j t", j=32)
            w1b = [A.bf16(8 * 512).rearrange("p (k n) -> p k n", k=8) for _ in range(2)]
            w2b = [A.bf16(32 * 128).rearrange("p (j n) -> p j n", j=32) for _ in range(2)]
            r32 = [A.f32(512) for _ in range(2)]
            fT = A.f32(8 * 512).rearrange("p (c t) -> p c t", c=8)
            rstd = A.f32(512)
            ntmp = A.f32(512)
            sqb = A.bf16(512)
            wi = 0
            w2i = 0
            for tb in range(4):
                t0 = tb * 512
                norm_block(l, 2, t0, 512, hT, 'hT', sqb, rstd, ntmp)
                for g in range(8):
                    wb = w1b[wi % 2]
                    wk = f'w1b:{wi % 2}'
                    wi += 1
                    P.dma('pool', wb, w1_d[l, :, g * 512:(g + 1) * 512].rearrange("(k p) n -> p k n", p=128), writes=[wk])
                    for jj in range(4):
                        jx = g * 4 + jj
                        b = jx % 2
                        for k in range(8):
                            P.op('pe', lambda e, wb=wb, jj=jj, k=k, b=b: e.matmul(ps[b][:, 0:512], wb[:, k, jj * 128:(jj + 1) * 128], hT[:, k, :],
                                                                                start=(k == 0), stop=(k == 7)),
                                 reads=[wk] + [f'hT:{k}'], writes=[f'ps:{b}'])
                        P.op('act', lambda e, b=b: e.activation(r32[b], ps[b][:, 0:512], AF.Relu), reads=[f'ps:{b}'], writes=[f'r32:{b}'])
                        P.op('pool' if jx % 2 else 'dve', lambda e, b=b, jx=jx: e.tensor_tensor(hid[:, jx, :], r32[b], r32[b], ALU.mult),
                             reads=[f'r32:{b}'], writes=[f'hid:{jx}'])
                for c in range(8):
                    wb = w2b[w2i % 2]
                    wk = f'w2b:{w2i % 2}'
                    w2i += 1
                    P.dma('pool', wb, w2_d[l, :, c * 128:(c + 1) * 128].rearrange("(j p) n -> p j n", p=128), writes=[wk])
                    b = 4 + c % 2
                    for jx in range(32):
                        P.op('pe', lambda e, wb=wb, jx=jx, b=b: e.matmul(ps[b][:, 0:512], wb[:, jx, :], hid[:, jx, :],
                                                                       start=(jx == 0), stop=(jx == 31)),
                             reads=[wk, f'hid:{jx}'], writes=[f'ps:{b}'])
                    P.op('act', lambda e, c=c, b=b: e.activation(fT[:, c, :], ps[b][:, 0:512], AF.Copy), reads=[f'ps:{b}'], writes=[f'fT:{c}'])
                    P.op('act', lambda e, c=c, b=b: e.activation(sqb, ps[b][:, 0:512], AF.Square), reads=[f'ps:{b}'], writes=['sqb'])
                    P.op('pe', lambda e, c=c: e.matmul(ps[2][:, 0:512], ones_bf, sqb, start=(c == 0), stop=(c == 7)),
                         reads=['sqb', 'ones'], writes=['ps:2'])
                residual_update(l, 24, t0, 512, fT, 'fT', rstd, ntmp)
            A.release(m)
            P.barrier()


        def mm(out, lhsT, rhs, start, stop, reads, writes):
            P.op('pe', lambda e: e.matmul(out, lhsT, rhs, start=start, stop=stop), reads=reads, writes=writes)

        def tt(eng, out, a, b, op, reads, writes):
            P.op(eng, lambda e: e.tensor_tensor(out, a, b, op), reads=reads, writes=writes)

        def stt(out, a, sc, b, op0, op1, reads, writes):
            P.op('dve', lambda e: e.scalar_tensor_tensor(out, a, sc, b, op0, op1), reads=reads, writes=writes)

        def tsc(out, a, s1, op, reads, writes):
            P.op('dve', lambda e: e.tensor_scalar(out, a, s1, None, op), reads=reads, writes=writes)

        def act(out, in_, func, reads, writes, **kw):
            P.op('act', lambda e: e.activation(out, in_, func, **kw), reads=reads, writes=writes)

        def run_interleaved(gens):
            gens = list(gens)
            while gens:
                for g in list(gens):
                    try:
                        next(g)
                    except StopIteration:
                        gens.remove(g)

        def rwkv_layer(l):
            j = l // 2
            m_layer = A.mark()
            bones32 = A.f32(128)
            bo64 = A.f32(128)
            ones32 = A.f32(128)
            ident32 = A.f32(128)
            keepf = A.f32(1)
            lneps = A.f32(1)
            P.dma('sp', bones32, bones_d, writes=['bones'])
            P.dma('sp', ident32, ident_d, writes=['ident32'])
            P.dma('sp', keepf, keepf_d, writes=['keepf'])
            P.op('dve', lambda e: e.memset(ones32, 1.0), writes=['ones32'])
            P.op('dve', lambda e: e.memset(lneps, 64e-5), writes=['lneps'])
            P.op('dve', lambda e: e.tensor_scalar(bo64, bones32, 1.0 / 64, None, ALU.mult), reads=['bones'], writes=['bo64'])
            vecs = A.f32(72).rearrange("p (i k) -> p i k", i=9)
            muT = A.f32(48).rearrange("p (n k) -> p n k", n=6)
            P.dma('sp', vecs, rvecs_d[j].rearrange("p (i k) -> p i k", i=9), writes=['vecs'])
            P.dma('sp', muT, rmu_d[j].rearrange("p (n k) -> p n k", n=6), writes=['muT'])

            m_d = A.mark()
            hT = A.bf16(8 * T).rearrange("p (c t) -> p c t", c=8)
            xxT = A.bf16(8 * T).rearrange("p (c t) -> p c t", c=8)
            tw = A.bf16(T)
            ta = A.bf16(T)
            gsg = A.bf16(T)
            m_x = A.mark()
            rstd = A.f32(512)
            ntmp = A.f32(512)
            sqb = A.bf16(512)
            mp = A.bf16(T)
            mn = A.bf16(T)
            t1 = A.f32(T)
            t2 = A.f32(T)
            for tb in range(4):
                norm_block(l, 1, tb * 512, 512, hT[:, :, tb * 512:(tb + 1) * 512], f'hT{tb}', sqb, rstd, ntmp)

            def hkeys(c):
                return [f'hT{tb}:{c}' for tb in range(4)]
            allh = [k_ for c in range(8) for k_ in hkeys(c)]
            P.dma('pool', mp, shiftm_d[0], writes=['mp'])
            P.dma('pool', mn, shiftm_d[1], writes=['mn'])
            for c in range(8):
                P.op('pool', lambda e: e.memset(t1[:, 0:1], 0.0), writes=['t1a'])
                tt('pool', t1[:, 1:T], hT[:, c, 0:T - 1], mp[:, 1:T], ALU.mult, hkeys(c) + ['mp'], ['t1b'])
                P.op('dve', lambda e: e.memset(t2[:, T - 1:T], 0.0), writes=['t2a'])
                tt('dve', t2[:, 0:T - 1], hT[:, c, 1:T], mn[:, 0:T - 1], ALU.mult, hkeys(c) + ['mn'], ['t2b'])
                tt('dve', t1, t1, t2, ALU.add, ['t1a', 't1b', 't2a', 't2b'], ['t1a', 't1b'])
                tt('pool', xxT[:, c, :], t1, hT[:, c, :], ALU.subtract, ['t1a', 't1b'] + hkeys(c), [f'xxT:{c}'])
            allxx = [f'xxT:{c}' for c in range(8)]

            def proj(bank, bkey, W, Ws, wkeys, blk):
                for k in range(8):
                    mm(bank, W[:, k, :], hT[:, k, blk], k == 0, False, wkeys + hkeys(k), [bkey])
                for k in range(8):
                    mm(bank, Ws[:, k, :], xxT[:, k, blk], False, k == 7, wkeys + [f'xxT:{k}'], [bkey])

            lws = []
            for i, (src, n) in enumerate([(w1c_d, 1), (a1c_d, 4), (g1_d, 5)]):
                W = A.bf16(8 * 128).rearrange("p (k n) -> p k n", k=8)
                Ws = A.bf16(8 * 128).rearrange("p (k n) -> p k n", k=8)
                P.dma('pool', W, src[j].rearrange("(k p) n -> p k n", p=128), writes=[f'lw:{i}'])
                tt('pool', Ws, W, muT[:, n, :].unsqueeze(2).broadcast_to([128, 8, 128]), ALU.mult, [f'lw:{i}', 'muT'], [f'lws:{i}'])
                lws.append((W, Ws))
            lfun = [AF.Tanh, AF.Copy, AF.Sigmoid]
            ldst = [(tw, 'tw'), (ta, 'ta'), (gsg, 'gsg')]
            for tb in range(4):
                blk = slice(tb * 512, (tb + 1) * 512)
                for i in range(3):
                    b = (tb * 3 + i) % 4
                    proj(ps[b], f'ps:{b}', lws[i][0], lws[i][1], [f'lw:{i}', f'lws:{i}'], blk)
                    act(ldst[i][0][:, blk], ps[b], lfun[i], [f'ps:{b}'], [f'{ldst[i][1]}:{tb}'])
            A.release(m_x)
            P.barrier()
            if rstage < 2:
                A.release(m_layer)
                P.barrier()
                return

            w2x = [A.bf16(D) for _ in range(2)]
            a2x = [A.bf16(D) for _ in range(2)]
            g2t = A.bf16(D)
            for d in range(2):
                P.dma('pool', w2x[d], w2x_d[j, d], writes=[f'w2x:{d}'])
                P.dma('pool', a2x[d], a2x_d[j, d], writes=[f'a2x:{d}'])
            P.dma('pool', g2t, g2_d[j], writes=['g2t'])
            Wp = [A.bf16(8 * 128).rearrange("p (k n) -> p k n", k=8) for _ in range(6)]
            TL = [A.f32(512) for _ in range(13)]
            r32, k32, v32, kka, kk, sq, lwn, at, kd, bt, rk, gb, vt32 = TL
            for p in range(8):
                pc = slice(p * 128, (p + 1) * 128)
                for i, n in enumerate((0, 2, 3)):
                    P.dma('pool', Wp[2 * i], wrkv_d[j, i, :, pc].rearrange("(k q) n -> q k n", q=128), writes=[f'Wp:{2 * i}'])
                    tt('pool', Wp[2 * i + 1], Wp[2 * i], muT[:, n, :].unsqueeze(2).broadcast_to([128, 8, 128]), ALU.mult,
                       [f'Wp:{2 * i}', 'muT'], [f'Wp:{2 * i + 1}'])
                for tb in range(4):
                    blk = slice(tb * 512, (tb + 1) * 512)
                    proj(ps[0], 'ps:0', Wp[0], Wp[1], ['Wp:0', 'Wp:1'], blk)
                    act(r32, ps[0], AF.Copy, ['ps:0'], ['r32'])
                    P.dma('sp', SCR['R'][pc, blk], r32, reads=['r32'], writes=[K('scr')])
                    proj(ps[1], 'ps:1', Wp[2], Wp[3], ['Wp:2', 'Wp:3'], blk)
                    act(k32, ps[1], AF.Copy, ['ps:1'], ['k32'])
                    proj(ps[2], 'ps:2', Wp[4], Wp[5], ['Wp:4', 'Wp:5'], blk)
                    act(v32, ps[2], AF.Copy, ['ps:2'], ['v32'])
                    mm(ps[7], g2t[:, pc], gsg[:, blk], True, True, ['g2t', f'gsg:{tb}'], ['ps:7'])
                    act(gb, ps[7], AF.Copy, ['ps:7'], ['gb'])
                    P.dma('sp', SCR['G'][pc, blk], gb, reads=['gb'], writes=[K('scr')])
                    act(kka, k32, AF.Copy, ['k32', 'vecs'], ['kka'], scale=vecs[:, 5, p:p + 1])
                    act(kk, k32, AF.Copy, ['k32', 'vecs'], ['kk'], scale=vecs[:, 4, p:p + 1])
                    tt('pool', sq, kk, kk, ALU.mult, ['kk'], ['sq'])
                    mm(ps[3], bones32, sq, True, True, ['bones', 'sq'], ['ps:3'])
                    act(sq, ps[3], AF.Sqrt, ['ps:3'], ['sq'], bias=1e-12, scale=1.0)
                    P.op('dve', lambda e: e.reciprocal(sq, sq), reads=['sq'], writes=['sq'])
                    tt('pool', kk, kk, sq, ALU.mult, ['kk', 'sq'], ['kk'])
                    P.dma('sp', SCR['KA'][pc, blk], kk, reads=['kk'], writes=[K('scr')])
                    for d in range(2):
                        mm(ps[4], w2x[d][:, pc], tw[:, blk], True, True, [f'w2x:{d}', f'tw:{tb}'], ['ps:4'])
                        act(lwn, ps[4], AF.Sigmoid, ['ps:4', 'vecs'], ['lwn'], bias=vecs[:, d, p:p + 1], scale=1.0)
                        P.op('pool', lambda e: e.tensor_scalar(lwn, lwn, float(np.exp(-0.5)), None, ALU.mult), reads=['lwn'], writes=['lwn'])
                        P.dma('sp', SCR[f'LW{d}'][pc, blk], lwn, reads=['lwn'], writes=[K('scr')])
                        mm(ps[5], a2x[d][:, pc], ta[:, blk], True, True, [f'a2x:{d}', f'ta:{tb}'], ['ps:5'])
                        act(at, ps[5], AF.Sigmoid, ['ps:5', 'vecs'], ['at'], bias=vecs[:, 2 + d, p:p + 1], scale=1.0)
                        stt(kd, at, -1.0, kka, ALU.add, ALU.mult, ['at', 'kka'], ['kd'])
                        tt('pool', kd, kd, k32, ALU.add, ['kd', 'k32'], ['kd'])
                        P.dma('sp', SCR[f'KD{d}'][pc, blk], kd, reads=['kd'], writes=[K('scr')])
                        tt('pool', bt, kk, at, ALU.mult, ['kk', 'at'], ['bt'])
                        P.dma('sp', SCR[f'B{d}'][pc, blk], bt, reads=['bt'], writes=[K('scr')])
                        stt(rk, r32, vecs[:, 6, p:p + 1], kd, ALU.mult, ALU.mult, ['r32', 'kd', 'vecs'], ['rk'])
                        mm(ps[6], bones32, rk, d == 0, d == 1, ['bones', 'rk'], ['ps:6'])
                    tt('dve', gb, v32, ps[6], ALU.mult, ['v32', 'ps:6', 'gb'], ['gb'])
                    P.dma('sp', SCR['BON'][pc, blk], gb, reads=['gb'], writes=[K('scr')])
                    for sub in range(4):
                        tk = slice(tb * 512 + sub * 128, tb * 512 + (sub + 1) * 128)
                        osl = ps[3][:, sub * 128:(sub + 1) * 128]
                        for k in range(8):
                            mm(osl, hT[:, k, tk], Wp[4][:, k, :], k == 0, False, ['Wp:4'] + hkeys(k), ['ps:3'])
                        for k in range(8):
                            mm(osl, xxT[:, k, tk], Wp[5][:, k, :], False, k == 7, ['Wp:5', f'xxT:{k}'], ['ps:3'])
                    act(vt32, ps[3], AF.Copy, ['ps:3'], ['vt32'])
                    P.dma('sp', Vt_s[tb * 512:(tb + 1) * 512, pc].rearrange("(s t) n -> t s n", t=128),
                          vt32.rearrange("p (s n) -> p s n", s=4), reads=['vt32'], writes=[K('scr')])
            A.release(m_d)
            P.barrier()
            if rstage < 3:
                A.release(m_layer)
                P.barrier()
                return

            m_s = A.mark()
            maskA = []
            maskN = []
            for d in range(2):
                ma = A.f32(512)
                mnn = A.f32(128)
                P.dma('sp', ma, rmask_d[d, :, 0:512], writes=[f'maskA:{d}'])
                P.dma('sp', mnn, rmask_d[d, :, 512:640], writes=[f'maskN:{d}'])
                maskA.append(ma)
                maskN.append(mnn)
            bd16 = A.f32(256)
            P.dma('sp', bd16, bd16_d, writes=['bd16'])
            NU = 2
            U = []
            for u in range(NU):
                t = {}
                for nm in ('cum', 'cu2', 'E1', 'E2', 'E3', 'Kx0', 'Kx1', 'Bx0', 'Bx1', 'Kp', 'Bp', 'N0', 'N1', 'W0', 'W1',
                           'Ux0', 'Ux1', 'KTt', 'BTt', 'Hx', 'yt', 'dl', 'gn',
                           'Md0', 'Md1', 'Nd0', 'Nd1', 'Mo0', 'Mo1', 'Pa0', 'Pa1', 'Pb0', 'Pb1', 'X'):
                    t[nm] = A.f32(128)
                t['KR'] = A.f32(256)
                t['AT0'] = A.f32(512)
                t['AT1'] = A.f32(512)
                t['MN'] = [[A.f32(256) for _ in range(3)] for _ in range(2)]
                t['ld'] = [{nm: A.f32(128) for nm in ('r', 'ka', 'kd', 'b', 'lw', 'Vx0', 'Vx1')} for _ in range(2)]
                for nm in ('Kx0', 'Kx1', 'Bx0', 'Bx1', 'Ux0', 'Ux1'):
                    P.op('pool', lambda e, tl=t[nm]: e.memset(tl, 0.0), writes=[f'u{u}:{nm}'])
                for par in range(2):
                    for nm in ('Vx0', 'Vx1'):
                        P.op('pool', lambda e, tl=t['ld'][par][nm]: e.memset(tl, 0.0), writes=[f'u{u}:ld{par}:{nm}'])
                U.append(t)

            def chain(p, d, u):
                t = U[u]
                kp = f'u{u}:'
                base = 4 * u
                bA = [ps[base], ps[base + 1]]
                bAk = [f'ps:{base}', f'ps:{base + 1}']
                bB = ps[base + 2]
                bC = ps[base + 3]
                pr = slice(p * 128, (p + 1) * 128)
                Hx = t['Hx']
                P.dma('sp', Hx, st0_d[j, d, p], writes=[kp + 'Hx'])
                yscr = SCR['YF'] if d == 0 else SCR['YB']
                order = list(range(16)) if d == 0 else list(range(15, -1, -1))
                for ci, c in enumerate(order):
                    par = ci % 2
                    L = t['ld'][par]
                    lk = lambda nm: f'{kp}ld{par}:{nm}'
                    cs = slice(c * 128, (c + 1) * 128)
                    P.dma('sp', L['r'], SCR['R'][pr, cs], writes=[lk('r')])
                    P.dma('sp', L['ka'], SCR['KA'][pr, cs], writes=[lk('ka')])
                    P.dma('sp', L['kd'], SCR[f'KD{d}'][pr, cs], writes=[lk('kd')])
                    P.dma('sp', L['b'], SCR[f'B{d}'][pr, cs], writes=[lk('b')])
                    P.dma('sp', L['lw'], SCR[f'LW{d}'][pr, cs], writes=[lk('lw')])
                    for h in range(2):
                        P.dma('sp', L[f'Vx{h}'][:, h * 64:(h + 1) * 64], Vt_s[cs, p * 128 + h * 64:p * 128 + (h + 1) * 64],
                              writes=[lk(f'Vx{h}')])
                    yield
                    cum = t['cum']
                    P.op('dve', lambda e, cum=cum, L=L: e.tensor_tensor_scan(cum, ones32, L['lw'], 0.0, ALU.mult, ALU.add),
                         reads=[lk('lw'), 'ones32'], writes=[kp + 'cum'])
                    if d == 0:
                        cu = cum
                        cuk = kp + 'cum'
                    else:
                        cu = t['cu2']
                        cuk = kp + 'cu2'
                        tt('dve', cu, L['lw'], cum, ALU.subtract, [lk('lw'), kp + 'cum'], [cuk])
                        tsc(cu, cu, cum[:, 127:128], ALU.add, [cuk, kp + 'cum'], [cuk])
                    act(t['E1'], cu, AF.Exp, [cuk], [kp + 'E1'], scale=-1.0)
                    act(t['E2'], cu, AF.Exp, [cuk], [kp + 'E2'], scale=1.0)
                    tt('pool', t['E3'], cu, L['lw'], ALU.subtract, [cuk, lk('lw')], [kp + 'E3'])
                    act(t['E3'], t['E3'], AF.Exp, [kp + 'E3'], [kp + 'E3'], scale=-1.0)
                    gam = t['E1'][:, 127:128] if d == 0 else t['E1'][:, 0:1]
                    KR = t['KR']
                    tt('pool', KR[:, 128:256], L['r'], t['E1'], ALU.mult, [lk('r'), kp + 'E1'], [kp + 'KRr'])
                    tt('pool', KR[:, 0:128], L['ka'], t['E3'], ALU.mult, [lk('ka'), kp + 'E3'], [kp + 'KRk'])
                    for h in range(2):
                        hs = slice(h * 64, (h + 1) * 64)
                        tt('pool', t[f'Kx{h}'][hs, :], L['kd'][hs, :], t['E2'][hs, :], ALU.mult, [lk('kd'), kp + 'E2'], [kp + f'Kx{h}'])
                        stt(t[f'Bx{h}'][hs, :], L['b'][hs, :], -1.0, t['E2'][hs, :], ALU.mult, ALU.mult, [lk('b'), kp + 'E2'], [kp + f'Bx{h}'])
                    stt(t['Kp'], L['kd'], gam, t['E2'], ALU.mult, ALU.mult, [lk('kd'), kp + 'E1', kp + 'E2'], [kp + 'Kp'])
                    tsc(t['gn'][:, 0:1], gam, -1.0, ALU.mult, [kp + 'E1'], [kp + 'gn'])
                    stt(t['Bp'], L['b'], t['gn'][:, 0:1], t['E2'], ALU.mult, ALU.mult, [lk('b'), kp + 'gn', kp + 'E2'], [kp + 'Bp'])
                    yield
                    AT = [t['AT0'], t['AT1']]
                    Nn = [t['N0'], t['N1']]
                    for h in range(2):
                        mm(bA[h][:, 0:256], t[f'Kx{h}'], KR, True, True, [kp + f'Kx{h}', kp + 'KRr', kp + 'KRk'], [bAk[h]])
                        mm(bA[h][:, 256:512], t[f'Bx{h}'], KR, True, True, [kp + f'Bx{h}', kp + 'KRr', kp + 'KRk'], [bAk[h]])
                        mm(bB[:, h * 128:(h + 1) * 128], KR[:, 0:128], t[f'Bx{h}'], True, True, [kp + f'Bx{h}', kp + 'KRk'], [f'ps:{base + 2}'])
                    for h in range(2):
                        tt('dve', AT[h], bA[h], maskA[d], ALU.mult, [bAk[h], f'maskA:{d}'], [kp + f'AT{h}'])
                        tt('dve', Nn[h], bB[:, h * 128:(h + 1) * 128], maskN[d], ALU.mult, [f'ps:{base + 2}', f'maskN:{d}'], [kp + f'N{h}'])
                    yield
                    wps = bB[:, 256:384]
                    import os as _os
                    _rw = _os.environ.get('RW_W', '')
                    if _rw == '1':
                        mm(wps, KR[:, 0:128], Hx, True, True, [kp + 'KRk', kp + 'Hx'], [f'ps:{base + 2}'])
                    elif _rw == '2':
                        mm(wps, AT[0][:, 0:128], L['Vx0'], True, True, [kp + 'AT0', lk('Vx0')], [f'ps:{base + 2}'])
                    elif _rw == '3':
                        mm(wps, AT[0][:, 0:128], L['r'], True, True, [kp + 'AT0', lk('r')], [f'ps:{base + 2}'])
                    else:
                        mm(wps, KR[:, 0:128], Hx, True, False, [kp + 'KRk', kp + 'Hx'], [f'ps:{base + 2}'])
                        for h in range(2):
                            mm(wps, AT[h][:, 0:128], L[f'Vx{h}'], False, h == 1, [kp + f'AT{h}', lk(f'Vx{h}')], [f'ps:{base + 2}'])
                    Wt = [t['W0'], t['W1']]
                    act(Wt[0], wps, AF.Copy, [f'ps:{base + 2}'], [kp + 'W0'])
                    yield
                    Md = [t['Md0'], t['Md1']]
                    Nd = [t['Nd0'], t['Nd1']]
                    Mo = [t['Mo0'], t['Mo1']]
                    Pa = [t['Pa0'], t['Pa1']]
                    Pb = [t['Pb0'], t['Pb1']]
                    for h in range(2):
                        tt('pool', Md[h], AT[h][:, 256:384], bd16[:, 0:128], ALU.mult, [kp + f'AT{h}', 'bd16'], [kp + f'Md{h}'])
                        tt('pool', Mo[h], AT[h][:, 256:384], bd16[:, 128:256], ALU.mult, [kp + f'AT{h}', 'bd16'], [kp + f'Mo{h}'])
                        tt('pool', Nd[h], Nn[h], bd16[:, 0:128], ALU.mult, [kp + f'N{h}', 'bd16'], [kp + f'Nd{h}'])
                        tt('pool', Pa[h], Md[h], ident32, ALU.add, [kp + f'Md{h}', 'ident32'], [kp + f'Pa{h}'])
                    yield
                    bCk = f'ps:{base + 3}'
                    bBk = f'ps:{base + 2}'
                    MN2 = [t['MN'][h][0] for h in range(2)]
                    MN4 = [t['MN'][h][1] for h in range(2)]
                    N8 = [t['MN'][h][2][:, 0:128] for h in range(2)]
                    for h in range(2):
                        cps = bC[:, h * 256:(h + 1) * 256]
                        mm(cps[:, 0:128], Nd[h], Md[h], True, True, [kp + f'Md{h}', kp + f'Nd{h}'], [bCk])
                        mm(cps[:, 128:256], Md[h], Nd[h], True, True, [kp + f'Md{h}', kp + f'Nd{h}'], [bCk])
                    for h in range(2):
                        act(MN2[h], bC[:, h * 256:(h + 1) * 256], AF.Copy, [bCk], [kp + f'MN2{h}'])
                    yield
                    for h in range(2):
                        cps = bC[:, h * 256:(h + 1) * 256]
                        mm(cps[:, 0:128], MN2[h][:, 128:256], MN2[h][:, 0:128], True, True, [kp + f'MN2{h}'], [bCk])
                        mm(cps[:, 128:256], MN2[h][:, 0:128], MN2[h][:, 128:256], True, True, [kp + f'MN2{h}'], [bCk])
                    for h in range(2):
                        act(MN4[h], bC[:, h * 256:(h + 1) * 256], AF.Copy, [bCk], [kp + f'MN4{h}'])
                    yield
                    for h in range(2):
                        mm(bC[:, h * 128:(h + 1) * 128], MN4[h][:, 0:128], MN4[h][:, 128:256], True, True, [kp + f'MN4{h}'], [bCk])
                    for h in range(2):
                        act(N8[h], bC[:, h * 128:(h + 1) * 128], AF.Copy, [bCk], [kp + f'N8{h}'])
                    yield
                    Pc, Pn, pck, pnk = Pa, Pb, 'Pa', 'Pb'
                    for lhs, lk_ in ((lambda h: MN2[h][:, 128:256], 'MN2'), (lambda h: MN4[h][:, 128:256], 'MN4'), (lambda h: N8[h], 'N8')):
                        for h in range(2):
                            mm(bC[:, h * 128:(h + 1) * 128], lhs(h), Pc[h], True, True, [kp + f'{lk_}{h}', kp + f'{pck}{h}'], [bCk])
                        for h in range(2):
                            tt('dve', Pn[h], Pc[h], bC[:, h * 128:(h + 1) * 128], ALU.add, [kp + f'{pck}{h}', bCk], [kp + f'{pnk}{h}'])
                        Pc, Pn, pck, pnk = Pn, Pc, pnk, pck
                        yield
                    TdT, tdk = Pc, pck
                    Wm = Wt[0]
                    Ut = Wt[1]
                    Uk = kp + 'W1'
                    Xt = t['X']
                    ups = bC[:, 0:128]
                    zps = bB[:, 256:384]
                    for h in range(2):
                        hc_ = slice(h * 64, (h + 1) * 64)
                        mm(ups[:, hc_], TdT[h], Wm[:, hc_], True, True, [kp + f'{tdk}{h}', kp + 'W0'], [bCk])
                    act(Ut, ups, AF.Copy, [bCk], [Uk])
                    yield
                    for it in range(7):
                        for h in range(2):
                            hc_ = slice(h * 64, (h + 1) * 64)
                            mm(zps[:, hc_], Mo[h], Ut[:, hc_], True, True, [kp + f'Mo{h}', Uk], [bBk])
                        tt('dve', Xt, Wm, zps, ALU.add, [kp + 'W0', bBk], [kp + 'X'])
                        for h in range(2):
                            hc_ = slice(h * 64, (h + 1) * 64)
                            mm(ups[:, hc_], TdT[h], Xt[:, hc_], True, True, [kp + f'{tdk}{h}', kp + 'X'], [bCk])
                        act(Ut, ups, AF.Copy, [bCk], [Uk])
                        yield
                    for h in range(2):
                        P.op('pool', lambda e, h=h, Ut=Ut: e.tensor_copy(t[f'Ux{h}'][:, h * 64:(h + 1) * 64], Ut[:, h * 64:(h + 1) * 64]),
                             reads=[Uk], writes=[kp + f'Ux{h}'])
                    yps = bB[:, 384:512]
                    mm(yps, Hx, KR[:, 128:256], True, False, [kp + 'Hx', kp + 'KRr'], [f'ps:{base + 2}'])
                    for h in range(2):
                        mm(yps, L[f'Vx{h}'], AT[h][:, 128:256], False, False, [lk(f'Vx{h}'), kp + f'AT{h}'], [f'ps:{base + 2}'])
                    for h in range(2):
                        mm(yps, t[f'Ux{h}'], AT[h][:, 384:512], False, h == 1, [kp + f'Ux{h}', kp + f'AT{h}'], [f'ps:{base + 2}'])
                    act(t['yt'], yps, AF.Copy, [f'ps:{base + 2}'], [kp + 'yt'])
                    P.dma('sp', yscr[pr, cs], t['yt'], reads=[kp + 'yt'], writes=[f'Y{d}:{p}:{c}'])
                    yield
                    mm(bA[0][:, 0:128], t['Kp'], ident32, True, True, [kp + 'Kp', 'ident32'], [bAk[0]])
                    mm(bA[0][:, 128:256], t['Bp'], ident32, True, True, [kp + 'Bp', 'ident32'], [bAk[0]])
                    act(t['KTt'], bA[0][:, 0:128], AF.Copy, [bAk[0]], [kp + 'KTt'])
                    act(t['BTt'], bA[0][:, 128:256], AF.Copy, [bAk[0]], [kp + 'BTt'])
                    dps = bA[1][:, 0:128]
                    mm(dps, t['KTt'], L['Vx0'], True, False, [kp + 'KTt', lk('Vx0')], [bAk[1]])
                    mm(dps, t['KTt'], L['Vx1'], False, False, [kp + 'KTt', lk('Vx1')], [bAk[1]])
                    mm(dps, t['BTt'], Ut, False, True, [kp + 'BTt', Uk], [bAk[1]])
                    tt('dve', t['dl'], dps, bones32, ALU.mult, [bAk[1], 'bones'], [kp + 'dl'])
                    stt(Hx, Hx, gam, t['dl'], ALU.mult, ALU.add, [kp + 'Hx', kp + 'E1', kp + 'dl'], [kp + 'Hx'])
                    if ci % 2 == 1:
                        seq = c // 2
                        P.dma('sp', ostate_d[j, seq, d, p], Hx, reads=[kp + 'Hx'], writes=[K('ost')], final=True)
                        if ci < 15:
                            tsc(Hx, Hx, keepf[:, 0:1], ALU.mult, [kp + 'Hx', 'keepf'], [kp + 'Hx'])
                    yield

            PT = [A.f32(512) for _ in range(6)]
            def limited(g):
                n = 0
                for _ in g:
                    n += 1
                    if n >= rstop:
                        return
                    yield

            for p in range(8):
                if rstop < 999:
                    if p > 0:
                        break
                    import os as _os
                    if _os.environ.get('RW_SINGLE') == '0':
                        run_interleaved([limited(chain(p, 0, 0))])
                    elif _os.environ.get('RW_SINGLE') == '1':
                        run_interleaved([limited(chain(p, 1, 1))])
                    else:
                        run_interleaved([limited(chain(p, 0, 0)), limited(chain(p, 1, 1))])
                    continue
                run_interleaved([chain(p, 0, 0), chain(p, 1, 1)])
                pr = slice(p * 128, (p + 1) * 128)
                for tb in range(4):
                    blk = slice(tb * 512, (tb + 1) * 512)
                    yf, yb, dd, s2, bn, gg_ = PT
                    ykeys = [f'Y{d}:{p}:{c}' for d in range(2) for c in range(tb * 4, tb * 4 + 4)]
                    P.dma('sp', yf, SCR['YF'][pr, blk], reads=ykeys, writes=['pt:yf'])
                    P.dma('sp', yb, SCR['YB'][pr, blk], reads=ykeys, writes=['pt:yb'])
                    P.dma('sp', bn, SCR['BON'][pr, blk], writes=['pt:bn'])
                    P.dma('sp', gg_, SCR['G'][pr, blk], writes=['pt:gg'])
                    tt('pool', yf, yf, yb, ALU.add, ['pt:yf', 'pt:yb'], ['pt:yf'])
                    mm(ps[0], bo64, yf, True, True, ['bo64', 'pt:yf'], ['ps:0'])
                    tt('dve', dd, yf, ps[0], ALU.subtract, ['pt:yf', 'ps:0'], ['pt:dd'])
                    tt('pool', s2, dd, dd, ALU.mult, ['pt:dd'], ['pt:s2'])
                    mm(ps[1], bo64, s2, True, True, ['bo64', 'pt:s2'], ['ps:1'])
                    act(s2, ps[1], AF.Sqrt, ['ps:1', 'lneps'], ['pt:s2'], bias=lneps[:, 0:1], scale=1.0)
                    P.op('dve', lambda e, s2=s2: e.reciprocal(s2, s2), reads=['pt:s2'], writes=['pt:s2'])
                    tt('pool', dd, dd, s2, ALU.mult, ['pt:dd', 'pt:s2'], ['pt:dd'])
                    P.op('dve', lambda e, dd=dd, p=p: e.tensor_scalar(dd, dd, vecs[:, 7, p:p + 1], vecs[:, 8, p:p + 1], ALU.mult, ALU.add),
                         reads=['pt:dd', 'vecs'], writes=['pt:dd'])
                    tt('pool', dd, dd, bn, ALU.add, ['pt:dd', 'pt:bn'], ['pt:dd'])
                    tt('pool', dd, dd, gg_, ALU.mult, ['pt:dd', 'pt:gg'], ['pt:dd'])
                    P.dma('sp', SCR['YG'][pr, blk], dd, reads=['pt:dd'], writes=[f'YG:{p}:{tb}'])
            A.release(m_s)
            P.barrier()
            if rstage < 4:
                A.release(m_layer)
                P.barrier()
                return

            m_o = A.mark()
            Wo = A.bf16(8 * 1024).rearrange("p (k n) -> p k n", k=8)
            ygT = A.bf16(8 * 512).rearrange("p (c t) -> p c t", c=8)
            mT = A.f32(8 * 512).rearrange("p (c t) -> p c t", c=8)
            rstd = A.f32(512)
            ntmp = A.f32(512)
            sqb = A.bf16(512)
            P.dma('pool', Wo, wor_d[j].rearrange("(k p) n -> p k n", p=128), writes=['Wo'])
            for tb in range(4):
                blk = slice(tb * 512, (tb + 1) * 512)
                for c in range(8):
                    P.dma('pool', ygT[:, c, :], SCR['YG'][c * 128:(c + 1) * 128, blk], writes=[f'ygT:{c}'])
                for c in range(8):
                    b = c % 2
                    for k in range(8):
                        mm(ps[b], Wo[:, k, c * 128:(c + 1) * 128], ygT[:, k, :], k == 0, k == 7, ['Wo', f'ygT:{k}'], [f'ps:{b}'])
                    act(mT[:, c, :], ps[b], AF.Copy, [f'ps:{b}'], [f'mT:{c}'])
                    act(sqb, ps[b], AF.Square, [f'ps:{b}'], ['sqb'])
                    mm(ps[2], ones_bf, sqb, c == 0, c == 7, ['sqb', 'ones'], ['ps:2'])
                residual_update(l, 8, tb * 512, 512, mT, 'mT', rstd, ntmp)
            A.release(m_layer)
            P.barrier()

        for l in range(n_layers):
            if l % 2 == 0 and stage >= 2:
                attn_layer(l)
            elif l % 2 == 1 and do_rwkv:
                rwkv_layer(l)
            if stage >= 4:
                mlp(l)

        for c in range(8):
            P.dma('sp', yT_d[c * 128:(c + 1) * 128, :], xT[:, c, :], reads=[f'x:{c}:{tb}' for tb in range(4)], writes=[K('yT')], final=True)
        P.emit()
    return nc, P


_PROG = {}


def _rope_tables(rows=32, grid_w=64):
    row = np.repeat(np.arange(rows), grid_w).astype(np.float32)
    col = np.tile(np.arange(grid_w), rows).astype(np.float32)
    inv = (1.0 / (np.float32(10000.0) ** (np.arange(0, 32, 2, dtype=np.float32) / np.float32(32)))).astype(np.float32)
    ar = row[:, None] * inv[None, :]
    ac = col[:, None] * inv[None, :]
    ang = np.concatenate([ar, ar, ac, ac], axis=-1).astype(np.float32)
    return np.cos(ang).astype(np.float32), np.sin(ang).astype(np.float32)


def _tok_major_table(t):
    return np.ascontiguousarray(t.reshape(16, 128, 64).transpose(1, 0, 2).reshape(128, 1024))


def kernel(x_prompt, x_sample, c, cache_k_gqa, cache_v_gqa, cache_k_diff, cache_v_diff, state_rwkv,
           c_ctx, w_ada, b_ada, norm_gains, attn_w_in, attn_w_out, attn_qk_gain, diff_lambda, diff_subln,
           rwkv_mu, rwkv_w_rkv, rwkv_w_o, rwkv_w0, rwkv_w1, rwkv_w2, rwkv_a0, rwkv_a1, rwkv_a2,
           rwkv_g1, rwkv_g2, rwkv_kvec, rwkv_lnx, mlp_w1, mlp_w2):
    f = lambda a: np.ascontiguousarray(np.asarray(a, dtype=np.float32))
    x_prompt, x_sample, c = f(x_prompt), f(x_sample), f(c)
    if 'nc' not in _PROG:
        _PROG['nc'] = build_program()[0]
    nc = _PROG['nc']

    qa_perm = np.concatenate([np.r_[h * 64:(h + 1) * 64, (4 + h) * 64:(5 + h) * 64] for h in range(4)])
    cols = np.concatenate([qa_perm, np.arange(768, 1280), np.arange(512, 640), np.arange(1280, 1792),
                           np.arange(640, 768), np.arange(1792, 2304)])
    w_in = f(np.asarray(attn_w_in)[:, :, cols])
    rows = np.concatenate([qa_perm, np.arange(512, 1024)])
    w_out = f(np.asarray(attn_w_out)[:, rows, :])
    gqk = f(np.concatenate([np.broadcast_to(np.asarray(attn_qk_gain)[:, None, 0, :], (2, 128, 64)),
                            np.broadcast_to(np.asarray(attn_qk_gain)[:, None, 1, :], (2, 128, 64))], axis=2))
    lamv = f(np.broadcast_to(np.asarray(diff_lambda).reshape(2, 1, 256), (2, 128, 256)))
    subln = f(np.broadcast_to(np.asarray(diff_subln).reshape(2, 1, 128), (2, 128, 128)))
    bada = f(np.asarray(b_ada).reshape(4, 48, 128).transpose(0, 2, 1))
    gains = f(np.asarray(norm_gains).reshape(4, 4, 8, 128).transpose(0, 3, 1, 2).reshape(4, 128, 32))
    ident = np.eye(128, dtype=np.float32)
    cos, sin = _rope_tables()
    sinS = sin.reshape(2048, 2, 2, 16).copy()
    sinS[:, :, 0, :] *= -1.0
    sinS = sinS.reshape(2048, 64)
    cos_s, sin_s = _tok_major_table(cos), _tok_major_table(sinS)
    cos_p, sin_p = np.ones((128, 1024), np.float32), np.zeros((128, 1024), np.float32)
    mask_s = np.zeros((128, 8, 20), np.float32)
    mask_p = np.full((128, 8, 20), -25.0, np.float32)
    for i in range(8):
        mask_p[:, i, 4 + 2 * i: 6 + 2 * i] = 0.0
    w_ada_, w1_, w2_ = f(w_ada), f(mlp_w1), f(mlp_w2)

    W1, A1 = np.asarray(rwkv_w1), np.asarray(rwkv_a1)
    w1c = f(np.concatenate([W1[:, 0], W1[:, 1]], axis=-1))
    a1c = f(np.concatenate([A1[:, 0], A1[:, 1]], axis=-1))
    w2x = np.zeros((2, 2, 128, 1024), np.float32)
    a2x = np.zeros((2, 2, 128, 1024), np.float32)
    for d_ in range(2):
        w2x[:, d_, d_ * 64:(d_ + 1) * 64, :] = np.asarray(rwkv_w2)[:, d_]
        a2x[:, d_, d_ * 64:(d_ + 1) * 64, :] = np.asarray(rwkv_a2)[:, d_]
    rmu = f(np.asarray(rwkv_mu).reshape(2, 6, 8, 128).transpose(0, 3, 1, 2).reshape(2, 128, 48))
    vec9 = np.stack([np.asarray(rwkv_w0)[:, 0], np.asarray(rwkv_w0)[:, 1], np.asarray(rwkv_a0)[:, 0], np.asarray(rwkv_a0)[:, 1],
                     np.asarray(rwkv_kvec)[:, 0], np.asarray(rwkv_kvec)[:, 1], np.asarray(rwkv_kvec)[:, 2],
                     np.asarray(rwkv_lnx)[:, 0], np.asarray(rwkv_lnx)[:, 1]], axis=1)
    rvecs = f(vec9.reshape(2, 9, 8, 128).transpose(0, 3, 1, 2).reshape(2, 128, 72))
    bones = np.zeros((128, 128), np.float32)
    bones[0:64, 0:64] = 1.0
    bones[64:128, 64:128] = 1.0
    ii = np.arange(128)
    lt = (ii[:, None] < ii[None, :]).astype(np.float32)
    le = (ii[:, None] <= ii[None, :]).astype(np.float32)
    rmask = np.zeros((2, 128, 640), np.float32)
    rmask[0] = np.concatenate([lt, le, lt, le, lt.T], axis=1)
    rmask[1] = np.concatenate([lt.T, le.T, lt.T, le.T, lt], axis=1)
    b16 = (ii[:, None] // 16 == ii[None, :] // 16).astype(np.float32)
    bd16 = f(np.concatenate([b16, 1.0 - b16], axis=1))
    tpos = np.arange(2048)
    sh_s = np.stack([np.where(tpos == 0, 0.0, 0.5), np.where(tpos == 2047, 0.0, 0.5)]).astype(np.float32)
    sh_p = np.stack([np.where(tpos % 256 == 0, 0.0, 0.5), np.where(tpos % 256 == 255, 0.0, 0.5)]).astype(np.float32)
    sh_s = f(np.broadcast_to(sh_s[:, None, :], (2, 128, 2048)))
    sh_p = f(np.broadcast_to(sh_p[:, None, :], (2, 128, 2048)))
    wrkv_, wor_, g1_, g2_ = f(rwkv_w_rkv), f(rwkv_w_o), f(rwkv_g1), f(rwkv_g2)
    st_all = np.asarray(state_rwkv, dtype=np.float32)

    in_maps = []
    for core in range(8):
        if core < 4:
            b = core
            x = x_sample[b]
            cond = c[b]
            kT = np.concatenate([np.asarray(cache_k_gqa)[b].reshape(2, 512, 128).transpose(0, 2, 1),
                                 np.asarray(cache_k_diff)[b].reshape(2, 512, 512).transpose(0, 2, 1)], axis=1)
            v = np.concatenate([np.asarray(cache_v_gqa)[b].reshape(2, 512, 128),
                                np.asarray(cache_v_diff)[b].reshape(2, 512, 512)], axis=2)
            cs, sn, mk = cos_s, sin_s, mask_s
            st0 = np.zeros((2, 2, 8, 128, 128), np.float32)
            for p_ in range(8):
                for h_ in range(2):
                    st0[:, :, p_, h_ * 64:(h_ + 1) * 64, h_ * 64:(h_ + 1) * 64] = st_all[b][:, :, 2 * p_ + h_].transpose(0, 1, 3, 2)
            keep, shm = np.ones((128, 1), np.float32), sh_s
        else:
            st0 = np.zeros((2, 2, 8, 128, 128), np.float32)
            keep, shm = np.zeros((128, 1), np.float32), sh_p
            p0 = (core - 4) * 8
            x = x_prompt[p0:p0 + 8].reshape(2048, 1024)
            cond = np.asarray(c_ctx)
            kT = np.zeros((2, 640, 512), np.float32)
            v = np.zeros((2, 512, 640), np.float32)
            cs, sn, mk = cos_p, sin_p, mask_p
        in_maps.append({
            "xT": f(x.T), "cond": f(np.asarray(cond).reshape(8, 128).T), "b_ada": bada, "gains": gains,
            "w_ada": w_ada_, "w_in": w_in, "w_out": w_out, "gqk": gqk, "lamv": lamv, "subln": subln,
            "kTc": f(kT), "vc": f(v), "maskb": f(mk.reshape(128, 160)), "cos": f(cs), "sin": f(sn),
            "mlp_w1": w1_, "mlp_w2": w2_, "ident": ident,
            "wrkv": wrkv_, "wo_r": wor_, "w1c": w1c, "a1c": a1c, "g1": g1_, "w2x": w2x, "a2x": a2x, "g2": g2_,
            "rmu": rmu, "rvecs": rvecs, "st0": st0, "keepf": keep, "shiftm": shm, "bones": bones, "rmask": rmask, "bd16": bd16,
        })
    if _PROG.get('dbg_cores'):
        ncd = _PROG['dbg_cores']
        return run_bass_kernel_spmd(nc, [in_maps[i] for i in ncd], core_ids=list(range(len(ncd)))).results
    res = run_bass_kernel_spmd(nc, in_maps, core_ids=list(range(8))).results
    _PROG['last'] = res

    y_sample = np.stack([res[b]["yT"].T for b in range(4)], axis=0).astype(np.float32)
    y_prompt = np.concatenate([res[4 + g]["yT"].T.reshape(8, 256, 1024) for g in range(4)], axis=0).astype(np.float32)
    okv = np.concatenate([res[4 + g]["okv"].reshape(2, 8, 256, 1280).transpose(1, 0, 2, 3) for g in range(4)], axis=0)
    new_k_gqa = np.ascontiguousarray(okv[..., 0:128].reshape(32, 2, 256, 2, 64)).astype(np.float32)
    new_k_diff = np.ascontiguousarray(okv[..., 128:640].reshape(32, 2, 256, 4, 2, 64)).astype(np.float32)
    new_v_gqa = np.ascontiguousarray(okv[..., 640:768].reshape(32, 2, 256, 2, 64)).astype(np.float32)
    new_v_diff = np.ascontiguousarray(okv[..., 768:1280].reshape(32, 2, 256, 4, 128)).astype(np.float32)
    new_state = np.zeros((32, 2, 2, 16, 64, 64), np.float32)
    for g in range(4):
        os_ = res[4 + g]["ostate"]
        for p_ in range(8):
            for h_ in range(2):
                blkv = os_[:, :, :, p_, h_ * 64:(h_ + 1) * 64, h_ * 64:(h_ + 1) * 64]
                new_state[8 * g:8 * g + 8, :, :, 2 * p_ + h_] = blkv.transpose(1, 0, 2, 4, 3)
    return (y_prompt, y_sample, new_k_gqa, new_v_gqa, new_k_diff, new_v_diff, new_state)
```

```python
from contextlib import ExitStack
import numpy as np
import concourse.bass as bass
import concourse.mybir as mybir
from concourse.bass_utils import run_bass_kernel_spmd

F32 = mybir.dt.float32
BF16 = mybir.dt.bfloat16
AF = mybir.ActivationFunctionType
ALU = mybir.AluOpType
AX = mybir.AxisListType

ENG_NAMES = ['pe', 'act', 'dve', 'pool', 'sp']


class _Op:
    __slots__ = ('eng', 'fn', 'deps', 'flag', 'num', 'sk', 'inc', 'is_dma', 'prev_same_sem')

    def __init__(self, eng, fn, deps, is_dma):
        self.eng = eng
        self.fn = fn
        self.deps = deps
        self.flag = is_dma
        self.num = 0
        self.sk = None
        self.inc = 16 if is_dma else 1
        self.is_dma = is_dma
        self.prev_same_sem = None


class Prog:
    def __init__(self, nc, ndma=24):
        self.nc = nc
        self.all = []
        self.last_w = {}
        self.readers = {}
        self.ndma = ndma
        self.dma_rr = {e: 0 for e in ENG_NAMES}
        self.dma_last = {}
        self.finals = []
        self.bar = None
        self.bar_tile = None

    def barrier(self):
        deps = {}
        for o in self.last_w.values():
            deps[id(o)] = o
        for r in self.readers.values():
            for v in r.values():
                for o in (v if isinstance(v, list) else [v]):
                    deps[id(o)] = o
        if self.bar is not None:
            deps[id(self.bar)] = self.bar
        bt = self.bar_tile
        o = _Op('dve', (lambda e: e.memset(bt, 0.0)), list(deps.values()), False)
        self.all.append(o)
        self.last_w = {}
        self.readers = {}
        self.bar = o

    def _collect(self, eng, reads, writes):
        deps = []
        for k in reads:
            o = self.last_w.get(k)
            if o is not None:
                deps.append(o)
        for k in writes:
            o = self.last_w.get(k)
            if o is not None:
                deps.append(o)
            r = self.readers.get(k)
            if r:
                for v in r.values():
                    if isinstance(v, list):
                        deps.extend(v)
                    else:
                        deps.append(v)
        if self.bar is not None:
            deps.append(self.bar)
        out = []
        seen = set()
        for d in deps:
            if id(d) in seen:
                continue
            seen.add(id(d))
            if d.eng == 'pe' and eng == 'pe' and not d.is_dma:
                continue
            out.append(d)
        return out

    def _commit(self, op, reads, writes):
        for k in writes:
            self.last_w[k] = op
            self.readers[k] = {}
        for k in reads:
            r = self.readers.setdefault(k, {})
            if op.is_dma:
                r.setdefault('dma', []).append(op)
            else:
                r[op.eng] = op

    def op(self, eng, fn, reads=(), writes=()):
        deps = self._collect(eng, reads, writes)
        o = _Op(eng, fn, deps, False)
        self.all.append(o)
        self._commit(o, reads, writes)
        return o

    def dma(self, q, out, in_, reads=(), writes=(), final=False):
        deps = self._collect(q, reads, writes)
        o = _Op(q, (lambda e, out=out, in_=in_: e.dma_start(out=out, in_=in_)), deps, True)
        i = self.dma_rr[q]
        self.dma_rr[q] += 1
        o.sk = ('d', q, i % self.ndma)
        o.prev_same_sem = self.dma_last.get(o.sk)
        self.dma_last[o.sk] = o
        self.all.append(o)
        self._commit(o, reads, writes)
        if final:
            self.finals.append(o)
        return o

    def emit(self):
        nc = self.nc
        for o in self.all:
            for d in o.deps:
                d.flag = True
        cnt = {}
        for o in self.all:
            if o.is_dma:
                k = cnt.get(o.sk, 0) + 1
                cnt[o.sk] = k
                o.num = 16 * k
            elif o.flag:
                o.sk = ('c', o.eng)
                k = cnt.get(o.sk, 0) + 1
                cnt[o.sk] = k
                o.num = k
        waited = {e: {} for e in ENG_NAMES}
        streams = {e: [] for e in ENG_NAMES}
        for o in self.all:
            need = {}
            for d in o.deps:
                if waited[o.eng].get(d.sk, 0) >= d.num:
                    continue
                need[d.sk] = max(need.get(d.sk, 0), d.num)
            if o.is_dma and o.prev_same_sem is not None:
                p = o.prev_same_sem
                if waited[o.eng].get(p.sk, 0) < p.num:
                    need[p.sk] = max(need.get(p.sk, 0), p.num)
            for sk, v in need.items():
                waited[o.eng][sk] = v
            streams[o.eng].append((list(need.items()), o))
        fin = {}
        for o in self.finals:
            fin[o.sk] = max(fin.get(o.sk, 0), o.num)
        self.n_ops = {e: len(streams[e]) for e in ENG_NAMES}
        self.streams = streams
        self.fin = fin
        with ExitStack() as st:
            sems = {}
            for sk in cnt:
                sems[sk] = st.enter_context(nc.semaphore("s_" + "_".join(str(x) for x in sk)))
            block = st.enter_context(nc.Block())

            def mk(name):
                def f(eng):
                    for waits, o in streams[name]:
                        for (wk, v) in waits:
                            eng.wait_ge(sems[wk], v)
                        ins = o.fn(eng)
                        if o.flag:
                            ins.then_inc(sems[o.sk], o.inc)
                    if name == 'sp':
                        for sk, v in fin.items():
                            eng.wait_ge(sems[sk], v)
                return f

            block.tensor(mk('pe'))
            block.scalar(mk('act'))
            block.vector(mk('dve'))
            block.gpsimd(mk('pool'))
            block.sync(mk('sp'))


D = 1024
T = 2048
NKT = 20
EPS = 1e-6
VW = 652
LAM_INIT = {0: 0.8 - 0.6 * float(np.exp(-0.3 * 0)), 2: 0.8 - 0.6 * float(np.exp(-0.3 * 2))}


class Arena:
    def __init__(self, ap, nwords):
        self.ap = ap
        self.n = nwords
        self.top = 0

    def mark(self):
        return self.top

    def release(self, m):
        self.top = m

    def f32(self, n):
        a = self.top
        self.top += (n + 7) // 8 * 8
        assert self.top <= self.n, ("arena overflow", self.top, self.n)
        return self.ap[:, a:a + n]

    def bf16(self, n):
        w = (n + 1) // 2
        a = self.top
        self.top += (w + 7) // 8 * 8
        assert self.top <= self.n, ("arena overflow", self.top, self.n)
        return self.ap[:, a:a + w].bitcast(BF16)[:, 0:n]


def build_program(n_layers=4, do_rwkv=True, stage=9, rstage=9, rstop=999):
    nc = bass.Bass("TRN2", target_bir_lowering=False)

    def din(name, shape):
        return nc.dram_tensor(name, list(shape), F32, kind="ExternalInput").ap()

    def dout(name, shape):
        return nc.dram_tensor(name, list(shape), F32, kind="ExternalOutput").ap()

    xT_d = din("xT", [D, T])
    cond_d = din("cond", [128, 8])
    bada_d = din("b_ada", [4, 128, 48])
    gains_d = din("gains", [4, 128, 32])
    wada_d = din("w_ada", [4, D, 6 * D])
    win_d = din("w_in", [2, D, 2304])
    wout_d = din("w_out", [2, D, D])
    gqk_d = din("gqk", [2, 128, 128])
    lamv_d = din("lamv", [2, 128, 256])
    subln_d = din("subln", [2, 128, 128])
    kTc_d = din("kTc", [2, 640, 512])
    vc_d = din("vc", [2, 512, 640])
    maskb_d = din("maskb", [128, 160])
    cos_d = din("cos", [128, 16 * 64])
    sin_d = din("sin", [128, 16 * 64])
    w1_d = din("mlp_w1", [4, D, 4 * D])
    w2_d = din("mlp_w2", [4, 4 * D, D])
    ident_d = din("ident", [128, 128])

    wrkv_d = din("wrkv", [2, 3, D, D])
    wor_d = din("wo_r", [2, D, D])
    w1c_d = din("w1c", [2, D, 128])
    a1c_d = din("a1c", [2, D, 128])
    g1_d = din("g1", [2, D, 128])
    w2x_d = din("w2x", [2, 2, 128, D])
    a2x_d = din("a2x", [2, 2, 128, D])
    g2_d = din("g2", [2, 128, D])
    rmu_d = din("rmu", [2, 128, 48])
    rvecs_d = din("rvecs", [2, 128, 72])
    st0_d = din("st0", [2, 2, 8, 128, 128])
    keepf_d = din("keepf", [128, 1])
    shiftm_d = din("shiftm", [2, 128, T])
    bones_d = din("bones", [128, 128])
    rmask_d = din("rmask", [2, 128, 640])
    bd16_d = din("bd16", [128, 256])

    yT_d = dout("yT", [D, T])
    okv_d = dout("okv", [2, T, 1280])
    ostate_d = dout("ostate", [2, 8, 2, 8, 128, 128])

    def dscr(name, shape):
        return nc.dram_tensor(name, list(shape), F32, kind="Internal").ap()

    SCR = {n: dscr("scr_" + n, [D, T]) for n in ("R", "KA", "KD0", "KD1", "B0", "B1", "LW0", "LW1", "G", "BON", "YF", "YB", "YG")}
    Vt_s = dscr("scr_Vt", [T, D])

    P = Prog(nc)
    NW = 53000
    with ExitStack() as st:
        arena_t = st.enter_context(nc.sbuf_tensor("arena", [128, NW], F32))
        A = Arena(arena_t, NW)
        psbig = st.enter_context(nc.psum_tensor("psbig", [128, 4096], F32))
        ps = [psbig[:, i * 512:(i + 1) * 512] for i in range(8)]
        psb = [p.bitcast(BF16) for p in ps]

        uid = [0]

        def K(s):
            uid[0] += 1
            return f"{s}#{uid[0]}"

        xT = A.f32(8 * T).rearrange("p (c t) -> p c t", c=8)
        ones_bf = A.bf16(128)
        ident_bf = A.bf16(128)
        cs_box = {}
        maskb = A.f32(160).rearrange("p (i k) -> p i k", i=8)
        mods = A.f32(4 * 48).rearrange("p (l m) -> p l m", l=4)
        gains = A.f32(4 * 32).rearrange("p (l m) -> p l m", l=4)
        gsv = A.f32(4 * 32).rearrange("p (l m) -> p l m", l=4)
        condT = A.f32(8)
        silu_bf = A.bf16(8)
        bada = A.f32(4 * 48).rearrange("p (l m) -> p l m", l=4)
        epsb = A.f32(1)
        P.bar_tile = A.f32(1)

        P.op('dve', lambda e: e.memset(ones_bf, 1.0), writes=['ones'])
        P.op('dve', lambda e: e.memset(epsb, EPS), writes=['epsb'])
        P.dma('pool', ident_bf, ident_d, writes=['identbf'])
        P.dma('sp', maskb, maskb_d.rearrange("p (i k) -> p i k", i=8), writes=['maskb'])
        P.dma('sp', condT, cond_d, writes=['cond'])
        P.dma('sp', bada, bada_d.rearrange("l p m -> p l m"), writes=['bada'])
        P.dma('sp', gains, gains_d.rearrange("l p m -> p l m"), writes=['gains'])
        for c in range(8):
            P.dma('sp', xT[:, c, :], xT_d[c * 128:(c + 1) * 128, :], writes=[f'x:{c}:{tb}' for tb in range(4)])

        P.op('act', lambda e: e.activation(silu_bf, condT, AF.Silu), reads=['cond'], writes=['silu'])
        m0 = A.mark()
        wab = [A.bf16(8 * 512).rearrange("p (k n) -> p k n", k=8) for _ in range(2)]
        bi = 0
        for l in range(n_layers):
            for blk in range(12):
                buf = wab[bi % 2]
                key = f'wab:{bi % 2}'
                bi += 1
                P.dma('pool', buf, wada_d[l, :, blk * 512:(blk + 1) * 512].rearrange("(k p) n -> p k n", p=128),
                      writes=[key])
                for jj in range(4):
                    j = blk * 4 + jj
                    for k in range(8):
                        P.op('pe', lambda e, buf=buf, jj=jj, k=k, j=j: e.matmul(
                            ps[7][:, j:j + 1], buf[:, k, jj * 128:(jj + 1) * 128], silu_bf[:, k:k + 1],
                            start=(k == 0), stop=(k == 7)), reads=[key, 'silu'], writes=['ps:7'])
            P.op('dve', lambda e, l=l: e.tensor_tensor(mods[:, l, :], ps[7][:, 0:48], bada[:, l, :], ALU.add),
                 reads=['ps:7', 'bada'], writes=[f'mods:{l}'])
            P.op('dve', lambda e, l=l: e.scalar_tensor_tensor(gsv[:, l, 0:8], mods[:, l, 8:16], 1.0, gains[:, l, 0:8],
                                                              ALU.add, ALU.mult), reads=[f'mods:{l}', 'gains'], writes=[f'gsv:{l}:0'])
            P.op('dve', lambda e, l=l: e.tensor_tensor(gsv[:, l, 8:16], mods[:, l, 16:24], gains[:, l, 8:16], ALU.mult),
                 reads=[f'mods:{l}', 'gains'], writes=[f'gsv:{l}:1'])
            P.op('dve', lambda e, l=l: e.scalar_tensor_tensor(gsv[:, l, 16:24], mods[:, l, 32:40], 1.0, gains[:, l, 16:24],
                                                              ALU.add, ALU.mult), reads=[f'mods:{l}', 'gains'], writes=[f'gsv:{l}:2'])
            P.op('dve', lambda e, l=l: e.tensor_tensor(gsv[:, l, 24:32], mods[:, l, 40:48], gains[:, l, 24:32], ALU.mult),
                 reads=[f'mods:{l}', 'gains'], writes=[f'gsv:{l}:3'])
        A.release(m0)
        P.barrier()

        def rstd_from_ps(pst, n, dst, scale, keyr, keyw):
            P.op('act', lambda e: e.activation(dst, pst, AF.Sqrt, bias=epsb[:, 0:1], scale=scale),
                 reads=keyr + ['epsb'], writes=[keyw])
            P.op('dve', lambda e: e.reciprocal(dst, dst), reads=[keyw], writes=[keyw])

        def norm_block(l, which, t0, n, hT, hkey, sqb, rstd, tmp):
            gi, shi = (0, 0) if which == 1 else (16, 24)
            tb = t0 // 512
            for c in range(8):
                P.op('act', lambda e, c=c: e.activation(sqb[:, 0:n], xT[:, c, t0:t0 + n], AF.Square),
                     reads=[f'x:{c}:{tb}'], writes=['sqb'])
                P.op('pe', lambda e, c=c: e.matmul(ps[2][:, 0:n], ones_bf, sqb[:, 0:n], start=(c == 0), stop=(c == 7)),
                     reads=['sqb', 'ones'], writes=['ps:2'])
            rstd_from_ps(ps[2][:, 0:n], n, rstd[:, 0:n], 1.0 / D, ['ps:2'], 'rstd')
            for c in range(8):
                P.op('dve', lambda e, c=c: e.scalar_tensor_tensor(tmp[:, 0:n], xT[:, c, t0:t0 + n], gsv[:, l, gi + c:gi + c + 1],
                                                                  rstd[:, 0:n], ALU.mult, ALU.mult),
                     reads=[f'x:{c}:{tb}', 'rstd', f'gsv:{l}:{gi // 8}'], writes=['ntmp'])
                P.op('act', lambda e, c=c: e.activation(hT[:, c, 0:n], tmp[:, 0:n], AF.Identity,
                                                        bias=mods[:, l, shi + c:shi + c + 1], scale=1.0),
                     reads=['ntmp', f'mods:{l}'], writes=[f'{hkey}:{c}'])

        def residual_update(l, gidx, t0, n, mT, mkey, rstd, tmp):
            tb = t0 // 512
            rstd_from_ps(ps[2][:, 0:n], n, rstd[:, 0:n], 1.0 / D, ['ps:2'], 'rstd')
            for c in range(8):
                P.op('dve', lambda e, c=c: e.scalar_tensor_tensor(tmp[:, 0:n], mT[:, c, 0:n], gsv[:, l, gidx + c:gidx + c + 1],
                                                                  rstd[:, 0:n], ALU.mult, ALU.mult),
                     reads=[f'{mkey}:{c}', 'rstd', f'gsv:{l}:{gidx // 8}'], writes=['ntmp'])
                P.op('pool', lambda e, c=c: e.tensor_tensor(xT[:, c, t0:t0 + n], xT[:, c, t0:t0 + n], tmp[:, 0:n], ALU.add),
                     reads=['ntmp', f'x:{c}:{tb}'], writes=[f'x:{c}:{tb}'])

        def rope_tm(src, nh, st_idx, dst_bf, t1, t2, kin, kout):
            s5 = src.rearrange("p (h a b d) -> p h a b d", h=nh, a=2, b=2)
            t5 = t1.rearrange("p (h a b d) -> p h a b d", h=nh, a=2, b=2)
            sin_t, cos_t = cs_box['sin'], cs_box['cos']
            sn = sin_t[:, st_idx, :].rearrange("p (a b d) -> p a b d", a=2, b=2)
            P.op('dve', lambda e: e.tensor_tensor(t5[:, :, :, 0, :], s5[:, :, :, 1, :],
                                                  sn[:, :, 0, :].unsqueeze(1).broadcast_to([128, nh, 2, 16]), ALU.mult),
                 reads=kin + ['sin'], writes=[kout + 'a'])
            P.op('pool', lambda e: e.tensor_tensor(t5[:, :, :, 1, :], s5[:, :, :, 0, :],
                                                   sn[:, :, 1, :].unsqueeze(1).broadcast_to([128, nh, 2, 16]), ALU.mult),
                 reads=kin + ['sin'], writes=[kout + 'b'])
            s3 = src.rearrange("p (h d) -> p h d", h=nh)
            P.op('dve', lambda e: e.tensor_tensor(t2.rearrange("p (h d) -> p h d", h=nh), s3,
                                                  cos_t[:, st_idx, :].unsqueeze(1).broadcast_to([128, nh, 64]), ALU.mult),
                 reads=kin + ['cos'], writes=[kout + 'c'])
            P.op('dve', lambda e: e.tensor_tensor(dst_bf, t1, t2, ALU.add),
                 reads=[kout + 'a', kout + 'b', kout + 'c'], writes=[kout])

        def head_rmsnorm(src, nh, gain_bc, ss, tmp, kin, kout):
            s3 = src.rearrange("p (h d) -> p h d", h=nh)
            t3 = tmp.rearrange("p (h d) -> p h d", h=nh)
            P.op('dve', lambda e: e.tensor_tensor(tmp, src, src, ALU.mult), reads=kin, writes=['hn_tmp'])
            P.op('dve', lambda e: e.tensor_reduce(ss[:, 0:nh], t3, AX.X, ALU.add), reads=['hn_tmp'], writes=['hn_ss'])
            P.op('act', lambda e: e.activation(ss[:, 0:nh], ss[:, 0:nh], AF.Sqrt, bias=epsb[:, 0:1], scale=1.0 / 64),
                 reads=['hn_ss', 'epsb'], writes=['hn_ss'])
            P.op('dve', lambda e: e.reciprocal(ss[:, 0:nh], ss[:, 0:nh]), reads=['hn_ss'], writes=['hn_ss'])
            P.op('dve', lambda e: e.tensor_tensor(s3, s3, ss[:, 0:nh].unsqueeze(2).broadcast_to([128, nh, 64]), ALU.mult),
                 reads=kin + ['hn_ss'], writes=[kout])
            P.op('dve', lambda e: e.tensor_tensor(s3, s3, gain_bc.unsqueeze(1).broadcast_to([128, nh, 64]), ALU.mult),
                 reads=[kout, 'gqk'], writes=[kout])

        def attn_layer(l):
            j = l // 2
            lam_init = LAM_INIT[l]
            m_layer = A.mark()
            cos_t = A.f32(1024).rearrange("p (s d) -> p s d", s=16)
            sin_t = A.f32(1024).rearrange("p (s d) -> p s d", s=16)
            cs_box['cos'], cs_box['sin'] = cos_t, sin_t
            P.dma('sp', cos_t, cos_d.rearrange("p (s d) -> p s d", s=16), writes=['cos'])
            P.dma('sp', sin_t, sin_d.rearrange("p (s d) -> p s d", s=16), writes=['sin'])
            KT = A.bf16(5 * 2560).rearrange("p (c t) -> p c t", c=5)
            Vp = A.bf16(NKT * VW).rearrange("p (k w) -> p k w", k=NKT)
            gqk = A.f32(128)
            lamv = A.f32(256)
            sublnS = A.f32(128)
            lam_t = A.f32(4)
            rstd = A.f32(512)
            ntmp = A.f32(512)
            sqb = A.bf16(512)
            P.dma('sp', gqk, gqk_d[j], writes=['gqk'])
            P.dma('sp', lamv, lamv_d[j], writes=['lamv'])
            P.dma('sp', sublnS, subln_d[j], writes=['subln'])
            l4 = lamv.rearrange("p (a d) -> p a d", a=4)
            P.op('dve', lambda e: e.tensor_tensor(l4[:, 0, :], l4[:, 0, :], l4[:, 1, :], ALU.mult), reads=['lamv'], writes=['lamv'])
            P.op('dve', lambda e: e.tensor_tensor(l4[:, 2, :], l4[:, 2, :], l4[:, 3, :], ALU.mult), reads=['lamv'], writes=['lamv'])
            P.op('dve', lambda e: e.tensor_reduce(lam_t[:, 0:1], l4[:, 0, :], AX.X, ALU.add), reads=['lamv'], writes=['lam'])
            P.op('dve', lambda e: e.tensor_reduce(lam_t[:, 1:2], l4[:, 2, :], AX.X, ALU.add), reads=['lamv'], writes=['lam'])
            P.op('act', lambda e: e.activation(lam_t[:, 0:2], lam_t[:, 0:2], AF.Exp), reads=['lam'], writes=['lam'])
            P.op('dve', lambda e: e.tensor_tensor(lam_t[:, 2:3], lam_t[:, 0:1], lam_t[:, 1:2], ALU.subtract), reads=['lam'], writes=['lam'])
            P.op('dve', lambda e: e.tensor_scalar(lam_t[:, 2:3], lam_t[:, 2:3], lam_init, None, ALU.add), reads=['lam'], writes=['lam'])
            P.op('dve', lambda e: e.tensor_scalar(sublnS, sublnS, 1.0 - lam_init, None, ALU.mult), reads=['subln'], writes=['subln'])
            P.op('pool', lambda e: e.memset(Vp, 1.0), writes=[f'Vp:{k}' for k in range(NKT)])
            for c in range(5):
                P.dma('pool', KT[:, c, 0:512], kTc_d[j, c * 128:(c + 1) * 128, :], writes=[f'KT:{c}:c'])
            for k in range(4):
                P.dma('pool', Vp[:, k, 0:132].rearrange("p (h w) -> p h w", h=2)[:, :, 0:64],
                      vc_d[j, k * 128:(k + 1) * 128, 0:128].rearrange("p (h d) -> p h d", h=2),
                      reads=[f'Vp:{k}'], writes=[f'Vp:{k}'])
                P.dma('pool', Vp[:, k, 132:652].rearrange("p (h w) -> p h w", h=4)[:, :, 0:128],
                      vc_d[j, k * 128:(k + 1) * 128, 128:640].rearrange("p (h d) -> p h d", h=4),
                      reads=[f'Vp:{k}'], writes=[f'Vp:{k}'])

            m1 = A.mark()
            Wkv = A.bf16(8 * 1280).rearrange("p (k n) -> p k n", k=8)
            hT = A.bf16(8 * 512).rearrange("p (c t) -> p c t", c=8)
            kv32 = A.f32(1280)
            t1 = A.f32(640)
            t2 = A.f32(640)
            kb16 = A.bf16(640)
            ss = A.f32(16)
            P.dma('pool', Wkv, win_d[j, :, 1024:2304].rearrange("(k p) n -> p k n", p=128), writes=['Wkv'])
            for tb in range(4):
                norm_block(l, 1, tb * 512, 512, hT, 'hT', sqb, rstd, ntmp)
                for s in range(4):
                    sti = tb * 4 + s
                    tok0 = sti * 128
                    for cb, (c0, cn) in enumerate([(0, 512), (512, 512), (1024, 256)]):
                        for k in range(8):
                            P.op('pe', lambda e, k=k, s=s, c0=c0, cn=cn, cb=cb: e.matmul(
                                ps[cb % 2][:, 0:cn], hT[:, k, s * 128:(s + 1) * 128], Wkv[:, k, c0:c0 + cn],
                                start=(k == 0), stop=(k == 7)), reads=[f'hT:{k}', 'Wkv'], writes=[f'ps:{cb % 2}'])
                        eng = 'act' if cb != 1 else 'dve'
                        if eng == 'act':
                            P.op('act', lambda e, c0=c0, cn=cn, cb=cb: e.activation(kv32[:, c0:c0 + cn], ps[cb % 2][:, 0:cn], AF.Copy),
                                 reads=[f'ps:{cb % 2}'], writes=[f'kv32:{cb}'])
                        else:
                            P.op('dve', lambda e, c0=c0, cn=cn, cb=cb: e.tensor_copy(kv32[:, c0:c0 + cn], ps[cb % 2][:, 0:cn]),
                                 reads=[f'ps:{cb % 2}'], writes=[f'kv32:{cb}'])
                    head_rmsnorm(kv32[:, 0:128], 2, gqk[:, 64:128], ss, t1[:, 0:128], ['kv32:0'], 'kv32:0')
                    P.dma('sp', okv_d[j, tok0:tok0 + 128, :], kv32, reads=['kv32:0', 'kv32:1', 'kv32:2'], writes=[K('okv')], final=True)
                    rope_tm(kv32[:, 0:640], 10, sti, kb16, t1, t2, ['kv32:0', 'kv32:1'], 'kb16')
                    for c in range(5):
                        P.op('pe', lambda e, c=c: e.transpose(psb[3][:, c * 128:(c + 1) * 128], kb16[:, c * 128:(c + 1) * 128], ident_bf),
                             reads=['kb16', 'identbf'], writes=['ps:3'])
                    P.op('act', lambda e, tok0=tok0: e.activation(KT[:, :, 512 + tok0:512 + tok0 + 128],
                                                                 psb[3][:, 0:640].rearrange("p (c t) -> p c t", c=5), AF.Copy),
                         reads=['ps:3'], writes=[f'KT:{c}:{sti}' for c in range(5)])
                    kt = 4 + sti
                    P.op('pool', lambda e, kt=kt: e.tensor_copy(Vp[:, kt, 0:132].rearrange("p (h w) -> p h w", h=2)[:, :, 0:64],
                                                               kv32[:, 640:768].rearrange("p (h d) -> p h d", h=2)),
                         reads=['kv32:1', 'kv32:2', f'Vp:{kt}'], writes=[f'Vp:{kt}'])
                    P.op('pool', lambda e, kt=kt: e.tensor_copy(Vp[:, kt, 132:652].rearrange("p (h w) -> p h w", h=4)[:, :, 0:128],
                                                               kv32[:, 768:1280].rearrange("p (h d) -> p h d", h=4)),
                         reads=['kv32:1', 'kv32:2', f'Vp:{kt}'], writes=[f'Vp:{kt}'])
            A.release(m1)
            P.barrier()
            if stage < 3:
                A.release(m_layer)
                return

            Wq = A.bf16(8 * 1024).rearrange("p (k n) -> p k n", k=8)
            Wo = A.bf16(8 * 1024).rearrange("p (k n) -> p k n", k=8)
            hTq = A.bf16(8 * 256).rearrange("p (c t) -> p c t", c=8)
            QT = A.bf16(8 * 256).rearrange("p (c t) -> p c t", c=8)
            catT = A.bf16(8 * 256).rearrange("p (c t) -> p c t", c=8)
            q32 = A.f32(1024)
            qt1 = A.f32(1024)
            qt2 = A.f32(1024)
            qb16 = A.bf16(1024)
            cat = [A.bf16(1024) for _ in range(2)]
            Pt = [A.bf16(512) for _ in range(2)]
            mT = A.f32(8 * 256).rearrange("p (c t) -> p c t", c=8)
            ss = A.f32(16)
            rz = A.f32(8)
            od = A.f32(128)
            od2 = A.f32(128)
            P.dma('pool', Wq, win_d[j, :, 0:1024].rearrange("(k p) n -> p k n", p=128), writes=['Wq'])
            P.dma('pool', Wo, wout_d[j].rearrange("(k p) n -> p k n", p=128), writes=['Wo'])
            all_kt_keys = [[f'KT:{c}:c'] + [f'KT:{c}:{s}' for s in range(16)] for c in range(5)]
            pti = 0
            for qi in range(8):
                t0 = qi * 256
                tb = t0 // 512
                norm_block(l, 1, t0, 256, hTq, 'hTq', sqb, rstd, ntmp)
                for s in range(2):
                    sti = qi * 2 + s
                    for cb in range(2):
                        for k in range(8):
                            P.op('pe', lambda e, k=k, s=s, cb=cb: e.matmul(
                                ps[cb][:, 0:512], hTq[:, k, s * 128:(s + 1) * 128], Wq[:, k, cb * 512:(cb + 1) * 512],
                                start=(k == 0), stop=(k == 7)), reads=[f'hTq:{k}', 'Wq'], writes=[f'ps:{cb}'])
                        P.op('act', lambda e, cb=cb: e.activation(q32[:, cb * 512:(cb + 1) * 512], ps[cb][:, 0:512], AF.Copy),
                             reads=[f'ps:{cb}'], writes=[f'q32:{cb}'])
                    head_rmsnorm(q32[:, 0:512], 8, gqk[:, 0:64], ss, qt1[:, 0:512], ['q32:0'], 'q32:0')
                    rope_tm(q32, 16, sti, qb16, qt1, qt2, ['q32:0', 'q32:1'], 'qb16')
                    for c in range(8):
                        P.op('pe', lambda e, c=c: e.transpose(psb[3][:, c * 128:(c + 1) * 128], qb16[:, c * 128:(c + 1) * 128], ident_bf),
                             reads=['qb16', 'identbf'], writes=['ps:3'])
                    P.op('act', lambda e, s=s: e.activation(QT[:, :, s * 128:(s + 1) * 128],
                                                           psb[3][:, 0:1024].rearrange("p (c t) -> p c t", c=8), AF.Copy),
                         reads=['ps:3'], writes=[f'QT:{s}'])
                for hc in range(8):
                    gqa = hc < 4
                    kc = 0 if gqa else 1 + (hc - 4)
                    W = 66 if gqa else 130
                    accs = [ps[4], ps[4]] if gqa else [ps[4], ps[5]]
                    acck = ['ps:4', 'ps:4'] if gqa else ['ps:4', 'ps:5']
                    for kt in range(NKT):
                        sb = 0 if kt % 2 == 0 else 6
                        ktkeys = all_kt_keys[kc]
                        for half in range(2):
                            P.op('pe', lambda e, half=half, kt=kt, sb=sb, kc=kc, hc=hc: e.matmul(
                                ps[sb + half][:, 0:256], KT[half * 64:(half + 1) * 64, kc, kt * 128:(kt + 1) * 128],
                                QT[half * 64:(half + 1) * 64, hc, :], start=True, stop=True),
                                reads=ktkeys + ['QT:0', 'QT:1'], writes=[f'ps:{sb + half}'])
                        pt = Pt[pti % 2]
                        pk = f"Pt:{pti % 2}"
                        pti += 1
                        P.op('act', lambda e, pt=pt, sb=sb, kt=kt, qi=qi: e.activation(
                            pt.rearrange("p (b n) -> p b n", b=2),
                            psbig[:, sb * 512:(sb + 2) * 512].rearrange("p (b n) -> p b n", b=2)[:, :, 0:256], AF.Exp,
                            bias=maskb[:, qi, kt:kt + 1], scale=0.125),
                             reads=[f'ps:{sb}', f'ps:{sb + 1}', 'maskb'], writes=[pk])
                        for half in range(2):
                            voff = half * 66 if gqa else 132 + (hc - 4) * 130
                            for s in range(2):
                                a0 = (half * 2 + s) * W if gqa else s * W
                                P.op('pe', lambda e, pt=pt, half=half, s=s, a0=a0, voff=voff, W=W, kt=kt, accs=accs, gqa=gqa: e.matmul(
                                    accs[half][:, a0:a0 + W], pt[:, half * 256 + s * 128: half * 256 + (s + 1) * 128],
                                    Vp[:, kt, voff:voff + W], start=(kt == 0 and s == 0 and (half == 0 or not gqa)), stop=(kt == NKT - 1)),
                                    reads=[pk, f'Vp:{kt}'], writes=[acck[half]])
                    if gqa:
                        a3 = ps[4][:, 0:264].rearrange("p (g w) -> p g w", g=4)
                        P.op('dve', lambda e, a3=a3: e.reciprocal(rz[:, 0:4], a3[:, :, 64]), reads=['ps:4'], writes=['rz'])
                        for s in range(2):
                            src = ps[4][:, 0:264].rearrange("p (h s w) -> p h s w", h=2, s=2)[:, :, s, 0:64]
                            rzv = rz[:, 0:4].rearrange("p (h s) -> p h s", h=2)[:, :, s]
                            P.op('dve', lambda e, s=s, src=src, rzv=rzv, hc=hc: e.tensor_tensor(
                                cat[s][:, hc * 128:(hc + 1) * 128].rearrange("p (h d) -> p h d", h=2), src,
                                rzv.unsqueeze(2).broadcast_to([128, 2, 64]), ALU.mult),
                                reads=['ps:4', 'rz'], writes=[f'cat:{s}:{hc}'])
                    else:
                        for s in range(2):
                            P.op('dve', lambda e, s=s: e.reciprocal(rz[:, 0:1], ps[4][:, s * 130 + 128:s * 130 + 129]), reads=['ps:4'], writes=['rz'])
                            P.op('dve', lambda e, s=s: e.reciprocal(rz[:, 1:2], ps[5][:, s * 130 + 128:s * 130 + 129]), reads=['ps:5'], writes=['rz'])
                            P.op('dve', lambda e: e.tensor_tensor(rz[:, 1:2], rz[:, 1:2], lam_t[:, 2:3], ALU.mult), reads=['rz', 'lam'], writes=['rz'])
                            P.op('dve', lambda e, s=s: e.tensor_scalar(od2, ps[5][:, s * 130:s * 130 + 128], rz[:, 1:2], None, ALU.mult),
                                 reads=['ps:5', 'rz'], writes=['od2'])
                            P.op('dve', lambda e, s=s: e.scalar_tensor_tensor(od, ps[4][:, s * 130:s * 130 + 128], rz[:, 0:1], od2,
                                                                              ALU.mult, ALU.subtract), reads=['ps:4', 'rz', 'od2'], writes=['od'])
                            P.op('dve', lambda e: e.tensor_tensor(od2, od, od, ALU.mult), reads=['od'], writes=['od2'])
                            P.op('dve', lambda e: e.tensor_reduce(rz[:, 2:3], od2, AX.X, ALU.add), reads=['od2'], writes=['rz2'])
                            P.op('act', lambda e: e.activation(rz[:, 2:3], rz[:, 2:3], AF.Sqrt, bias=epsb[:, 0:1], scale=1.0 / 128),
                                 reads=['rz2', 'epsb'], writes=['rz2'])
                            P.op('dve', lambda e: e.reciprocal(rz[:, 2:3], rz[:, 2:3]), reads=['rz2'], writes=['rz2'])
                            P.op('dve', lambda e, s=s, hc=hc: e.scalar_tensor_tensor(cat[s][:, hc * 128:(hc + 1) * 128], od, rz[:, 2:3], sublnS,
                                                                                     ALU.mult, ALU.mult),
                                 reads=['od', 'rz2', 'subln'], writes=[f'cat:{s}:{hc}'])
                for s in range(2):
                    for c in range(8):
                        P.op('pe', lambda e, c=c, s=s: e.transpose(psb[3][:, c * 128:(c + 1) * 128], cat[s][:, c * 128:(c + 1) * 128], ident_bf),
                             reads=[f'cat:{s}:{c}', 'identbf'], writes=['ps:3'])
                    P.op('act', lambda e, s=s: e.activation(catT[:, :, s * 128:(s + 1) * 128],
                                                           psb[3][:, 0:1024].rearrange("p (c t) -> p c t", c=8), AF.Copy),
                         reads=['ps:3'], writes=[f'catT:{s}'])
                for c in range(8):
                    b = c % 2
                    for k in range(8):
                        P.op('pe', lambda e, c=c, k=k, b=b: e.matmul(ps[b][:, 0:256], Wo[:, k, c * 128:(c + 1) * 128], catT[:, k, :],
                                                                   start=(k == 0), stop=(k == 7)),
                             reads=['Wo', 'catT:0', 'catT:1'], writes=[f'ps:{b}'])
                    P.op('act', lambda e, c=c, b=b: e.activation(mT[:, c, :], ps[b][:, 0:256], AF.Copy), reads=[f'ps:{b}'], writes=[f'mT:{c}'])
                    P.op('act', lambda e, c=c, b=b: e.activation(sqb[:, 0:256], ps[b][:, 0:256], AF.Square), reads=[f'ps:{b}'], writes=['sqb'])
                    P.op('pe', lambda e, c=c: e.matmul(ps[2][:, 0:256], ones_bf, sqb[:, 0:256], start=(c == 0), stop=(c == 7)),
                         reads=['sqb', 'ones'], writes=['ps:2'])
                residual_update(l, 8, t0, 256, mT, 'mT', rstd, ntmp)
            A.release(m_layer)
            P.barrier()

        def mlp(l):
            m = A.mark()
            hT = A.bf16(8 * 512).rearrange("p (c t) -> p c t", c=8)
            hid = A.bf16(32 * 512).rearrange("p (j t) -> p j t", j=32)
            w1b = [A.bf16(8 * 512).rearrange("p (k n) -> p k n", k=8) for _ in range(2)]
            w2b = [A.bf16(32 * 128).rearrange("p (j n) -> p j n", j=32) for _ in range(2)]
            r32 = [A.f32(512) for _ in range(2)]
            fT = A.f32(8 * 512).rearrange("p (c t) -> p c t", c=8)
            rstd = A.f32(512)
            ntmp = A.f32(512)
            sqb = A.bf16(512)
            wi = 0
            w2i = 0
            for tb in range(4):
                t0 = tb * 512
                norm_block(l, 2, t0, 512, hT, 'hT', sqb, rstd, ntmp)
                for g in range(8):
                    wb = w1b[wi % 2]
                    wk = f'w1b:{wi % 2}'
                    wi += 1
                    P.dma('pool', wb, w1_d[l, :, g * 512:(g + 1) * 512].rearrange("(k p) n -> p k n", p=128), writes=[wk])
                    for jj in range(4):
                        jx = g * 4 + jj
                        b = jx % 2
                        for k in range(8):
                            P.op('pe', lambda e, wb=wb, jj=jj, k=k, b=b: e.matmul(ps[b][:, 0:512], wb[:, k, jj * 128:(jj + 1) * 128], hT[:, k, :],
                                                                                start=(k == 0), stop=(k == 7)),
                                 reads=[wk] + [f'hT:{k}'], writes=[f'ps:{b}'])
                        P.op('act', lambda e, b=b: e.activation(r32[b], ps[b][:, 0:512], AF.Relu), reads=[f'ps:{b}'], writes=[f'r32:{b}'])
                        P.op('pool' if jx % 2 else 'dve', lambda e, b=b, jx=jx: e.tensor_tensor(hid[:, jx, :], r32[b], r32[b], ALU.mult),
                             reads=[f'r32:{b}'], writes=[f'hid:{jx}'])
                for c in range(8):
                    wb = w2b[w2i % 2]
                    wk = f'w2b:{w2i % 2}'
                    w2i += 1
                    P.dma('pool', wb, w2_d[l, :, c * 128:(c + 1) * 128].rearrange("(j p) n -> p j n", p=128), writes=[wk])
                    b = 4 + c % 2
                    for jx in range(32):
                        P.op('pe', lambda e, wb=wb, jx=jx, b=b: e.matmul(ps[b][:, 0:512], wb[:, jx, :], hid[:, jx, :],
                                                                       start=(jx == 0), stop=(jx == 31)),
                             reads=[wk, f'hid:{jx}'], writes=[f'ps:{b}'])
                    P.op('act', lambda e, c=c, b=b: e.activation(fT[:, c, :], ps[b][:, 0:512], AF.Copy), reads=[f'ps:{b}'], writes=[f'fT:{c}'])
                    P.op('act', lambda e, c=c, b=b: e.activation(sqb, ps[b][:, 0:512], AF.Square), reads=[f'ps:{b}'], writes=['sqb'])
                    P.op('pe', lambda e, c=c: e.matmul(ps[2][:, 0:512], ones_bf, sqb, start=(c == 0), stop=(c == 7)),
                         reads=['sqb', 'ones'], writes=['ps:2'])
                residual_update(l, 24, t0, 512, fT, 'fT', rstd, ntmp)
            A.release(m)
            P.barrier()


        def mm(out, lhsT, rhs, start, stop, reads, writes):
            P.op('pe', lambda e: e.matmul(out, lhsT, rhs, start=start, stop=stop), reads=reads, writes=writes)

        def tt(eng, out, a, b, op, reads, writes):
            P.op(eng, lambda e: e.tensor_tensor(out, a, b, op), reads=reads, writes=writes)

        def stt(out, a, sc, b, op0, op1, reads, writes):
            P.op('dve', lambda e: e.scalar_tensor_tensor(out, a, sc, b, op0, op1), reads=reads, writes=writes)

        def tsc(out, a, s1, op, reads, writes):
            P.op('dve', lambda e: e.tensor_scalar(out, a, s1, None, op), reads=reads, writes=writes)

        def act(out, in_, func, reads, writes, **kw):
            P.op('act', lambda e: e.activation(out, in_, func, **kw), reads=reads, writes=writes)

        def run_interleaved(gens):
            gens = list(gens)
            while gens:
                for g in list(gens):
                    try:
                        next(g)
                    except StopIteration:
                        gens.remove(g)

        def rwkv_layer(l):
            j = l // 2
            m_layer = A.mark()
            bones32 = A.f32(128)
            bo64 = A.f32(128)
            ones32 = A.f32(128)
            ident32 = A.f32(128)
            keepf = A.f32(1)
            lneps = A.f32(1)
            P.dma('sp', bones32, bones_d, writes=['bones'])
            P.dma('sp', ident32, ident_d, writes=['ident32'])
            P.dma('sp', keepf, keepf_d, writes=['keepf'])
            P.op('dve', lambda e: e.memset(ones32, 1.0), writes=['ones32'])
            P.op('dve', lambda e: e.memset(lneps, 64e-5), writes=['lneps'])
            P.op('dve', lambda e: e.tensor_scalar(bo64, bones32, 1.0 / 64, None, ALU.mult), reads=['bones'], writes=['bo64'])
            vecs = A.f32(72).rearrange("p (i k) -> p i k", i=9)
            muT = A.f32(48).rearrange("p (n k) -> p n k", n=6)
            P.dma('sp', vecs, rvecs_d[j].rearrange("p (i k) -> p i k", i=9), writes=['vecs'])
            P.dma('sp', muT, rmu_d[j].rearrange("p (n k) -> p n k", n=6), writes=['muT'])

            m_d = A.mark()
            hT = A.bf16(8 * T).rearrange("p (c t) -> p c t", c=8)
            xxT = A.bf16(8 * T).rearrange("p (c t) -> p c t", c=8)
            tw = A.bf16(T)
            ta = A.bf16(T)
            gsg = A.bf16(T)
            m_x = A.mark()
            rstd = A.f32(512)
            ntmp = A.f32(512)
            sqb = A.bf16(512)
            mp = A.bf16(T)
            mn = A.bf16(T)
            t1 = A.f32(T)
            t2 = A.f32(T)
            for tb in range(4):
                norm_block(l, 1, tb * 512, 512, hT[:, :, tb * 512:(tb + 1) * 512], f'hT{tb}', sqb, rstd, ntmp)

            def hkeys(c):
                return [f'hT{tb}:{c}' for tb in range(4)]
            allh = [k_ for c in range(8) for k_ in hkeys(c)]
            P.dma('pool', mp, shiftm_d[0], writes=['mp'])
            P.dma('pool', mn, shiftm_d[1], writes=['mn'])
            for c in range(8):
                P.op('pool', lambda e: e.memset(t1[:, 0:1], 0.0), writes=['t1a'])
                tt('pool', t1[:, 1:T], hT[:, c, 0:T - 1], mp[:, 1:T], ALU.mult, hkeys(c) + ['mp'], ['t1b'])
                P.op('dve', lambda e: e.memset(t2[:, T - 1:T], 0.0), writes=['t2a'])
                tt('dve', t2[:, 0:T - 1], hT[:, c, 1:T], mn[:, 0:T - 1], ALU.mult, hkeys(c) + ['mn'], ['t2b'])
                tt('dve', t1, t1, t2, ALU.add, ['t1a', 't1b', 't2a', 't2b'], ['t1a', 't1b'])
                tt('pool', xxT[:, c, :], t1, hT[:, c, :], ALU.subtract, ['t1a', 't1b'] + hkeys(c), [f'xxT:{c}'])
            allxx = [f'xxT:{c}' for c in range(8)]

            def proj(bank, bkey, W, Ws, wkeys, blk):
                for k in range(8):
                    mm(bank, W[:, k, :], hT[:, k, blk], k == 0, False, wkeys + hkeys(k), [bkey])
                for k in range(8):
                    mm(bank, Ws[:, k, :], xxT[:, k, blk], False, k == 7, wkeys + [f'xxT:{k}'], [bkey])

            lws = []
            for i, (src, n) in enumerate([(w1c_d, 1), (a1c_d, 4), (g1_d, 5)]):
                W = A.bf16(8 * 128).rearrange("p (k n) -> p k n", k=8)
                Ws = A.bf16(8 * 128).rearrange("p (k n) -> p k n", k=8)
                P.dma('pool', W, src[j].rearrange("(k p) n -> p k n", p=128), writes=[f'lw:{i}'])
                tt('pool', Ws, W, muT[:, n, :].unsqueeze(2).broadcast_to([128, 8, 128]), ALU.mult, [f'lw:{i}', 'muT'], [f'lws:{i}'])
                lws.append((W, Ws))
            lfun = [AF.Tanh, AF.Copy, AF.Sigmoid]
            ldst = [(tw, 'tw'), (ta, 'ta'), (gsg, 'gsg')]
            for tb in range(4):
                blk = slice(tb * 512, (tb + 1) * 512)
                for i in range(3):
                    b = (tb * 3 + i) % 4
                    proj(ps[b], f'ps:{b}', lws[i][0], lws[i][1], [f'lw:{i}', f'lws:{i}'], blk)
                    act(ldst[i][0][:, blk], ps[b], lfun[i], [f'ps:{b}'], [f'{ldst[i][1]}:{tb}'])
            A.release(m_x)
            P.barrier()
            if rstage < 2:
                A.release(m_layer)
                P.barrier()
                return

            w2x = [A.bf16(D) for _ in range(2)]
            a2x = [A.bf16(D) for _ in range(2)]
            g2t = A.bf16(D)
            for d in range(2):
                P.dma('pool', w2x[d], w2x_d[j, d], writes=[f'w2x:{d}'])
                P.dma('pool', a2x[d], a2x_d[j, d], writes=[f'a2x:{d}'])
            P.dma('pool', g2t, g2_d[j], writes=['g2t'])
            Wp = [A.bf16(8 * 128).rearrange("p (k n) -> p k n", k=8) for _ in range(6)]
            TL = [A.f32(512) for _ in range(13)]
            r32, k32, v32, kka, kk, sq, lwn, at, kd, bt, rk, gb, vt32 = TL
            for p in range(8):
                pc = slice(p * 128, (p + 1) * 128)
                for i, n in enumerate((0, 2, 3)):
                    P.dma('pool', Wp[2 * i], wrkv_d[j, i, :, pc].rearrange("(k q) n -> q k n", q=128), writes=[f'Wp:{2 * i}'])
                    tt('pool', Wp[2 * i + 1], Wp[2 * i], muT[:, n, :].unsqueeze(2).broadcast_to([128, 8, 128]), ALU.mult,
                       [f'Wp:{2 * i}', 'muT'], [f'Wp:{2 * i + 1}'])
                for tb in range(4):
                    blk = slice(tb * 512, (tb + 1) * 512)
                    proj(ps[0], 'ps:0', Wp[0], Wp[1], ['Wp:0', 'Wp:1'], blk)
                    act(r32, ps[0], AF.Copy, ['ps:0'], ['r32'])
                    P.dma('sp', SCR['R'][pc, blk], r32, reads=['r32'], writes=[K('scr')])
                    proj(ps[1], 'ps:1', Wp[2], Wp[3], ['Wp:2', 'Wp:3'], blk)
                    act(k32, ps[1], AF.Copy, ['ps:1'], ['k32'])
                    proj(ps[2], 'ps:2', Wp[4], Wp[5], ['Wp:4', 'Wp:5'], blk)
                    act(v32, ps[2], AF.Copy, ['ps:2'], ['v32'])
                    mm(ps[7], g2t[:, pc], gsg[:, blk], True, True, ['g2t', f'gsg:{tb}'], ['ps:7'])
                    act(gb, ps[7], AF.Copy, ['ps:7'], ['gb'])
                    P.dma('sp', SCR['G'][pc, blk], gb, reads=['gb'], writes=[K('scr')])
                    act(kka, k32, AF.Copy, ['k32', 'vecs'], ['kka'], scale=vecs[:, 5, p:p + 1])
                    act(kk, k32, AF.Copy, ['k32', 'vecs'], ['kk'], scale=vecs[:, 4, p:p + 1])
                    tt('pool', sq, kk, kk, ALU.mult, ['kk'], ['sq'])
                    mm(ps[3], bones32, sq, True, True, ['bones', 'sq'], ['ps:3'])
                    act(sq, ps[3], AF.Sqrt, ['ps:3'], ['sq'], bias=1e-12, scale=1.0)
                    P.op('dve', lambda e: e.reciprocal(sq, sq), reads=['sq'], writes=['sq'])
                    tt('pool', kk, kk, sq, ALU.mult, ['kk', 'sq'], ['kk'])
                    P.dma('sp', SCR['KA'][pc, blk], kk, reads=['kk'], writes=[K('scr')])
                    for d in range(2):
                        mm(ps[4], w2x[d][:, pc], tw[:, blk], True, True, [f'w2x:{d}', f'tw:{tb}'], ['ps:4'])
                        act(lwn, ps[4], AF.Sigmoid, ['ps:4', 'vecs'], ['lwn'], bias=vecs[:, d, p:p + 1], scale=1.0)
                        P.op('pool', lambda e: e.tensor_scalar(lwn, lwn, float(np.exp(-0.5)), None, ALU.mult), reads=['lwn'], writes=['lwn'])
                        P.dma('sp', SCR[f'LW{d}'][pc, blk], lwn, reads=['lwn'], writes=[K('scr')])
                        mm(ps[5], a2x[d][:, pc], ta[:, blk], True, True, [f'a2x:{d}', f'ta:{tb}'], ['ps:5'])
                        act(at, ps[5], AF.Sigmoid, ['ps:5', 'vecs'], ['at'], bias=vecs[:, 2 + d, p:p + 1], scale=1.0)
                        stt(kd, at, -1.0, kka, ALU.add, ALU.mult, ['at', 'kka'], ['kd'])
                        tt('pool', kd, kd, k32, ALU.add, ['kd', 'k32'], ['kd'])
                        P.dma('sp', SCR[f'KD{d}'][pc, blk], kd, reads=['kd'], writes=[K('scr')])
                        tt('pool', bt, kk, at, ALU.mult, ['kk', 'at'], ['bt'])
                        P.dma('sp', SCR[f'B{d}'][pc, blk], bt, reads=['bt'], writes=[K('scr')])
                        stt(rk, r32, vecs[:, 6, p:p + 1], kd, ALU.mult, ALU.mult, ['r32', 'kd', 'vecs'], ['rk'])
                        mm(ps[6], bones32, rk, d == 0, d == 1, ['bones', 'rk'], ['ps:6'])
                    tt('dve', gb, v32, ps[6], ALU.mult, ['v32', 'ps:6', 'gb'], ['gb'])
                    P.dma('sp', SCR['BON'][pc, blk], gb, reads=['gb'], writes=[K('scr')])
                    for sub in range(4):
                        tk = slice(tb * 512 + sub * 128, tb * 512 + (sub + 1) * 128)
                        osl = ps[3][:, sub * 128:(sub + 1) * 128]
                        for k in range(8):
                            mm(osl, hT[:, k, tk], Wp[4][:, k, :], k == 0, False, ['Wp:4'] + hkeys(k), ['ps:3'])
                        for k in range(8):
                            mm(osl, xxT[:, k, tk], Wp[5][:, k, :], False, k == 7, ['Wp:5', f'xxT:{k}'], ['ps:3'])
                    act(vt32, ps[3], AF.Copy, ['ps:3'], ['vt32'])
                    P.dma('sp', Vt_s[tb * 512:(tb + 1) * 512, pc].rearrange("(s t) n -> t s n", t=128),
                          vt32.rearrange("p (s n) -> p s n", s=4), reads=['vt32'], writes=[K('scr')])
            A.release(m_d)
            P.barrier()
            if rstage < 3:
                A.release(m_layer)
                P.barrier()
                return

            m_s = A.mark()
            maskA = []
            maskN = []
            for d in range(2):
                ma = A.f32(512)
                mnn = A.f32(128)
                P.dma('sp', ma, rmask_d[d, :, 0:512], writes=[f'maskA:{d}'])
                P.dma('sp', mnn, rmask_d[d, :, 512:640], writes=[f'maskN:{d}'])
                maskA.append(ma)
                maskN.append(mnn)
            bd16 = A.f32(256)
            P.dma('sp', bd16, bd16_d, writes=['bd16'])
            NU = 4
            U = []
            for u in range(NU):
                t = {}
                for nm in ('cum', 'cu2', 'E1', 'E2', 'E3', 'Kx0', 'Kx1', 'Bx0', 'Bx1', 'Kp', 'Bp', 'N0', 'N1', 'W0', 'W1',
                           'Ux0', 'Ux1', 'KTt', 'BTt', 'Hx', 'yt', 'dl', 'gn',
                           'Md0', 'Md1', 'Nd0', 'Nd1', 'Mo0', 'Mo1', 'Pa0', 'Pa1', 'Pb0', 'Pb1', 'X'):
                    t[nm] = A.f32(128)
                t['KR'] = A.f32(256)
                t['AT0'] = A.f32(512)
                t['AT1'] = A.f32(512)
                t['MN'] = [[A.f32(256), A.f32(256), A.f32(128)] for _ in range(2)]
                t['ld'] = [{nm: A.f32(128) for nm in ('r', 'ka', 'kd', 'b', 'lw', 'Vx0', 'Vx1')} for _ in range(1)]
                for nm in ('Kx0', 'Kx1', 'Bx0', 'Bx1', 'Ux0', 'Ux1'):
                    P.op('pool', lambda e, tl=t[nm]: e.memset(tl, 0.0), writes=[f'u{u}:{nm}'])
                for par in range(1):
                    for nm in ('Vx0', 'Vx1'):
                        P.op('pool', lambda e, tl=t['ld'][par][nm]: e.memset(tl, 0.0), writes=[f'u{u}:ld{par}:{nm}'])
                U.append(t)

            def chain(p, d, u):
                t = U[u]
                kp = f'u{u}:'
                bX, bY = 2 * u, 2 * u + 1
                bA = [ps[bX], ps[bX]]
                bAk = [f'ps:{bX}', f'ps:{bX}']
                bB = ps[bY]
                bC = ps[bX]
                pr = slice(p * 128, (p + 1) * 128)
                Hx = t['Hx']
                P.dma('sp', Hx, st0_d[j, d, p], writes=[kp + 'Hx'])
                yscr = SCR['YF'] if d == 0 else SCR['YB']
                order = list(range(16)) if d == 0 else list(range(15, -1, -1))
                for ci, c in enumerate(order):
                    par = 0
                    L = t['ld'][par]
                    lk = lambda nm: f'{kp}ld{par}:{nm}'
                    cs = slice(c * 128, (c + 1) * 128)
                    P.dma('sp', L['r'], SCR['R'][pr, cs], writes=[lk('r')])
                    P.dma('sp', L['ka'], SCR['KA'][pr, cs], writes=[lk('ka')])
                    P.dma('sp', L['kd'], SCR[f'KD{d}'][pr, cs], writes=[lk('kd')])
                    P.dma('sp', L['b'], SCR[f'B{d}'][pr, cs], writes=[lk('b')])
                    P.dma('sp', L['lw'], SCR[f'LW{d}'][pr, cs], writes=[lk('lw')])
                    for h in range(2):
                        P.dma('sp', L[f'Vx{h}'][:, h * 64:(h + 1) * 64], Vt_s[cs, p * 128 + h * 64:p * 128 + (h + 1) * 64],
                              writes=[lk(f'Vx{h}')])
                    yield
                    cum = t['cum']
                    P.op('dve', lambda e, cum=cum, L=L: e.tensor_tensor_scan(cum, ones32, L['lw'], 0.0, ALU.mult, ALU.add),
                         reads=[lk('lw'), 'ones32'], writes=[kp + 'cum'])
                    if d == 0:
                        cu = cum
                        cuk = kp + 'cum'
                    else:
                        cu = t['cu2']
                        cuk = kp + 'cu2'
                        tt('dve', cu, L['lw'], cum, ALU.subtract, [lk('lw'), kp + 'cum'], [cuk])
                        tsc(cu, cu, cum[:, 127:128], ALU.add, [cuk, kp + 'cum'], [cuk])
                    act(t['E1'], cu, AF.Exp, [cuk], [kp + 'E1'], scale=-1.0)
                    act(t['E2'], cu, AF.Exp, [cuk], [kp + 'E2'], scale=1.0)
                    tt('pool', t['E3'], cu, L['lw'], ALU.subtract, [cuk, lk('lw')], [kp + 'E3'])
                    act(t['E3'], t['E3'], AF.Exp, [kp + 'E3'], [kp + 'E3'], scale=-1.0)
                    gam = t['E1'][:, 127:128] if d == 0 else t['E1'][:, 0:1]
                    KR = t['KR']
                    tt('pool', KR[:, 128:256], L['r'], t['E1'], ALU.mult, [lk('r'), kp + 'E1'], [kp + 'KRr'])
                    tt('pool', KR[:, 0:128], L['ka'], t['E3'], ALU.mult, [lk('ka'), kp + 'E3'], [kp + 'KRk'])
                    for h in range(2):
                        hs = slice(h * 64, (h + 1) * 64)
                        tt('pool', t[f'Kx{h}'][hs, :], L['kd'][hs, :], t['E2'][hs, :], ALU.mult, [lk('kd'), kp + 'E2'], [kp + f'Kx{h}'])
                        stt(t[f'Bx{h}'][hs, :], L['b'][hs, :], -1.0, t['E2'][hs, :], ALU.mult, ALU.mult, [lk('b'), kp + 'E2'], [kp + f'Bx{h}'])
                    stt(t['Kp'], L['kd'], gam, t['E2'], ALU.mult, ALU.mult, [lk('kd'), kp + 'E1', kp + 'E2'], [kp + 'Kp'])
                    tsc(t['gn'][:, 0:1], gam, -1.0, ALU.mult, [kp + 'E1'], [kp + 'gn'])
                    stt(t['Bp'], L['b'], t['gn'][:, 0:1], t['E2'], ALU.mult, ALU.mult, [lk('b'), kp + 'gn', kp + 'E2'], [kp + 'Bp'])
                    yield
                    AT = [t['AT0'], t['AT1']]
                    Nn = [t['N0'], t['N1']]
                    for h in range(2):
                        mm(bB[:, h * 128:(h + 1) * 128], KR[:, 0:128], t[f'Bx{h}'], True, True, [kp + f'Bx{h}', kp + 'KRk'], [f'ps:{bY}'])
                    for h in range(2):
                        mm(bA[h][:, 0:256], t[f'Kx{h}'], KR, True, True, [kp + f'Kx{h}', kp + 'KRr', kp + 'KRk'], [bAk[h]])
                        mm(bA[h][:, 256:512], t[f'Bx{h}'], KR, True, True, [kp + f'Bx{h}', kp + 'KRr', kp + 'KRk'], [bAk[h]])
                        tt('dve', AT[h], bA[h], maskA[d], ALU.mult, [bAk[h], f'maskA:{d}'], [kp + f'AT{h}'])
                    for h in range(2):
                        tt('dve', Nn[h], bB[:, h * 128:(h + 1) * 128], maskN[d], ALU.mult, [f'ps:{bY}', f'maskN:{d}'], [kp + f'N{h}'])
                    yield
                    wps = bB[:, 256:384]
                    import os as _os
                    _rw = _os.environ.get('RW_W', '')
                    if _rw == '1':
                        mm(wps, KR[:, 0:128], Hx, True, True, [kp + 'KRk', kp + 'Hx'], [f'ps:{bY}'])
                    elif _rw == '2':
                        mm(wps, AT[0][:, 0:128], L['Vx0'], True, True, [kp + 'AT0', lk('Vx0')], [f'ps:{bY}'])
                    elif _rw == '3':
                        mm(wps, AT[0][:, 0:128], L['r'], True, True, [kp + 'AT0', lk('r')], [f'ps:{bY}'])
                    else:
                        mm(wps, KR[:, 0:128], Hx, True, False, [kp + 'KRk', kp + 'Hx'], [f'ps:{bY}'])
                        for h in range(2):
                            mm(wps, AT[h][:, 0:128], L[f'Vx{h}'], False, h == 1, [kp + f'AT{h}', lk(f'Vx{h}')], [f'ps:{bY}'])
                    Wt = [t['W0'], t['W1']]
                    act(Wt[0], wps, AF.Copy, [f'ps:{bY}'], [kp + 'W0'])
                    yield
                    Md = [t['Md0'], t['Md1']]
                    Nd = [t['Nd0'], t['Nd1']]
                    Mo = [t['Mo0'], t['Mo1']]
                    Pa = [t['Pa0'], t['Pa1']]
                    Pb = [t['Pb0'], t['Pb1']]
                    for h in range(2):
                        tt('pool', Md[h], AT[h][:, 256:384], bd16[:, 0:128], ALU.mult, [kp + f'AT{h}', 'bd16'], [kp + f'Md{h}'])
                        tt('pool', Mo[h], AT[h][:, 256:384], bd16[:, 128:256], ALU.mult, [kp + f'AT{h}', 'bd16'], [kp + f'Mo{h}'])
                        tt('pool', Nd[h], Nn[h], bd16[:, 0:128], ALU.mult, [kp + f'N{h}', 'bd16'], [kp + f'Nd{h}'])
                        tt('pool', Pa[h], Md[h], ident32, ALU.add, [kp + f'Md{h}', 'ident32'], [kp + f'Pa{h}'])
                    yield
                    bCk = f'ps:{bX}'
                    bBk = f'ps:{bY}'
                    MN2 = [t['MN'][h][0] for h in range(2)]
                    MN4 = [t['MN'][h][1] for h in range(2)]
                    N8 = [t['MN'][h][2] for h in range(2)]
                    for h in range(2):
                        cps = bC[:, h * 256:(h + 1) * 256]
                        mm(cps[:, 0:128], Nd[h], Md[h], True, True, [kp + f'Md{h}', kp + f'Nd{h}'], [bCk])
                        mm(cps[:, 128:256], Md[h], Nd[h], True, True, [kp + f'Md{h}', kp + f'Nd{h}'], [bCk])
                    for h in range(2):
                        act(MN2[h], bC[:, h * 256:(h + 1) * 256], AF.Copy, [bCk], [kp + f'MN2{h}'])
                    yield
                    for h in range(2):
                        cps = bC[:, h * 256:(h + 1) * 256]
                        mm(cps[:, 0:128], MN2[h][:, 128:256], MN2[h][:, 0:128], True, True, [kp + f'MN2{h}'], [bCk])
                        mm(cps[:, 128:256], MN2[h][:, 0:128], MN2[h][:, 128:256], True, True, [kp + f'MN2{h}'], [bCk])
                    for h in range(2):
                        act(MN4[h], bC[:, h * 256:(h + 1) * 256], AF.Copy, [bCk], [kp + f'MN4{h}'])
                    yield
                    for h in range(2):
                        mm(bC[:, h * 128:(h + 1) * 128], MN4[h][:, 0:128], MN4[h][:, 128:256], True, True, [kp + f'MN4{h}'], [bCk])
                    for h in range(2):
                        act(N8[h], bC[:, h * 128:(h + 1) * 128], AF.Copy, [bCk], [kp + f'N8{h}'])
                    yield
                    Pc, Pn, pck, pnk = Pa, Pb, 'Pa', 'Pb'
                    for lhs, lk_ in ((lambda h: MN2[h][:, 128:256], 'MN2'), (lambda h: MN4[h][:, 128:256], 'MN4'), (lambda h: N8[h], 'N8')):
                        for h in range(2):
                            mm(bC[:, h * 128:(h + 1) * 128], lhs(h), Pc[h], True, True, [kp + f'{lk_}{h}', kp + f'{pck}{h}'], [bCk])
                        for h in range(2):
                            tt('dve', Pn[h], Pc[h], bC[:, h * 128:(h + 1) * 128], ALU.add, [kp + f'{pck}{h}', bCk], [kp + f'{pnk}{h}'])
                        Pc, Pn, pck, pnk = Pn, Pc, pnk, pck
                        yield
                    TdT, tdk = Pc, pck
                    Wm = Wt[0]
                    Ut = Wt[1]
                    Uk = kp + 'W1'
                    Xt = t['X']
                    ups = bC[:, 0:128]
                    zps = bB[:, 256:384]
                    for h in range(2):
                        hc_ = slice(h * 64, (h + 1) * 64)
                        mm(ups[:, hc_], TdT[h], Wm[:, hc_], True, True, [kp + f'{tdk}{h}', kp + 'W0'], [bCk])
                    act(Ut, ups, AF.Copy, [bCk], [Uk])
                    yield
                    for it in range(7):
                        for h in range(2):
                            hc_ = slice(h * 64, (h + 1) * 64)
                            mm(zps[:, hc_], Mo[h], Ut[:, hc_], True, True, [kp + f'Mo{h}', Uk], [bBk])
                        tt('dve', Xt, Wm, zps, ALU.add, [kp + 'W0', bBk], [kp + 'X'])
                        for h in range(2):
                            hc_ = slice(h * 64, (h + 1) * 64)
                            mm(ups[:, hc_], TdT[h], Xt[:, hc_], True, True, [kp + f'{tdk}{h}', kp + 'X'], [bCk])
                        act(Ut, ups, AF.Copy, [bCk], [Uk])
                        yield
                    for h in range(2):
                        P.op('pool', lambda e, h=h, Ut=Ut: e.tensor_copy(t[f'Ux{h}'][:, h * 64:(h + 1) * 64], Ut[:, h * 64:(h + 1) * 64]),
                             reads=[Uk], writes=[kp + f'Ux{h}'])
                    yps = bB[:, 384:512]
                    mm(yps, Hx, KR[:, 128:256], True, False, [kp + 'Hx', kp + 'KRr'], [f'ps:{bY}'])
                    for h in range(2):
                        mm(yps, L[f'Vx{h}'], AT[h][:, 128:256], False, False, [lk(f'Vx{h}'), kp + f'AT{h}'], [f'ps:{bY}'])
                    for h in range(2):
                        mm(yps, t[f'Ux{h}'], AT[h][:, 384:512], False, h == 1, [kp + f'Ux{h}', kp + f'AT{h}'], [f'ps:{bY}'])
                    act(t['yt'], yps, AF.Copy, [f'ps:{bY}'], [kp + 'yt'])
                    P.dma('sp', yscr[pr, cs], t['yt'], reads=[kp + 'yt'], writes=[f'Y{d}:{p}:{c}'])
                    yield
                    mm(bA[0][:, 0:128], t['Kp'], ident32, True, True, [kp + 'Kp', 'ident32'], [bAk[0]])
                    mm(bA[0][:, 128:256], t['Bp'], ident32, True, True, [kp + 'Bp', 'ident32'], [bAk[0]])
                    act(t['KTt'], bA[0][:, 0:128], AF.Copy, [bAk[0]], [kp + 'KTt'])
                    act(t['BTt'], bA[0][:, 128:256], AF.Copy, [bAk[0]], [kp + 'BTt'])
                    dps = bA[1][:, 0:128]
                    mm(dps, t['KTt'], L['Vx0'], True, False, [kp + 'KTt', lk('Vx0')], [bAk[1]])
                    mm(dps, t['KTt'], L['Vx1'], False, False, [kp + 'KTt', lk('Vx1')], [bAk[1]])
                    mm(dps, t['BTt'], Ut, False, True, [kp + 'BTt', Uk], [bAk[1]])
                    tt('dve', t['dl'], dps, bones32, ALU.mult, [bAk[1], 'bones'], [kp + 'dl'])
                    stt(Hx, Hx, gam, t['dl'], ALU.mult, ALU.add, [kp + 'Hx', kp + 'E1', kp + 'dl'], [kp + 'Hx'])
                    if ci % 2 == 1:
                        seq = c // 2
                        P.dma('sp', ostate_d[j, seq, d, p], Hx, reads=[kp + 'Hx'], writes=[K('ost')], final=True)
                        if ci < 15:
                            tsc(Hx, Hx, keepf[:, 0:1], ALU.mult, [kp + 'Hx', 'keepf'], [kp + 'Hx'])
                    yield

            PT = [A.f32(256) for _ in range(6)]
            def limited(g):
                n = 0
                for _ in g:
                    n += 1
                    if n >= rstop:
                        return
                    yield

            for p in range(8):
                if rstop < 999:
                    if p > 0:
                        break
                    import os as _os
                    if _os.environ.get('RW_SINGLE') == '0':
                        run_interleaved([limited(chain(p, 0, 0))])
                    elif _os.environ.get('RW_SINGLE') == '1':
                        run_interleaved([limited(chain(p, 1, 1))])
                    else:
                        run_interleaved([limited(chain(p, 0, 0)), limited(chain(p, 1, 1))])
                    continue
                if p % 2 == 0:
                    run_interleaved([chain(p, 0, 0), chain(p, 1, 1), chain(p + 1, 0, 2), chain(p + 1, 1, 3)])
                pr = slice(p * 128, (p + 1) * 128)
                for tb in range(8):
                    blk = slice(tb * 256, (tb + 1) * 256)
                    yf, yb, dd, s2, bn, gg_ = PT
                    ykeys = [f'Y{d}:{p}:{c}' for d in range(2) for c in range(tb * 2, tb * 2 + 2)]
                    P.dma('sp', yf, SCR['YF'][pr, blk], reads=ykeys, writes=['pt:yf'])
                    P.dma('sp', yb, SCR['YB'][pr, blk], reads=ykeys, writes=['pt:yb'])
                    P.dma('sp', bn, SCR['BON'][pr, blk], writes=['pt:bn'])
                    P.dma('sp', gg_, SCR['G'][pr, blk], writes=['pt:gg'])
                    tt('pool', yf, yf, yb, ALU.add, ['pt:yf', 'pt:yb'], ['pt:yf'])
                    mm(ps[0][:, 0:256], bo64, yf, True, True, ['bo64', 'pt:yf'], ['ps:0'])
                    tt('dve', dd, yf, ps[0][:, 0:256], ALU.subtract, ['pt:yf', 'ps:0'], ['pt:dd'])
                    tt('pool', s2, dd, dd, ALU.mult, ['pt:dd'], ['pt:s2'])
                    mm(ps[1][:, 0:256], bo64, s2, True, True, ['bo64', 'pt:s2'], ['ps:1'])
                    act(s2, ps[1][:, 0:256], AF.Sqrt, ['ps:1', 'lneps'], ['pt:s2'], bias=lneps[:, 0:1], scale=1.0)
                    P.op('dve', lambda e, s2=s2: e.reciprocal(s2, s2), reads=['pt:s2'], writes=['pt:s2'])
                    tt('pool', dd, dd, s2, ALU.mult, ['pt:dd', 'pt:s2'], ['pt:dd'])
                    P.op('dve', lambda e, dd=dd, p=p: e.tensor_scalar(dd, dd, vecs[:, 7, p:p + 1], vecs[:, 8, p:p + 1], ALU.mult, ALU.add),
                         reads=['pt:dd', 'vecs'], writes=['pt:dd'])
                    tt('pool', dd, dd, bn, ALU.add, ['pt:dd', 'pt:bn'], ['pt:dd'])
                    tt('pool', dd, dd, gg_, ALU.mult, ['pt:dd', 'pt:gg'], ['pt:dd'])
                    P.dma('sp', SCR['YG'][pr, blk], dd, reads=['pt:dd'], writes=[f'YG:{p}:{tb}'])
            A.release(m_s)
            P.barrier()
            if rstage < 4:
                A.release(m_layer)
                P.barrier()
                return

            m_o = A.mark()
            Wo = A.bf16(8 * 1024).rearrange("p (k n) -> p k n", k=8)
            ygT = A.bf16(8 * 512).rearrange("p (c t) -> p c t", c=8)
            mT = A.f32(8 * 512).rearrange("p (c t) -> p c t", c=8)
            rstd = A.f32(512)
            ntmp = A.f32(512)
            sqb = A.bf16(512)
            P.dma('pool', Wo, wor_d[j].rearrange("(k p) n -> p k n", p=128), writes=['Wo'])
            for tb in range(4):
                blk = slice(tb * 512, (tb + 1) * 512)
                for c in range(8):
                    P.dma('pool', ygT[:, c, :], SCR['YG'][c * 128:(c + 1) * 128, blk], writes=[f'ygT:{c}'])
                for c in range(8):
                    b = c % 2
                    for k in range(8):
                        mm(ps[b], Wo[:, k, c * 128:(c + 1) * 128], ygT[:, k, :], k == 0, k == 7, ['Wo', f'ygT:{k}'], [f'ps:{b}'])
                    act(mT[:, c, :], ps[b], AF.Copy, [f'ps:{b}'], [f'mT:{c}'])
                    act(sqb, ps[b], AF.Square, [f'ps:{b}'], ['sqb'])
                    mm(ps[2], ones_bf, sqb, c == 0, c == 7, ['sqb', 'ones'], ['ps:2'])
                residual_update(l, 8, tb * 512, 512, mT, 'mT', rstd, ntmp)
            A.release(m_layer)
            P.barrier()

        for l in range(n_layers):
            if l % 2 == 0 and stage >= 2:
                attn_layer(l)
            elif l % 2 == 1 and do_rwkv:
                rwkv_layer(l)
            if stage >= 4:
                mlp(l)

        for c in range(8):
            P.dma('sp', yT_d[c * 128:(c + 1) * 128, :], xT[:, c, :], reads=[f'x:{c}:{tb}' for tb in range(4)], writes=[K('yT')], final=True)
        P.emit()
    return nc, P


_PROG = {}


def _rope_tables(rows=32, grid_w=64):
    row = np.repeat(np.arange(rows), grid_w).astype(np.float32)
    col = np.tile(np.arange(grid_w), rows).astype(np.float32)
    inv = (1.0 / (np.float32(10000.0) ** (np.arange(0, 32, 2, dtype=np.float32) / np.float32(32)))).astype(np.float32)
    ar = row[:, None] * inv[None, :]
    ac = col[:, None] * inv[None, :]
    ang = np.concatenate([ar, ar, ac, ac], axis=-1).astype(np.float32)
    return np.cos(ang).astype(np.float32), np.sin(ang).astype(np.float32)


def _tok_major_table(t):
    return np.ascontiguousarray(t.reshape(16, 128, 64).transpose(1, 0, 2).reshape(128, 1024))


def kernel(x_prompt, x_sample, c, cache_k_gqa, cache_v_gqa, cache_k_diff, cache_v_diff, state_rwkv,
           c_ctx, w_ada, b_ada, norm_gains, attn_w_in, attn_w_out, attn_qk_gain, diff_lambda, diff_subln,
           rwkv_mu, rwkv_w_rkv, rwkv_w_o, rwkv_w0, rwkv_w1, rwkv_w2, rwkv_a0, rwkv_a1, rwkv_a2,
           rwkv_g1, rwkv_g2, rwkv_kvec, rwkv_lnx, mlp_w1, mlp_w2):
    f = lambda a: np.ascontiguousarray(np.asarray(a, dtype=np.float32))
    x_prompt, x_sample, c = f(x_prompt), f(x_sample), f(c)
    if 'nc' not in _PROG:
        _PROG['nc'] = build_program()[0]
    nc = _PROG['nc']

    qa_perm = np.concatenate([np.r_[h * 64:(h + 1) * 64, (4 + h) * 64:(5 + h) * 64] for h in range(4)])
    cols = np.concatenate([qa_perm, np.arange(768, 1280), np.arange(512, 640), np.arange(1280, 1792),
                           np.arange(640, 768), np.arange(1792, 2304)])
    w_in = f(np.asarray(attn_w_in)[:, :, cols])
    rows = np.concatenate([qa_perm, np.arange(512, 1024)])
    w_out = f(np.asarray(attn_w_out)[:, rows, :])
    gqk = f(np.concatenate([np.broadcast_to(np.asarray(attn_qk_gain)[:, None, 0, :], (2, 128, 64)),
                            np.broadcast_to(np.asarray(attn_qk_gain)[:, None, 1, :], (2, 128, 64))], axis=2))
    lamv = f(np.broadcast_to(np.asarray(diff_lambda).reshape(2, 1, 256), (2, 128, 256)))
    subln = f(np.broadcast_to(np.asarray(diff_subln).reshape(2, 1, 128), (2, 128, 128)))
    bada = f(np.asarray(b_ada).reshape(4, 48, 128).transpose(0, 2, 1))
    gains = f(np.asarray(norm_gains).reshape(4, 4, 8, 128).transpose(0, 3, 1, 2).reshape(4, 128, 32))
    ident = np.eye(128, dtype=np.float32)
    cos, sin = _rope_tables()
    sinS = sin.reshape(2048, 2, 2, 16).copy()
    sinS[:, :, 0, :] *= -1.0
    sinS = sinS.reshape(2048, 64)
    cos_s, sin_s = _tok_major_table(cos), _tok_major_table(sinS)
    cos_p, sin_p = np.ones((128, 1024), np.float32), np.zeros((128, 1024), np.float32)
    mask_s = np.zeros((128, 8, 20), np.float32)
    mask_p = np.full((128, 8, 20), -25.0, np.float32)
    for i in range(8):
        mask_p[:, i, 4 + 2 * i: 6 + 2 * i] = 0.0
    w_ada_, w1_, w2_ = f(w_ada), f(mlp_w1), f(mlp_w2)

    W1, A1 = np.asarray(rwkv_w1), np.asarray(rwkv_a1)
    w1c = f(np.concatenate([W1[:, 0], W1[:, 1]], axis=-1))
    a1c = f(np.concatenate([A1[:, 0], A1[:, 1]], axis=-1))
    w2x = np.zeros((2, 2, 128, 1024), np.float32)
    a2x = np.zeros((2, 2, 128, 1024), np.float32)
    for d_ in range(2):
        w2x[:, d_, d_ * 64:(d_ + 1) * 64, :] = np.asarray(rwkv_w2)[:, d_]
        a2x[:, d_, d_ * 64:(d_ + 1) * 64, :] = np.asarray(rwkv_a2)[:, d_]
    rmu = f(np.asarray(rwkv_mu).reshape(2, 6, 8, 128).transpose(0, 3, 1, 2).reshape(2, 128, 48))
    vec9 = np.stack([np.asarray(rwkv_w0)[:, 0], np.asarray(rwkv_w0)[:, 1], np.asarray(rwkv_a0)[:, 0], np.asarray(rwkv_a0)[:, 1],
                     np.asarray(rwkv_kvec)[:, 0], np.asarray(rwkv_kvec)[:, 1], np.asarray(rwkv_kvec)[:, 2],
                     np.asarray(rwkv_lnx)[:, 0], np.asarray(rwkv_lnx)[:, 1]], axis=1)
    rvecs = f(vec9.reshape(2, 9, 8, 128).transpose(0, 3, 1, 2).reshape(2, 128, 72))
    bones = np.zeros((128, 128), np.float32)
    bones[0:64, 0:64] = 1.0
    bones[64:128, 64:128] = 1.0
    ii = np.arange(128)
    lt = (ii[:, None] < ii[None, :]).astype(np.float32)
    le = (ii[:, None] <= ii[None, :]).astype(np.float32)
    rmask = np.zeros((2, 128, 640), np.float32)
    rmask[0] = np.concatenate([lt, le, lt, le, lt.T], axis=1)
    rmask[1] = np.concatenate([lt.T, le.T, lt.T, le.T, lt], axis=1)
    b16 = (ii[:, None] // 16 == ii[None, :] // 16).astype(np.float32)
    bd16 = f(np.concatenate([b16, 1.0 - b16], axis=1))
    tpos = np.arange(2048)
    sh_s = np.stack([np.where(tpos == 0, 0.0, 0.5), np.where(tpos == 2047, 0.0, 0.5)]).astype(np.float32)
    sh_p = np.stack([np.where(tpos % 256 == 0, 0.0, 0.5), np.where(tpos % 256 == 255, 0.0, 0.5)]).astype(np.float32)
    sh_s = f(np.broadcast_to(sh_s[:, None, :], (2, 128, 2048)))
    sh_p = f(np.broadcast_to(sh_p[:, None, :], (2, 128, 2048)))
    wrkv_, wor_, g1_, g2_ = f(rwkv_w_rkv), f(rwkv_w_o), f(rwkv_g1), f(rwkv_g2)
    st_all = np.asarray(state_rwkv, dtype=np.float32)

    in_maps = []
    for core in range(8):
        if core < 4:
            b = core
            x = x_sample[b]
            cond = c[b]
            kT = np.concatenate([np.asarray(cache_k_gqa)[b].reshape(2, 512, 128).transpose(0, 2, 1),
                                 np.asarray(cache_k_diff)[b].reshape(2, 512, 512).transpose(0, 2, 1)], axis=1)
            v = np.concatenate([np.asarray(cache_v_gqa)[b].reshape(2, 512, 128),
                                np.asarray(cache_v_diff)[b].reshape(2, 512, 512)], axis=2)
            cs, sn, mk = cos_s, sin_s, mask_s
            st0 = np.zeros((2, 2, 8, 128, 128), np.float32)
            for p_ in range(8):
                for h_ in range(2):
                    st0[:, :, p_, h_ * 64:(h_ + 1) * 64, h_ * 64:(h_ + 1) * 64] = st_all[b][:, :, 2 * p_ + h_].transpose(0, 1, 3, 2)
            keep, shm = np.ones((128, 1), np.float32), sh_s
        else:
            st0 = np.zeros((2, 2, 8, 128, 128), np.float32)
            keep, shm = np.zeros((128, 1), np.float32), sh_p
            p0 = (core - 4) * 8
            x = x_prompt[p0:p0 + 8].reshape(2048, 1024)
            cond = np.asarray(c_ctx)
            kT = np.zeros((2, 640, 512), np.float32)
            v = np.zeros((2, 512, 640), np.float32)
            cs, sn, mk = cos_p, sin_p, mask_p
        in_maps.append({
            "xT": f(x.T), "cond": f(np.asarray(cond).reshape(8, 128).T), "b_ada": bada, "gains": gains,
            "w_ada": w_ada_, "w_in": w_in, "w_out": w_out, "gqk": gqk, "lamv": lamv, "subln": subln,
            "kTc": f(kT), "vc": f(v), "maskb": f(mk.reshape(128, 160)), "cos": f(cs), "sin": f(sn),
            "mlp_w1": w1_, "mlp_w2": w2_, "ident": ident,
            "wrkv": wrkv_, "wo_r": wor_, "w1c": w1c, "a1c": a1c, "g1": g1_, "w2x": w2x, "a2x": a2x, "g2": g2_,
            "rmu": rmu, "rvecs": rvecs, "st0": st0, "keepf": keep, "shiftm": shm, "bones": bones, "rmask": rmask, "bd16": bd16,
        })
    if _PROG.get('dbg_cores'):
        ncd = _PROG['dbg_cores']
        return run_bass_kernel_spmd(nc, [in_maps[i] for i in ncd], core_ids=list(range(len(ncd)))).results
    res = run_bass_kernel_spmd(nc, in_maps, core_ids=list(range(8))).results
    _PROG['last'] = res

    y_sample = np.stack([res[b]["yT"].T for b in range(4)], axis=0).astype(np.float32)
    y_prompt = np.concatenate([res[4 + g]["yT"].T.reshape(8, 256, 1024) for g in range(4)], axis=0).astype(np.float32)
    okv = np.concatenate([res[4 + g]["okv"].reshape(2, 8, 256, 1280).transpose(1, 0, 2, 3) for g in range(4)], axis=0)
    new_k_gqa = np.ascontiguousarray(okv[..., 0:128].reshape(32, 2, 256, 2, 64)).astype(np.float32)
    new_k_diff = np.ascontiguousarray(okv[..., 128:640].reshape(32, 2, 256, 4, 2, 64)).astype(np.float32)
    new_v_gqa = np.ascontiguousarray(okv[..., 640:768].reshape(32, 2, 256, 2, 64)).astype(np.float32)
    new_v_diff = np.ascontiguousarray(okv[..., 768:1280].reshape(32, 2, 256, 4, 128)).astype(np.float32)
    new_state = np.zeros((32, 2, 2, 16, 64, 64), np.float32)
    for g in range(4):
        os_ = res[4 + g]["ostate"]
        for p_ in range(8):
            for h_ in range(2):
                blkv = os_[:, :, :, p_, h_ * 64:(h_ + 1) * 64, h_ * 64:(h_ + 1) * 64]
                new_state[8 * g:8 * g + 8, :, :, 2 * p_ + h_] = blkv.transpose(1, 0, 2, 4, 3)
    return (y_prompt, y_sample, new_k_gqa, new_v_gqa, new_k_diff, new_v_diff, new_state)
```

```python
from contextlib import ExitStack
import numpy as np
import concourse.bass as bass
import concourse.mybir as mybir
from concourse.bass_utils import run_bass_kernel_spmd

F32 = mybir.dt.float32
BF16 = mybir.dt.bfloat16
AF = mybir.ActivationFunctionType
ALU = mybir.AluOpType
AX = mybir.AxisListType

ENG_NAMES = ['pe', 'act', 'dve', 'pool', 'sp']


class _Op:
    __slots__ = ('eng', 'fn', 'deps', 'flag', 'num', 'sk', 'inc', 'is_dma', 'prev_same_sem')

    def __init__(self, eng, fn, deps, is_dma):
        self.eng = eng
        self.fn = fn
        self.deps = deps
        self.flag = is_dma
        self.num = 0
        self.sk = None
        self.inc = 16 if is_dma else 1
        self.is_dma = is_dma
        self.prev_same_sem = None


class Prog:
    def __init__(self, nc, ndma=24):
        self.nc = nc
        self.all = []
        self.last_w = {}
        self.readers = {}
        self.ndma = ndma
        self.dma_rr = {e: 0 for e in ENG_NAMES}
        self.dma_last = {}
        self.finals = []
        self.bar = None
        self.bar_tile = None

    def barrier(self):
        deps = {}
        for o in self.last_w.values():
            deps[id(o)] = o
        for r in self.readers.values():
            for v in r.values():
                for o in (v if isinstance(v, list) else [v]):
                    deps[id(o)] = o
        if self.bar is not None:
            deps[id(self.bar)] = self.bar
        bt = self.bar_tile
        o = _Op('dve', (lambda e: e.memset(bt, 0.0)), list(deps.values()), False)
        self.all.append(o)
        self.last_w = {}
        self.readers = {}
        self.bar = o

    def _collect(self, eng, reads, writes):
        deps = []
        for k in reads:
            o = self.last_w.get(k)
            if o is not None:
                deps.append(o)
        for k in writes:
            o = self.last_w.get(k)
            if o is not None:
                deps.append(o)
            r = self.readers.get(k)
            if r:
                for v in r.values():
                    if isinstance(v, list):
                        deps.extend(v)
                    else:
                        deps.append(v)
        if self.bar is not None:
            deps.append(self.bar)
        out = []
        seen = set()
        for d in deps:
            if id(d) in seen:
                continue
            seen.add(id(d))
            if d.eng == 'pe' and eng == 'pe' and not d.is_dma:
                continue
            out.append(d)
        return out

    def _commit(self, op, reads, writes):
        for k in writes:
            self.last_w[k] = op
            self.readers[k] = {}
        for k in reads:
            r = self.readers.setdefault(k, {})
            if op.is_dma:
                r.setdefault('dma', []).append(op)
            else:
                r[op.eng] = op

    def op(self, eng, fn, reads=(), writes=()):
        deps = self._collect(eng, reads, writes)
        o = _Op(eng, fn, deps, False)
        self.all.append(o)
        self._commit(o, reads, writes)
        return o

    def dma(self, q, out, in_, reads=(), writes=(), final=False):
        deps = self._collect(q, reads, writes)
        o = _Op(q, (lambda e, out=out, in_=in_: e.dma_start(out=out, in_=in_)), deps, True)
        i = self.dma_rr[q]
        self.dma_rr[q] += 1
        o.sk = ('d', q, i % self.ndma)
        o.prev_same_sem = self.dma_last.get(o.sk)
        self.dma_last[o.sk] = o
        self.all.append(o)
        self._commit(o, reads, writes)
        if final:
            self.finals.append(o)
        return o

    def emit(self):
        nc = self.nc
        for o in self.all:
            for d in o.deps:
                d.flag = True
        cnt = {}
        for o in self.all:
            if o.is_dma:
                k = cnt.get(o.sk, 0) + 1
                cnt[o.sk] = k
                o.num = 16 * k
            elif o.flag:
                o.sk = ('c', o.eng)
                k = cnt.get(o.sk, 0) + 1
                cnt[o.sk] = k
                o.num = k
        waited = {e: {} for e in ENG_NAMES}
        streams = {e: [] for e in ENG_NAMES}
        for o in self.all:
            need = {}
            for d in o.deps:
                if waited[o.eng].get(d.sk, 0) >= d.num:
                    continue
                need[d.sk] = max(need.get(d.sk, 0), d.num)
            if o.is_dma and o.prev_same_sem is not None:
                p = o.prev_same_sem
                if waited[o.eng].get(p.sk, 0) < p.num:
                    need[p.sk] = max(need.get(p.sk, 0), p.num)
            for sk, v in need.items():
                waited[o.eng][sk] = v
            streams[o.eng].append((list(need.items()), o))
        fin = {}
        for o in self.finals:
            fin[o.sk] = max(fin.get(o.sk, 0), o.num)
        self.n_ops = {e: len(streams[e]) for e in ENG_NAMES}
        self.streams = streams
        self.fin = fin
        with ExitStack() as st:
            sems = {}
            for sk in cnt:
                sems[sk] = st.enter_context(nc.semaphore("s_" + "_".join(str(x) for x in sk)))
            block = st.enter_context(nc.Block())

            def mk(name):
                def f(eng):
                    for waits, o in streams[name]:
                        for (wk, v) in waits:
                            eng.wait_ge(sems[wk], v)
                        ins = o.fn(eng)
                        if o.flag:
                            ins.then_inc(sems[o.sk], o.inc)
                    if name == 'sp':
                        for sk, v in fin.items():
                            eng.wait_ge(sems[sk], v)
                return f

            block.tensor(mk('pe'))
            block.scalar(mk('act'))
            block.vector(mk('dve'))
            block.gpsimd(mk('pool'))
            block.sync(mk('sp'))


D = 1024
T = 2048
NKT = 20
EPS = 1e-6
VW = 652
LAM_INIT = {0: 0.8 - 0.6 * float(np.exp(-0.3 * 0)), 2: 0.8 - 0.6 * float(np.exp(-0.3 * 2))}


class Arena:
    def __init__(self, ap, nwords):
        self.ap = ap
        self.n = nwords
        self.top = 0

    def mark(self):
        return self.top

    def release(self, m):
        self.top = m

    def f32(self, n):
        a = self.top
        self.top += (n + 7) // 8 * 8
        assert self.top <= self.n, ("arena overflow", self.top, self.n)
        return self.ap[:, a:a + n]

    def bf16(self, n):
        w = (n + 1) // 2
        a = self.top
        self.top += (w + 7) // 8 * 8
        assert self.top <= self.n, ("arena overflow", self.top, self.n)
        return self.ap[:, a:a + w].bitcast(BF16)[:, 0:n]


def build_program(n_layers=4, do_rwkv=True, stage=9, rstage=9, rstop=999):
    nc = bass.Bass("TRN2", target_bir_lowering=False)

    def din(name, shape):
        return nc.dram_tensor(name, list(shape), F32, kind="ExternalInput").ap()

    def dout(name, shape):
        return nc.dram_tensor(name, list(shape), F32, kind="ExternalOutput").ap()

    xT_d = din("xT", [D, T])
    cond_d = din("cond", [128, 8])
    bada_d = din("b_ada", [4, 128, 48])
    gains_d = din("gains", [4, 128, 32])
    wada_d = din("w_ada", [4, D, 6 * D])
    win_d = din("w_in", [2, D, 2304])
    wout_d = din("w_out", [2, D, D])
    gqk_d = din("gqk", [2, 128, 128])
    lamv_d = din("lamv", [2, 128, 256])
    subln_d = din("subln", [2, 128, 128])
    kTc_d = din("kTc", [2, 640, 512])
    vc_d = din("vc", [2, 512, 640])
    maskb_d = din("maskb", [128, 160])
    cos_d = din("cos", [128, 16 * 64])
    sin_d = din("sin", [128, 16 * 64])
    w1_d = din("mlp_w1", [4, D, 4 * D])
    w2_d = din("mlp_w2", [4, 4 * D, D])
    ident_d = din("ident", [128, 128])

    wrkv_d = din("wrkv", [2, 3, D, D])
    wor_d = din("wo_r", [2, D, D])
    w1c_d = din("w1c", [2, D, 128])
    a1c_d = din("a1c", [2, D, 128])
    g1_d = din("g1", [2, D, 128])
    w2x_d = din("w2x", [2, 2, 128, D])
    a2x_d = din("a2x", [2, 2, 128, D])
    g2_d = din("g2", [2, 128, D])
    rmu_d = din("rmu", [2, 128, 48])
    rvecs_d = din("rvecs", [2, 128, 72])
    st0_d = din("st0", [2, 2, 8, 128, 128])
    keepf_d = din("keepf", [128, 1])
    shiftm_d = din("shiftm", [2, 128, T])
    bones_d = din("bones", [128, 128])
    rmask_d = din("rmask", [2, 128, 640])
    bd16_d = din("bd16", [128, 256])

    yT_d = dout("yT", [D, T])
    okv_d = dout("okv", [2, T, 1280])
    ostate_d = dout("ostate", [2, 8, 2, 8, 128, 128])

    def dscr(name, shape):
        return nc.dram_tensor(name, list(shape), F32, kind="Internal").ap()

    SCR = {n: dscr("scr_" + n, [D, T]) for n in ("R", "KA", "KD0", "KD1", "B0", "B1", "LW0", "LW1", "G", "BON", "YF", "YB", "YG")}
    Vt_s = dscr("scr_Vt", [T, D])

    P = Prog(nc)
    NW = 53000
    with ExitStack() as st:
        arena_t = st.enter_context(nc.sbuf_tensor("arena", [128, NW], F32))
        A = Arena(arena_t, NW)
        psbig = st.enter_context(nc.psum_tensor("psbig", [128, 4096], F32))
        ps = [psbig[:, i * 512:(i + 1) * 512] for i in range(8)]
        psb = [p.bitcast(BF16) for p in ps]

        uid = [0]

        def K(s):
            uid[0] += 1
            return f"{s}#{uid[0]}"

        xT = A.f32(8 * T).rearrange("p (c t) -> p c t", c=8)
        ones_bf = A.bf16(128)
        ident_bf = A.bf16(128)
        cs_box = {}
        maskb = A.f32(160).rearrange("p (i k) -> p i k", i=8)
        mods = A.f32(4 * 48).rearrange("p (l m) -> p l m", l=4)
        gains = A.f32(4 * 32).rearrange("p (l m) -> p l m", l=4)
        gsv = A.f32(4 * 32).rearrange("p (l m) -> p l m", l=4)
        condT = A.f32(8)
        silu_bf = A.bf16(8)
        bada = A.f32(4 * 48).rearrange("p (l m) -> p l m", l=4)
        epsb = A.f32(1)
        P.bar_tile = A.f32(1)

        P.op('dve', lambda e: e.memset(ones_bf, 1.0), writes=['ones'])
        P.op('dve', lambda e: e.memset(epsb, EPS), writes=['epsb'])
        P.dma('pool', ident_bf, ident_d, writes=['identbf'])
        P.dma('sp', maskb, maskb_d.rearrange("p (i k) -> p i k", i=8), writes=['maskb'])
        P.dma('sp', condT, cond_d, writes=['cond'])
        P.dma('sp', bada, bada_d.rearrange("l p m -> p l m"), writes=['bada'])
        P.dma('sp', gains, gains_d.rearrange("l p m -> p l m"), writes=['gains'])
        for c in range(8):
            P.dma('sp', xT[:, c, :], xT_d[c * 128:(c + 1) * 128, :], writes=[f'x:{c}:{tb}' for tb in range(4)])

        P.op('act', lambda e: e.activation(silu_bf, condT, AF.Silu), reads=['cond'], writes=['silu'])
        m0 = A.mark()
        wab = [A.bf16(8 * 512).rearrange("p (k n) -> p k n", k=8) for _ in range(2)]
        bi = 0
        for l in range(n_layers):
            for blk in range(12):
                buf = wab[bi % 2]
                key = f'wab:{bi % 2}'
                bi += 1
                P.dma('pool', buf, wada_d[l, :, blk * 512:(blk + 1) * 512].rearrange("(k p) n -> p k n", p=128),
                      writes=[key])
                for jj in range(4):
                    j = blk * 4 + jj
                    for k in range(8):
                        P.op('pe', lambda e, buf=buf, jj=jj, k=k, j=j: e.matmul(
                            ps[7][:, j:j + 1], buf[:, k, jj * 128:(jj + 1) * 128], silu_bf[:, k:k + 1],
                            start=(k == 0), stop=(k == 7)), reads=[key, 'silu'], writes=['ps:7'])
            P.op('dve', lambda e, l=l: e.tensor_tensor(mods[:, l, :], ps[7][:, 0:48], bada[:, l, :], ALU.add),
                 reads=['ps:7', 'bada'], writes=[f'mods:{l}'])
            P.op('dve', lambda e, l=l: e.scalar_tensor_tensor(gsv[:, l, 0:8], mods[:, l, 8:16], 1.0, gains[:, l, 0:8],
                                                              ALU.add, ALU.mult), reads=[f'mods:{l}', 'gains'], writes=[f'gsv:{l}:0'])
            P.op('dve', lambda e, l=l: e.tensor_tensor(gsv[:, l, 8:16], mods[:, l, 16:24], gains[:, l, 8:16], ALU.mult),
                 reads=[f'mods:{l}', 'gains'], writes=[f'gsv:{l}:1'])
            P.op('dve', lambda e, l=l: e.scalar_tensor_tensor(gsv[:, l, 16:24], mods[:, l, 32:40], 1.0, gains[:, l, 16:24],
                                                              ALU.add, ALU.mult), reads=[f'mods:{l}', 'gains'], writes=[f'gsv:{l}:2'])
            P.op('dve', lambda e, l=l: e.tensor_tensor(gsv[:, l, 24:32], mods[:, l, 40:48], gains[:, l, 24:32], ALU.mult),
                 reads=[f'mods:{l}', 'gains'], writes=[f'gsv:{l}:3'])
        A.release(m0)
        P.barrier()

        def rstd_from_ps(pst, n, dst, scale, keyr, keyw):
            P.op('act', lambda e: e.activation(dst, pst, AF.Sqrt, bias=epsb[:, 0:1], scale=scale),
                 reads=keyr + ['epsb'], writes=[keyw])
            P.op('dve', lambda e: e.reciprocal(dst, dst), reads=[keyw], writes=[keyw])

        def norm_block(l, which, t0, n, hT, hkey, sqb, rstd, tmp):
            gi, shi = (0, 0) if which == 1 else (16, 24)
            tb = t0 // 512
            for c in range(8):
                P.op('act', lambda e, c=c: e.activation(sqb[:, 0:n], xT[:, c, t0:t0 + n], AF.Square),
                     reads=[f'x:{c}:{tb}'], writes=['sqb'])
                P.op('pe', lambda e, c=c: e.matmul(ps[2][:, 0:n], ones_bf, sqb[:, 0:n], start=(c == 0), stop=(c == 7)),
                     reads=['sqb', 'ones'], writes=['ps:2'])
            rstd_from_ps(ps[2][:, 0:n], n, rstd[:, 0:n], 1.0 / D, ['ps:2'], 'rstd')
            for c in range(8):
                P.op('dve', lambda e, c=c: e.scalar_tensor_tensor(tmp[:, 0:n], xT[:, c, t0:t0 + n], gsv[:, l, gi + c:gi + c + 1],
                                                                  rstd[:, 0:n], ALU.mult, ALU.mult),
                     reads=[f'x:{c}:{tb}', 'rstd', f'gsv:{l}:{gi // 8}'], writes=['ntmp'])
                P.op('act', lambda e, c=c: e.activation(hT[:, c, 0:n], tmp[:, 0:n], AF.Identity,
                                                        bias=mods[:, l, shi + c:shi + c + 1], scale=1.0),
                     reads=['ntmp', f'mods:{l}'], writes=[f'{hkey}:{c}'])

        def residual_update(l, gidx, t0, n, mT, mkey, rstd, tmp):
            tb = t0 // 512
            rstd_from_ps(ps[2][:, 0:n], n, rstd[:, 0:n], 1.0 / D, ['ps:2'], 'rstd')
            for c in range(8):
                P.op('dve', lambda e, c=c: e.scalar_tensor_tensor(tmp[:, 0:n], mT[:, c, 0:n], gsv[:, l, gidx + c:gidx + c + 1],
                                                                  rstd[:, 0:n], ALU.mult, ALU.mult),
                     reads=[f'{mkey}:{c}', 'rstd', f'gsv:{l}:{gidx // 8}'], writes=['ntmp'])
                P.op('pool', lambda e, c=c: e.tensor_tensor(xT[:, c, t0:t0 + n], xT[:, c, t0:t0 + n], tmp[:, 0:n], ALU.add),
                     reads=['ntmp', f'x:{c}:{tb}'], writes=[f'x:{c}:{tb}'])

        def rope_tm(src, nh, st_idx, dst_bf, t1, t2, kin, kout):
            s5 = src.rearrange("p (h a b d) -> p h a b d", h=nh, a=2, b=2)
            t5 = t1.rearrange("p (h a b d) -> p h a b d", h=nh, a=2, b=2)
            sin_t, cos_t = cs_box['sin'], cs_box['cos']
            sn = sin_t[:, st_idx, :].rearrange("p (a b d) -> p a b d", a=2, b=2)
            P.op('dve', lambda e: e.tensor_tensor(t5[:, :, :, 0, :], s5[:, :, :, 1, :],
                                                  sn[:, :, 0, :].unsqueeze(1).broadcast_to([128, nh, 2, 16]), ALU.mult),
                 reads=kin + ['sin'], writes=[kout + 'a'])
            P.op('pool', lambda e: e.tensor_tensor(t5[:, :, :, 1, :], s5[:, :, :, 0, :],
                                                   sn[:, :, 1, :].unsqueeze(1).broadcast_to([128, nh, 2, 16]), ALU.mult),
                 reads=kin + ['sin'], writes=[kout + 'b'])
            s3 = src.rearrange("p (h d) -> p h d", h=nh)
            P.op('dve', lambda e: e.tensor_tensor(t2.rearrange("p (h d) -> p h d", h=nh), s3,
                                                  cos_t[:, st_idx, :].unsqueeze(1).broadcast_to([128, nh, 64]), ALU.mult),
                 reads=kin + ['cos'], writes=[kout + 'c'])
            P.op('dve', lambda e: e.tensor_tensor(dst_bf, t1, t2, ALU.add),
                 reads=[kout + 'a', kout + 'b', kout + 'c'], writes=[kout])

        def head_rmsnorm(src, nh, gain_bc, ss, tmp, kin, kout):
            s3 = src.rearrange("p (h d) -> p h d", h=nh)
            t3 = tmp.rearrange("p (h d) -> p h d", h=nh)
            P.op('dve', lambda e: e.tensor_tensor(tmp, src, src, ALU.mult), reads=kin, writes=['hn_tmp'])
            P.op('dve', lambda e: e.tensor_reduce(ss[:, 0:nh], t3, AX.X, ALU.add), reads=['hn_tmp'], writes=['hn_ss'])
            P.op('act', lambda e: e.activation(ss[:, 0:nh], ss[:, 0:nh], AF.Sqrt, bias=epsb[:, 0:1], scale=1.0 / 64),
                 reads=['hn_ss', 'epsb'], writes=['hn_ss'])
            P.op('dve', lambda e: e.reciprocal(ss[:, 0:nh], ss[:, 0:nh]), reads=['hn_ss'], writes=['hn_ss'])
            P.op('dve', lambda e: e.tensor_tensor(s3, s3, ss[:, 0:nh].unsqueeze(2).broadcast_to([128, nh, 64]), ALU.mult),
                 reads=kin + ['hn_ss'], writes=[kout])
            P.op('dve', lambda e: e.tensor_tensor(s3, s3, gain_bc.unsqueeze(1).broadcast_to([128, nh, 64]), ALU.mult),
                 reads=[kout, 'gqk'], writes=[kout])

        def attn_layer(l):
            j = l // 2
            lam_init = LAM_INIT[l]
            m_layer = A.mark()
            cos_t = A.f32(1024).rearrange("p (s d) -> p s d", s=16)
            sin_t = A.f32(1024).rearrange("p (s d) -> p s d", s=16)
            cs_box['cos'], cs_box['sin'] = cos_t, sin_t
            P.dma('sp', cos_t, cos_d.rearrange("p (s d) -> p s d", s=16), writes=['cos'])
            P.dma('sp', sin_t, sin_d.rearrange("p (s d) -> p s d", s=16), writes=['sin'])
            KT = A.bf16(5 * 2560).rearrange("p (c t) -> p c t", c=5)
            Vp = A.bf16(NKT * VW).rearrange("p (k w) -> p k w", k=NKT)
            gqk = A.f32(128)
            lamv = A.f32(256)
            sublnS = A.f32(128)
            lam_t = A.f32(4)
            rstd = A.f32(512)
            ntmp = A.f32(512)
            sqb = A.bf16(512)
            P.dma('sp', gqk, gqk_d[j], writes=['gqk'])
            P.dma('sp', lamv, lamv_d[j], writes=['lamv'])
            P.dma('sp', sublnS, subln_d[j], writes=['subln'])
            l4 = lamv.rearrange("p (a d) -> p a d", a=4)
            P.op('dve', lambda e: e.tensor_tensor(l4[:, 0, :], l4[:, 0, :], l4[:, 1, :], ALU.mult), reads=['lamv'], writes=['lamv'])
            P.op('dve', lambda e: e.tensor_tensor(l4[:, 2, :], l4[:, 2, :], l4[:, 3, :], ALU.mult), reads=['lamv'], writes=['lamv'])
            P.op('dve', lambda e: e.tensor_reduce(lam_t[:, 0:1], l4[:, 0, :], AX.X, ALU.add), reads=['lamv'], writes=['lam'])
            P.op('dve', lambda e: e.tensor_reduce(lam_t[:, 1:2], l4[:, 2, :], AX.X, ALU.add), reads=['lamv'], writes=['lam'])
            P.op('act', lambda e: e.activation(lam_t[:, 0:2], lam_t[:, 0:2], AF.Exp), reads=['lam'], writes=['lam'])
            P.op('dve', lambda e: e.tensor_tensor(lam_t[:, 2:3], lam_t[:, 0:1], lam_t[:, 1:2], ALU.subtract), reads=['lam'], writes=['lam'])
            P.op('dve', lambda e: e.tensor_scalar(lam_t[:, 2:3], lam_t[:, 2:3], lam_init, None, ALU.add), reads=['lam'], writes=['lam'])
            P.op('dve', lambda e: e.tensor_scalar(sublnS, sublnS, 1.0 - lam_init, None, ALU.mult), reads=['subln'], writes=['subln'])
            P.op('pool', lambda e: e.memset(Vp, 1.0), writes=[f'Vp:{k}' for k in range(NKT)])
            for c in range(5):
                P.dma('pool', KT[:, c, 0:512], kTc_d[j, c * 128:(c + 1) * 128, :], writes=[f'KT:{c}:c'])
            for k in range(4):
                P.dma('pool', Vp[:, k, 0:132].rearrange("p (h w) -> p h w", h=2)[:, :, 0:64],
                      vc_d[j, k * 128:(k + 1) * 128, 0:128].rearrange("p (h d) -> p h d", h=2),
                      reads=[f'Vp:{k}'], writes=[f'Vp:{k}'])
                P.dma('pool', Vp[:, k, 132:652].rearrange("p (h w) -> p h w", h=4)[:, :, 0:128],
                      vc_d[j, k * 128:(k + 1) * 128, 128:640].rearrange("p (h d) -> p h d", h=4),
                      reads=[f'Vp:{k}'], writes=[f'Vp:{k}'])

            m1 = A.mark()
            Wkv = A.bf16(8 * 1280).rearrange("p (k n) -> p k n", k=8)
            hT = A.bf16(8 * 512).rearrange("p (c t) -> p c t", c=8)
            kv32 = A.f32(1280)
            t1 = A.f32(640)
            t2 = A.f32(640)
            kb16 = A.bf16(640)
            ss = A.f32(16)
            P.dma('pool', Wkv, win_d[j, :, 1024:2304].rearrange("(k p) n -> p k n", p=128), writes=['Wkv'])
            for tb in range(4):
                norm_block(l, 1, tb * 512, 512, hT, 'hT', sqb, rstd, ntmp)
                for s in range(4):
                    sti = tb * 4 + s
                    tok0 = sti * 128
                    for cb, (c0, cn) in enumerate([(0, 512), (512, 512), (1024, 256)]):
                        for k in range(8):
                            P.op('pe', lambda e, k=k, s=s, c0=c0, cn=cn, cb=cb: e.matmul(
                                ps[cb % 2][:, 0:cn], hT[:, k, s * 128:(s + 1) * 128], Wkv[:, k, c0:c0 + cn],
                                start=(k == 0), stop=(k == 7)), reads=[f'hT:{k}', 'Wkv'], writes=[f'ps:{cb % 2}'])
                        eng = 'act' if cb != 1 else 'dve'
                        if eng == 'act':
                            P.op('act', lambda e, c0=c0, cn=cn, cb=cb: e.activation(kv32[:, c0:c0 + cn], ps[cb % 2][:, 0:cn], AF.Copy),
                                 reads=[f'ps:{cb % 2}'], writes=[f'kv32:{cb}'])
                        else:
                            P.op('dve', lambda e, c0=c0, cn=cn, cb=cb: e.tensor_copy(kv32[:, c0:c0 + cn], ps[cb % 2][:, 0:cn]),
                                 reads=[f'ps:{cb % 2}'], writes=[f'kv32:{cb}'])
                    head_rmsnorm(kv32[:, 0:128], 2, gqk[:, 64:128], ss, t1[:, 0:128], ['kv32:0'], 'kv32:0')
                    P.dma('sp', okv_d[j, tok0:tok0 + 128, :], kv32, reads=['kv32:0', 'kv32:1', 'kv32:2'], writes=[K('okv')], final=True)
                    rope_tm(kv32[:, 0:640], 10, sti, kb16, t1, t2, ['kv32:0', 'kv32:1'], 'kb16')
                    for c in range(5):
                        P.op('pe', lambda e, c=c: e.transpose(psb[3][:, c * 128:(c + 1) * 128], kb16[:, c * 128:(c + 1) * 128], ident_bf),
                             reads=['kb16', 'identbf'], writes=['ps:3'])
                    P.op('act', lambda e, tok0=tok0: e.activation(KT[:, :, 512 + tok0:512 + tok0 + 128],
                                                                 psb[3][:, 0:640].rearrange("p (c t) -> p c t", c=5), AF.Copy),
                         reads=['ps:3'], writes=[f'KT:{c}:{sti}' for c in range(5)])
                    kt = 4 + sti
                    P.op('pool', lambda e, kt=kt: e.tensor_copy(Vp[:, kt, 0:132].rearrange("p (h w) -> p h w", h=2)[:, :, 0:64],
                                                               kv32[:, 640:768].rearrange("p (h d) -> p h d", h=2)),
                         reads=['kv32:1', 'kv32:2', f'Vp:{kt}'], writes=[f'Vp:{kt}'])
                    P.op('pool', lambda e, kt=kt: e.tensor_copy(Vp[:, kt, 132:652].rearrange("p (h w) -> p h w", h=4)[:, :, 0:128],
                                                               kv32[:, 768:1280].rearrange("p (h d) -> p h d", h=4)),
                         reads=['kv32:1', 'kv32:2', f'Vp:{kt}'], writes=[f'Vp:{kt}'])
            A.release(m1)
            P.barrier()
            if stage < 3:
                A.release(m_layer)
                return

            Wq = A.bf16(8 * 1024).rearrange("p (k n) -> p k n", k=8)
            Wo = A.bf16(8 * 1024).rearrange("p (k n) -> p k n", k=8)
            hTq = A.bf16(8 * 256).rearrange("p (c t) -> p c t", c=8)
            QT = A.bf16(8 * 256).rearrange("p (c t) -> p c t", c=8)
            catT = A.bf16(8 * 256).rearrange("p (c t) -> p c t", c=8)
            q32 = A.f32(1024)
            qt1 = A.f32(1024)
            qt2 = A.f32(1024)
            qb16 = A.bf16(1024)
            cat = [A.bf16(1024) for _ in range(2)]
            Pt = [A.bf16(512) for _ in range(2)]
            mT = A.f32(8 * 256).rearrange("p (c t) -> p c t", c=8)
            ss = A.f32(16)
            rz = A.f32(8)
            od = A.f32(128)
            od2 = A.f32(128)
            P.dma('pool', Wq, win_d[j, :, 0:1024].rearrange("(k p) n -> p k n", p=128), writes=['Wq'])
            P.dma('pool', Wo, wout_d[j].rearrange("(k p) n -> p k n", p=128), writes=['Wo'])
            all_kt_keys = [[f'KT:{c}:c'] + [f'KT:{c}:{s}' for s in range(16)] for c in range(5)]
            pti = 0
            for qi in range(8):
                t0 = qi * 256
                tb = t0 // 512
                norm_block(l, 1, t0, 256, hTq, 'hTq', sqb, rstd, ntmp)
                for s in range(2):
                    sti = qi * 2 + s
                    for cb in range(2):
                        for k in range(8):
                            P.op('pe', lambda e, k=k, s=s, cb=cb: e.matmul(
                                ps[cb][:, 0:512], hTq[:, k, s * 128:(s + 1) * 128], Wq[:, k, cb * 512:(cb + 1) * 512],
                                start=(k == 0), stop=(k == 7)), reads=[f'hTq:{k}', 'Wq'], writes=[f'ps:{cb}'])
                        P.op('act', lambda e, cb=cb: e.activation(q32[:, cb * 512:(cb + 1) * 512], ps[cb][:, 0:512], AF.Copy),
                             reads=[f'ps:{cb}'], writes=[f'q32:{cb}'])
                    head_rmsnorm(q32[:, 0:512], 8, gqk[:, 0:64], ss, qt1[:, 0:512], ['q32:0'], 'q32:0')
                    rope_tm(q32, 16, sti, qb16, qt1, qt2, ['q32:0', 'q32:1'], 'qb16')
                    for c in range(8):
                        P.op('pe', lambda e, c=c: e.transpose(psb[3][:, c * 128:(c + 1) * 128], qb16[:, c * 128:(c + 1) * 128], ident_bf),
                             reads=['qb16', 'identbf'], writes=['ps:3'])
                    P.op('act', lambda e, s=s: e.activation(QT[:, :, s * 128:(s + 1) * 128],
                                                           psb[3][:, 0:1024].rearrange("p (c t) -> p c t", c=8), AF.Copy),
                         reads=['ps:3'], writes=[f'QT:{s}'])
                for hc in range(8):
                    gqa = hc < 4
                    kc = 0 if gqa else 1 + (hc - 4)
                    W = 66 if gqa else 130
                    accs = [ps[4], ps[4]] if gqa else [ps[4], ps[5]]
                    acck = ['ps:4', 'ps:4'] if gqa else ['ps:4', 'ps:5']
                    for kt in range(NKT):
                        sb = 0 if kt % 2 == 0 else 6
                        ktkeys = all_kt_keys[kc]
                        for half in range(2):
                            P.op('pe', lambda e, half=half, kt=kt, sb=sb, kc=kc, hc=hc: e.matmul(
                                ps[sb + half][:, 0:256], KT[half * 64:(half + 1) * 64, kc, kt * 128:(kt + 1) * 128],
                                QT[half * 64:(half + 1) * 64, hc, :], start=True, stop=True),
                                reads=ktkeys + ['QT:0', 'QT:1'], writes=[f'ps:{sb + half}'])
                        pt = Pt[pti % 2]
                        pk = f"Pt:{pti % 2}"
                        pti += 1
                        P.op('act', lambda e, pt=pt, sb=sb, kt=kt, qi=qi: e.activation(
                            pt.rearrange("p (b n) -> p b n", b=2),
                            psbig[:, sb * 512:(sb + 2) * 512].rearrange("p (b n) -> p b n", b=2)[:, :, 0:256], AF.Exp,
                            bias=maskb[:, qi, kt:kt + 1], scale=0.125),
                             reads=[f'ps:{sb}', f'ps:{sb + 1}', 'maskb'], writes=[pk])
                        for half in range(2):
                            voff = half * 66 if gqa else 132 + (hc - 4) * 130
                            for s in range(2):
                                a0 = (half * 2 + s) * W if gqa else s * W
                                P.op('pe', lambda e, pt=pt, half=half, s=s, a0=a0, voff=voff, W=W, kt=kt, accs=accs, gqa=gqa: e.matmul(
                                    accs[half][:, a0:a0 + W], pt[:, half * 256 + s * 128: half * 256 + (s + 1) * 128],
                                    Vp[:, kt, voff:voff + W], start=(kt == 0 and s == 0 and (half == 0 or not gqa)), stop=(kt == NKT - 1)),
                                    reads=[pk, f'Vp:{kt}'], writes=[acck[half]])
                    if gqa:
                        a3 = ps[4][:, 0:264].rearrange("p (g w) -> p g w", g=4)
                        P.op('dve', lambda e, a3=a3: e.reciprocal(rz[:, 0:4], a3[:, :, 64]), reads=['ps:4'], writes=['rz'])
                        for s in range(2):
                            src = ps[4][:, 0:264].rearrange("p (h s w) -> p h s w", h=2, s=2)[:, :, s, 0:64]
                            rzv = rz[:, 0:4].rearrange("p (h s) -> p h s", h=2)[:, :, s]
                            P.op('dve', lambda e, s=s, src=src, rzv=rzv, hc=hc: e.tensor_tensor(
                                cat[s][:, hc * 128:(hc + 1) * 128].rearrange("p (h d) -> p h d", h=2), src,
                                rzv.unsqueeze(2).broadcast_to([128, 2, 64]), ALU.mult),
                                reads=['ps:4', 'rz'], writes=[f'cat:{s}:{hc}'])
                    else:
                        for s in range(2):
                            P.op('dve', lambda e, s=s: e.reciprocal(rz[:, 0:1], ps[4][:, s * 130 + 128:s * 130 + 129]), reads=['ps:4'], writes=['rz'])
                            P.op('dve', lambda e, s=s: e.reciprocal(rz[:, 1:2], ps[5][:, s * 130 + 128:s * 130 + 129]), reads=['ps:5'], writes=['rz'])
                            P.op('dve', lambda e: e.tensor_tensor(rz[:, 1:2], rz[:, 1:2], lam_t[:, 2:3], ALU.mult), reads=['rz', 'lam'], writes=['rz'])
                            P.op('dve', lambda e, s=s: e.tensor_scalar(od2, ps[5][:, s * 130:s * 130 + 128], rz[:, 1:2], None, ALU.mult),
                                 reads=['ps:5', 'rz'], writes=['od2'])
                            P.op('dve', lambda e, s=s: e.scalar_tensor_tensor(od, ps[4][:, s * 130:s * 130 + 128], rz[:, 0:1], od2,
                                                                              ALU.mult, ALU.subtract), reads=['ps:4', 'rz', 'od2'], writes=['od'])
                            P.op('dve', lambda e: e.tensor_tensor(od2, od, od, ALU.mult), reads=['od'], writes=['od2'])
                            P.op('dve', lambda e: e.tensor_reduce(rz[:, 2:3], od2, AX.X, ALU.add), reads=['od2'], writes=['rz2'])
                            P.op('act', lambda e: e.activation(rz[:, 2:3], rz[:, 2:3], AF.Sqrt, bias=epsb[:, 0:1], scale=1.0 / 128),
                                 reads=['rz2', 'epsb'], writes=['rz2'])
                            P.op('dve', lambda e: e.reciprocal(rz[:, 2:3], rz[:, 2:3]), reads=['rz2'], writes=['rz2'])
                            P.op('dve', lambda e, s=s, hc=hc: e.scalar_tensor_tensor(cat[s][:, hc * 128:(hc + 1) * 128], od, rz[:, 2:3], sublnS,
                                                                                     ALU.mult, ALU.mult),
                                 reads=['od', 'rz2', 'subln'], writes=[f'cat:{s}:{hc}'])
                for s in range(2):
                    for c in range(8):
                        P.op('pe', lambda e, c=c, s=s: e.transpose(psb[3][:, c * 128:(c + 1) * 128], cat[s][:, c * 128:(c + 1) * 128], ident_bf),
                             reads=[f'cat:{s}:{c}', 'identbf'], writes=['ps:3'])
                    P.op('act', lambda e, s=s: e.activation(catT[:, :, s * 128:(s + 1) * 128],
                                                           psb[3][:, 0:1024].rearrange("p (c t) -> p c t", c=8), AF.Copy),
                         reads=['ps:3'], writes=[f'catT:{s}'])
                for c in range(8):
                    b = c % 2
                    for k in range(8):
                        P.op('pe', lambda e, c=c, k=k, b=b: e.matmul(ps[b][:, 0:256], Wo[:, k, c * 128:(c + 1) * 128], catT[:, k, :],
                                                                   start=(k == 0), stop=(k == 7)),
                             reads=['Wo', 'catT:0', 'catT:1'], writes=[f'ps:{b}'])
                    P.op('act', lambda e, c=c, b=b: e.activation(mT[:, c, :], ps[b][:, 0:256], AF.Copy), reads=[f'ps:{b}'], writes=[f'mT:{c}'])
                    P.op('act', lambda e, c=c, b=b: e.activation(sqb[:, 0:256], ps[b][:, 0:256], AF.Square), reads=[f'ps:{b}'], writes=['sqb'])
                    P.op('pe', lambda e, c=c: e.matmul(ps[2][:, 0:256], ones_bf, sqb[:, 0:256], start=(c == 0), stop=(c == 7)),
                         reads=['sqb', 'ones'], writes=['ps:2'])
                residual_update(l, 8, t0, 256, mT, 'mT', rstd, ntmp)
            A.release(m_layer)
            P.barrier()

        def mlp(l):
            m = A.mark()
            hT = A.bf16(8 * 512).rearrange("p (c t) -> p c t", c=8)
            hid = A.bf16(32 * 512).rearrange("p (j t) -> p j t", j=32)
            w1b = [A.bf16(8 * 512).rearrange("p (k n) -> p k n", k=8) for _ in range(2)]
            w2b = [A.bf16(4 * 1024).rearrange("p (j n) -> p j n", j=4) for _ in range(2)]
            r32 = [A.f32(512) for _ in range(2)]
            fT = A.f32(8 * 512).rearrange("p (c t) -> p c t", c=8)
            rstd = A.f32(512)
            ntmp = A.f32(512)
            sqb = A.bf16(512)
            wi = 0
            w2i = 0
            for tb in range(4):
                t0 = tb * 512
                norm_block(l, 2, t0, 512, hT, 'hT', sqb, rstd, ntmp)
                for g in range(8):
                    wb = w1b[wi % 2]
                    wk = f'w1b:{wi % 2}'
                    wi += 1
                    P.dma('pool', wb, w1_d[l, :, g * 512:(g + 1) * 512].rearrange("(k p) n -> p k n", p=128), writes=[wk])
                    for jj in range(4):
                        jx = g * 4 + jj
                        b = jx % 2
                        for k in range(8):
                            P.op('pe', lambda e, wb=wb, jj=jj, k=k, b=b: e.matmul(ps[b][:, 0:512], wb[:, k, jj * 128:(jj + 1) * 128], hT[:, k, :],
                                                                                start=(k == 0), stop=(k == 7)),
                                 reads=[wk] + [f'hT:{k}'], writes=[f'ps:{b}'])
                        P.op('act', lambda e, b=b: e.activation(r32[b], ps[b][:, 0:512], AF.Relu), reads=[f'ps:{b}'], writes=[f'r32:{b}'])
                        P.op('pool' if jx % 2 else 'dve', lambda e, b=b, jx=jx: e.tensor_tensor(hid[:, jx, :], r32[b], r32[b], ALU.mult),
                             reads=[f'r32:{b}'], writes=[f'hid:{jx}'])
                for jg in range(8):
                    wb = w2b[w2i % 2]
                    wk = f'w2b:{w2i % 2}'
                    w2i += 1
                    P.dma('pool', wb, w2_d[l, jg * 512:(jg + 1) * 512, :].rearrange("(j p) n -> p j n", p=128), writes=[wk])
                    for c in range(8):
                        for jj in range(4):
                            jx = jg * 4 + jj
                            P.op('pe', lambda e, wb=wb, jj=jj, jx=jx, c=c: e.matmul(ps[c][:, 0:512], wb[:, jj, c * 128:(c + 1) * 128], hid[:, jx, :],
                                                                                   start=(jx == 0), stop=(jx == 31)),
                                 reads=[wk, f'hid:{jx}'], writes=[f'ps:{c}'])
                for c in range(8):
                    P.op('act', lambda e, c=c: e.activation(fT[:, c, :], ps[c][:, 0:512], AF.Copy), reads=[f'ps:{c}'], writes=[f'fT:{c}'])
                for c in range(8):
                    P.op('act', lambda e, c=c: e.activation(sqb, fT[:, c, :], AF.Square), reads=[f'fT:{c}'], writes=['sqb'])
                    P.op('pe', lambda e, c=c: e.matmul(ps[2][:, 0:512], ones_bf, sqb, start=(c == 0), stop=(c == 7)),
                         reads=['sqb', 'ones'], writes=['ps:2'])
                residual_update(l, 24, t0, 512, fT, 'fT', rstd, ntmp)
            A.release(m)
            P.barrier()


        def mm(out, lhsT, rhs, start, stop, reads, writes):
            P.op('pe', lambda e: e.matmul(out, lhsT, rhs, start=start, stop=stop), reads=reads, writes=writes)

        def tt(eng, out, a, b, op, reads, writes):
            P.op(eng, lambda e: e.tensor_tensor(out, a, b, op), reads=reads, writes=writes)

        def stt(out, a, sc, b, op0, op1, reads, writes):
            P.op('dve', lambda e: e.scalar_tensor_tensor(out, a, sc, b, op0, op1), reads=reads, writes=writes)

        def tsc(out, a, s1, op, reads, writes):
            P.op('dve', lambda e: e.tensor_scalar(out, a, s1, None, op), reads=reads, writes=writes)

        def act(out, in_, func, reads, writes, **kw):
            P.op('act', lambda e: e.activation(out, in_, func, **kw), reads=reads, writes=writes)

        def run_interleaved(gens):
            gens = list(gens)
            while gens:
                for g in list(gens):
                    try:
                        next(g)
                    except StopIteration:
                        gens.remove(g)

        def rwkv_layer(l):
            j = l // 2
            m_layer = A.mark()
            bones32 = A.f32(128)
            bo64 = A.f32(128)
            ones32 = A.f32(128)
            ident32 = A.f32(128)
            keepf = A.f32(1)
            lneps = A.f32(1)
            P.dma('sp', bones32, bones_d, writes=['bones'])
            P.dma('sp', ident32, ident_d, writes=['ident32'])
            P.dma('sp', keepf, keepf_d, writes=['keepf'])
            P.op('dve', lambda e: e.memset(ones32, 1.0), writes=['ones32'])
            P.op('dve', lambda e: e.memset(lneps, 64e-5), writes=['lneps'])
            P.op('dve', lambda e: e.tensor_scalar(bo64, bones32, 1.0 / 64, None, ALU.mult), reads=['bones'], writes=['bo64'])
            vecs = A.f32(72).rearrange("p (i k) -> p i k", i=9)
            muT = A.f32(48).rearrange("p (n k) -> p n k", n=6)
            P.dma('sp', vecs, rvecs_d[j].rearrange("p (i k) -> p i k", i=9), writes=['vecs'])
            P.dma('sp', muT, rmu_d[j].rearrange("p (n k) -> p n k", n=6), writes=['muT'])

            m_d = A.mark()
            hT = A.bf16(8 * T).rearrange("p (c t) -> p c t", c=8)
            xxT = A.bf16(8 * T).rearrange("p (c t) -> p c t", c=8)
            tw = A.bf16(T)
            ta = A.bf16(T)
            gsg = A.bf16(T)
            m_x = A.mark()
            rstd = A.f32(512)
            ntmp = A.f32(512)
            sqb = A.bf16(512)
            mp = A.bf16(T)
            mn = A.bf16(T)
            t1 = A.f32(T)
            t2 = A.f32(T)
            for tb in range(4):
                norm_block(l, 1, tb * 512, 512, hT[:, :, tb * 512:(tb + 1) * 512], f'hT{tb}', sqb, rstd, ntmp)

            def hkeys(c):
                return [f'hT{tb}:{c}' for tb in range(4)]
            allh = [k_ for c in range(8) for k_ in hkeys(c)]
            P.dma('pool', mp, shiftm_d[0], writes=['mp'])
            P.dma('pool', mn, shiftm_d[1], writes=['mn'])
            for c in range(8):
                P.op('pool', lambda e: e.memset(t1[:, 0:1], 0.0), writes=['t1a'])
                tt('pool', t1[:, 1:T], hT[:, c, 0:T - 1], mp[:, 1:T], ALU.mult, hkeys(c) + ['mp'], ['t1b'])
                P.op('dve', lambda e: e.memset(t2[:, T - 1:T], 0.0), writes=['t2a'])
                tt('dve', t2[:, 0:T - 1], hT[:, c, 1:T], mn[:, 0:T - 1], ALU.mult, hkeys(c) + ['mn'], ['t2b'])
                tt('dve', t1, t1, t2, ALU.add, ['t1a', 't1b', 't2a', 't2b'], ['t1a', 't1b'])
                tt('pool', xxT[:, c, :], t1, hT[:, c, :], ALU.subtract, ['t1a', 't1b'] + hkeys(c), [f'xxT:{c}'])
            allxx = [f'xxT:{c}' for c in range(8)]

            def proj(bank, bkey, W, Ws, wkeys, blk):
                for k in range(8):
                    mm(bank, W[:, k, :], hT[:, k, blk], k == 0, False, wkeys + hkeys(k), [bkey])
                for k in range(8):
                    mm(bank, Ws[:, k, :], xxT[:, k, blk], False, k == 7, wkeys + [f'xxT:{k}'], [bkey])

            lws = []
            for i, (src, n) in enumerate([(w1c_d, 1), (a1c_d, 4), (g1_d, 5)]):
                W = A.bf16(8 * 128).rearrange("p (k n) -> p k n", k=8)
                Ws = A.bf16(8 * 128).rearrange("p (k n) -> p k n", k=8)
                P.dma('pool', W, src[j].rearrange("(k p) n -> p k n", p=128), writes=[f'lw:{i}'])
                tt('pool', Ws, W, muT[:, n, :].unsqueeze(2).broadcast_to([128, 8, 128]), ALU.mult, [f'lw:{i}', 'muT'], [f'lws:{i}'])
                lws.append((W, Ws))
            lfun = [AF.Tanh, AF.Copy, AF.Sigmoid]
            ldst = [(tw, 'tw'), (ta, 'ta'), (gsg, 'gsg')]
            for tb in range(4):
                blk = slice(tb * 512, (tb + 1) * 512)
                for i in range(3):
                    b = (tb * 3 + i) % 4
                    proj(ps[b], f'ps:{b}', lws[i][0], lws[i][1], [f'lw:{i}', f'lws:{i}'], blk)
                    act(ldst[i][0][:, blk], ps[b], lfun[i], [f'ps:{b}'], [f'{ldst[i][1]}:{tb}'])
            A.release(m_x)
            P.barrier()
            if rstage < 2:
                A.release(m_layer)
                P.barrier()
                return

            w2x = [A.bf16(D) for _ in range(2)]
            a2x = [A.bf16(D) for _ in range(2)]
            g2t = A.bf16(D)
            for d in range(2):
                P.dma('pool', w2x[d], w2x_d[j, d], writes=[f'w2x:{d}'])
                P.dma('pool', a2x[d], a2x_d[j, d], writes=[f'a2x:{d}'])
            P.dma('pool', g2t, g2_d[j], writes=['g2t'])
            Wp = [A.bf16(8 * 128).rearrange("p (k n) -> p k n", k=8) for _ in range(6)]
            TL = [A.f32(512) for _ in range(13)]
            r32, k32, v32, kka, kk, sq, lwn, at, kd, bt, rk, gb, vt32 = TL
            for p in range(8):
                pc = slice(p * 128, (p + 1) * 128)
                for i, n in enumerate((0, 2, 3)):
                    P.dma('pool', Wp[2 * i], wrkv_d[j, i, :, pc].rearrange("(k q) n -> q k n", q=128), writes=[f'Wp:{2 * i}'])
                    tt('pool', Wp[2 * i + 1], Wp[2 * i], muT[:, n, :].unsqueeze(2).broadcast_to([128, 8, 128]), ALU.mult,
                       [f'Wp:{2 * i}', 'muT'], [f'Wp:{2 * i + 1}'])
                for tb in range(4):
                    blk = slice(tb * 512, (tb + 1) * 512)
                    proj(ps[0], 'ps:0', Wp[0], Wp[1], ['Wp:0', 'Wp:1'], blk)
                    act(r32, ps[0], AF.Copy, ['ps:0'], ['r32'])
                    P.dma('sp', SCR['R'][pc, blk], r32, reads=['r32'], writes=[K('scr')])
                    proj(ps[1], 'ps:1', Wp[2], Wp[3], ['Wp:2', 'Wp:3'], blk)
                    act(k32, ps[1], AF.Copy, ['ps:1'], ['k32'])
                    proj(ps[2], 'ps:2', Wp[4], Wp[5], ['Wp:4', 'Wp:5'], blk)
                    act(v32, ps[2], AF.Copy, ['ps:2'], ['v32'])
                    mm(ps[7], g2t[:, pc], gsg[:, blk], True, True, ['g2t', f'gsg:{tb}'], ['ps:7'])
                    act(gb, ps[7], AF.Copy, ['ps:7'], ['gb'])
                    P.dma('sp', SCR['G'][pc, blk], gb, reads=['gb'], writes=[K('scr')])
                    act(kka, k32, AF.Copy, ['k32', 'vecs'], ['kka'], scale=vecs[:, 5, p:p + 1])
                    act(kk, k32, AF.Copy, ['k32', 'vecs'], ['kk'], scale=vecs[:, 4, p:p + 1])
                    tt('pool', sq, kk, kk, ALU.mult, ['kk'], ['sq'])
                    mm(ps[3], bones32, sq, True, True, ['bones', 'sq'], ['ps:3'])
                    act(sq, ps[3], AF.Sqrt, ['ps:3'], ['sq'], bias=1e-12, scale=1.0)
                    P.op('dve', lambda e: e.reciprocal(sq, sq), reads=['sq'], writes=['sq'])
                    tt('pool', kk, kk, sq, ALU.mult, ['kk', 'sq'], ['kk'])
                    P.dma('sp', SCR['KA'][pc, blk], kk, reads=['kk'], writes=[K('scr')])
                    for d in range(2):
                        mm(ps[4], w2x[d][:, pc], tw[:, blk], True, True, [f'w2x:{d}', f'tw:{tb}'], ['ps:4'])
                        act(lwn, ps[4], AF.Sigmoid, ['ps:4', 'vecs'], ['lwn'], bias=vecs[:, d, p:p + 1], scale=1.0)
                        P.op('pool', lambda e: e.tensor_scalar(lwn, lwn, float(np.exp(-0.5)), None, ALU.mult), reads=['lwn'], writes=['lwn'])
                        P.dma('sp', SCR[f'LW{d}'][pc, blk], lwn, reads=['lwn'], writes=[K('scr')])
                        mm(ps[5], a2x[d][:, pc], ta[:, blk], True, True, [f'a2x:{d}', f'ta:{tb}'], ['ps:5'])
                        act(at, ps[5], AF.Sigmoid, ['ps:5', 'vecs'], ['at'], bias=vecs[:, 2 + d, p:p + 1], scale=1.0)
                        stt(kd, at, -1.0, kka, ALU.add, ALU.mult, ['at', 'kka'], ['kd'])
                        tt('pool', kd, kd, k32, ALU.add, ['kd', 'k32'], ['kd'])
                        P.dma('sp', SCR[f'KD{d}'][pc, blk], kd, reads=['kd'], writes=[K('scr')])
                        tt('pool', bt, kk, at, ALU.mult, ['kk', 'at'], ['bt'])
                        P.dma('sp', SCR[f'B{d}'][pc, blk], bt, reads=['bt'], writes=[K('scr')])
                        stt(rk, r32, vecs[:, 6, p:p + 1], kd, ALU.mult, ALU.mult, ['r32', 'kd', 'vecs'], ['rk'])
                        mm(ps[6], bones32, rk, d == 0, d == 1, ['bones', 'rk'], ['ps:6'])
                    tt('dve', gb, v32, ps[6], ALU.mult, ['v32', 'ps:6', 'gb'], ['gb'])
                    P.dma('sp', SCR['BON'][pc, blk], gb, reads=['gb'], writes=[K('scr')])
                    for sub in range(4):
                        tk = slice(tb * 512 + sub * 128, tb * 512 + (sub + 1) * 128)
                        osl = ps[3][:, sub * 128:(sub + 1) * 128]
                        for k in range(8):
                            mm(osl, hT[:, k, tk], Wp[4][:, k, :], k == 0, False, ['Wp:4'] + hkeys(k), ['ps:3'])
                        for k in range(8):
                            mm(osl, xxT[:, k, tk], Wp[5][:, k, :], False, k == 7, ['Wp:5', f'xxT:{k}'], ['ps:3'])
                    act(vt32, ps[3], AF.Copy, ['ps:3'], ['vt32'])
                    P.dma('sp', Vt_s[tb * 512:(tb + 1) * 512, pc].rearrange("(s t) n -> t s n", t=128),
                          vt32.rearrange("p (s n) -> p s n", s=4), reads=['vt32'], writes=[K('scr')])
            A.release(m_d)
            P.barrier()
            if rstage < 3:
                A.release(m_layer)
                P.barrier()
                return

            m_s = A.mark()
            maskA = []
            maskN = []
            for d in range(2):
                ma = A.f32(512)
                mnn = A.f32(128)
                P.dma('sp', ma, rmask_d[d, :, 0:512], writes=[f'maskA:{d}'])
                P.dma('sp', mnn, rmask_d[d, :, 512:640], writes=[f'maskN:{d}'])
                maskA.append(ma)
                maskN.append(mnn)
            bd16 = A.f32(256)
            P.dma('sp', bd16, bd16_d, writes=['bd16'])
            NU = 4
            U = []
            for u in range(NU):
                t = {}
                for nm in ('cum', 'cu2', 'E1', 'E2', 'E3', 'Kx0', 'Kx1', 'Bx0', 'Bx1', 'Kp', 'Bp', 'N0', 'N1', 'W0', 'W1',
                           'Ux0', 'Ux1', 'KTt', 'BTt', 'Hx', 'yt', 'dl', 'gn',
                           'Md0', 'Md1', 'Nd0', 'Nd1', 'Mo0', 'Mo1', 'Pa0', 'Pa1', 'Pb0', 'Pb1', 'X'):
                    t[nm] = A.f32(128)
                t['KR'] = A.f32(256)
                t['AT0'] = A.f32(512)
                t['AT1'] = A.f32(512)
                t['MN'] = [[A.f32(256), A.f32(256), A.f32(128)] for _ in range(2)]
                t['ld'] = [{nm: A.f32(128) for nm in ('r', 'ka', 'kd', 'b', 'lw', 'Vx0', 'Vx1')} for _ in range(1)]
                for nm in ('Kx0', 'Kx1', 'Bx0', 'Bx1', 'Ux0', 'Ux1'):
                    P.op('pool', lambda e, tl=t[nm]: e.memset(tl, 0.0), writes=[f'u{u}:{nm}'])
                for par in range(1):
                    for nm in ('Vx0', 'Vx1'):
                        P.op('pool', lambda e, tl=t['ld'][par][nm]: e.memset(tl, 0.0), writes=[f'u{u}:ld{par}:{nm}'])
                U.append(t)

            def chain(p, d, u):
                t = U[u]
                kp = f'u{u}:'
                bX, bY = 2 * u, 2 * u + 1
                bA = [ps[bX], ps[bX]]
                bAk = [f'ps:{bX}', f'ps:{bX}']
                bB = ps[bY]
                bC = ps[bX]
                pr = slice(p * 128, (p + 1) * 128)
                Hx = t['Hx']
                P.dma('sp', Hx, st0_d[j, d, p], writes=[kp + 'Hx'])
                yscr = SCR['YF'] if d == 0 else SCR['YB']
                order = list(range(16)) if d == 0 else list(range(15, -1, -1))
                for ci, c in enumerate(order):
                    par = 0
                    L = t['ld'][par]
                    lk = lambda nm: f'{kp}ld{par}:{nm}'
                    cs = slice(c * 128, (c + 1) * 128)
                    P.dma('sp', L['r'], SCR['R'][pr, cs], writes=[lk('r')])
                    P.dma('sp', L['ka'], SCR['KA'][pr, cs], writes=[lk('ka')])
                    P.dma('sp', L['kd'], SCR[f'KD{d}'][pr, cs], writes=[lk('kd')])
                    P.dma('sp', L['b'], SCR[f'B{d}'][pr, cs], writes=[lk('b')])
                    P.dma('sp', L['lw'], SCR[f'LW{d}'][pr, cs], writes=[lk('lw')])
                    for h in range(2):
                        P.dma('sp', L[f'Vx{h}'][:, h * 64:(h + 1) * 64], Vt_s[cs, p * 128 + h * 64:p * 128 + (h + 1) * 64],
                              writes=[lk(f'Vx{h}')])
                    yield
                    cum = t['cum']
                    P.op('dve', lambda e, cum=cum, L=L: e.tensor_tensor_scan(cum, ones32, L['lw'], 0.0, ALU.mult, ALU.add),
                         reads=[lk('lw'), 'ones32'], writes=[kp + 'cum'])
                    if d == 0:
                        cu = cum
                        cuk = kp + 'cum'
                    else:
                        cu = t['cu2']
                        cuk = kp + 'cu2'
                        tt('dve', cu, L['lw'], cum, ALU.subtract, [lk('lw'), kp + 'cum'], [cuk])
                        tsc(cu, cu, cum[:, 127:128], ALU.add, [cuk, kp + 'cum'], [cuk])
                    act(t['E1'], cu, AF.Exp, [cuk], [kp + 'E1'], scale=-1.0)
                    act(t['E2'], cu, AF.Exp, [cuk], [kp + 'E2'], scale=1.0)
                    tt('pool', t['E3'], cu, L['lw'], ALU.subtract, [cuk, lk('lw')], [kp + 'E3'])
                    act(t['E3'], t['E3'], AF.Exp, [kp + 'E3'], [kp + 'E3'], scale=-1.0)
                    gam = t['E1'][:, 127:128] if d == 0 else t['E1'][:, 0:1]
                    KR = t['KR']
                    tt('pool', KR[:, 128:256], L['r'], t['E1'], ALU.mult, [lk('r'), kp + 'E1'], [kp + 'KRr'])
                    tt('pool', KR[:, 0:128], L['ka'], t['E3'], ALU.mult, [lk('ka'), kp + 'E3'], [kp + 'KRk'])
                    for h in range(2):
                        hs = slice(h * 64, (h + 1) * 64)
                        tt('pool', t[f'Kx{h}'][hs, :], L['kd'][hs, :], t['E2'][hs, :], ALU.mult, [lk('kd'), kp + 'E2'], [kp + f'Kx{h}'])
                        stt(t[f'Bx{h}'][hs, :], L['b'][hs, :], -1.0, t['E2'][hs, :], ALU.mult, ALU.mult, [lk('b'), kp + 'E2'], [kp + f'Bx{h}'])
                    stt(t['Kp'], L['kd'], gam, t['E2'], ALU.mult, ALU.mult, [lk('kd'), kp + 'E1', kp + 'E2'], [kp + 'Kp'])
                    tsc(t['gn'][:, 0:1], gam, -1.0, ALU.mult, [kp + 'E1'], [kp + 'gn'])
                    stt(t['Bp'], L['b'], t['gn'][:, 0:1], t['E2'], ALU.mult, ALU.mult, [lk('b'), kp + 'gn', kp + 'E2'], [kp + 'Bp'])
                    yield
                    AT = [t['AT0'], t['AT1']]
                    Nn = [t['N0'], t['N1']]
                    for h in range(2):
                        mm(bB[:, h * 128:(h + 1) * 128], KR[:, 0:128], t[f'Bx{h}'], True, True, [kp + f'Bx{h}', kp + 'KRk'], [f'ps:{bY}'])
                    for h in range(2):
                        mm(bA[h][:, 0:256], t[f'Kx{h}'], KR, True, True, [kp + f'Kx{h}', kp + 'KRr', kp + 'KRk'], [bAk[h]])
                        mm(bA[h][:, 256:512], t[f'Bx{h}'], KR, True, True, [kp + f'Bx{h}', kp + 'KRr', kp + 'KRk'], [bAk[h]])
                        tt('dve', AT[h], bA[h], maskA[d], ALU.mult, [bAk[h], f'maskA:{d}'], [kp + f'AT{h}'])
                    for h in range(2):
                        tt('dve', Nn[h], bB[:, h * 128:(h + 1) * 128], maskN[d], ALU.mult, [f'ps:{bY}', f'maskN:{d}'], [kp + f'N{h}'])
                    yield
                    wps = bB[:, 256:384]
                    import os as _os
                    _rw = _os.environ.get('RW_W', '')
                    if _rw == '1':
                        mm(wps, KR[:, 0:128], Hx, True, True, [kp + 'KRk', kp + 'Hx'], [f'ps:{bY}'])
                    elif _rw == '2':
                        mm(wps, AT[0][:, 0:128], L['Vx0'], True, True, [kp + 'AT0', lk('Vx0')], [f'ps:{bY}'])
                    elif _rw == '3':
                        mm(wps, AT[0][:, 0:128], L['r'], True, True, [kp + 'AT0', lk('r')], [f'ps:{bY}'])
                    else:
                        mm(wps, KR[:, 0:128], Hx, True, False, [kp + 'KRk', kp + 'Hx'], [f'ps:{bY}'])
                        for h in range(2):
                            mm(wps, AT[h][:, 0:128], L[f'Vx{h}'], False, h == 1, [kp + f'AT{h}', lk(f'Vx{h}')], [f'ps:{bY}'])
                    Wt = [t['W0'], t['W1']]
                    act(Wt[0], wps, AF.Copy, [f'ps:{bY}'], [kp + 'W0'])
                    yield
                    Md = [t['Md0'], t['Md1']]
                    Nd = [t['Nd0'], t['Nd1']]
                    Mo = [t['Mo0'], t['Mo1']]
                    Pa = [t['Pa0'], t['Pa1']]
                    Pb = [t['Pb0'], t['Pb1']]
                    for h in range(2):
                        tt('pool', Md[h], AT[h][:, 256:384], bd16[:, 0:128], ALU.mult, [kp + f'AT{h}', 'bd16'], [kp + f'Md{h}'])
                        tt('pool', Mo[h], AT[h][:, 256:384], bd16[:, 128:256], ALU.mult, [kp + f'AT{h}', 'bd16'], [kp + f'Mo{h}'])
                        tt('pool', Nd[h], Nn[h], bd16[:, 0:128], ALU.mult, [kp + f'N{h}', 'bd16'], [kp + f'Nd{h}'])
                        tt('pool', Pa[h], Md[h], ident32, ALU.add, [kp + f'Md{h}', 'ident32'], [kp + f'Pa{h}'])
                    yield
                    bCk = f'ps:{bX}'
                    bBk = f'ps:{bY}'
                    MN2 = [t['MN'][h][0] for h in range(2)]
                    MN4 = [t['MN'][h][1] for h in range(2)]
                    N8 = [t['MN'][h][2] for h in range(2)]
                    for h in range(2):
                        cps = bC[:, h * 256:(h + 1) * 256]
                        mm(cps[:, 0:128], Nd[h], Md[h], True, True, [kp + f'Md{h}', kp + f'Nd{h}'], [bCk])
                        mm(cps[:, 128:256], Md[h], Nd[h], True, True, [kp + f'Md{h}', kp + f'Nd{h}'], [bCk])
                    for h in range(2):
                        act(MN2[h], bC[:, h * 256:(h + 1) * 256], AF.Copy, [bCk], [kp + f'MN2{h}'])
                    yield
                    for h in range(2):
                        cps = bC[:, h * 256:(h + 1) * 256]
                        mm(cps[:, 0:128], MN2[h][:, 128:256], MN2[h][:, 0:128], True, True, [kp + f'MN2{h}'], [bCk])
                        mm(cps[:, 128:256], MN2[h][:, 0:128], MN2[h][:, 128:256], True, True, [kp + f'MN2{h}'], [bCk])
                    for h in range(2):
                        act(MN4[h], bC[:, h * 256:(h + 1) * 256], AF.Copy, [bCk], [kp + f'MN4{h}'])
                    yield
                    for h in range(2):
                        mm(bC[:, h * 128:(h + 1) * 128], MN4[h][:, 0:128], MN4[h][:, 128:256], True, True, [kp + f'MN4{h}'], [bCk])
                    for h in range(2):
                        act(N8[h], bC[:, h * 128:(h + 1) * 128], AF.Copy, [bCk], [kp + f'N8{h}'])
                    yield
                    Pc, Pn, pck, pnk = Pa, Pb, 'Pa', 'Pb'
                    for lhs, lk_ in ((lambda h: MN2[h][:, 128:256], 'MN2'), (lambda h: MN4[h][:, 128:256], 'MN4'), (lambda h: N8[h], 'N8')):
                        for h in range(2):
                            mm(bC[:, h * 128:(h + 1) * 128], lhs(h), Pc[h], True, True, [kp + f'{lk_}{h}', kp + f'{pck}{h}'], [bCk])
                        for h in range(2):
                            tt('dve', Pn[h], Pc[h], bC[:, h * 128:(h + 1) * 128], ALU.add, [kp + f'{pck}{h}', bCk], [kp + f'{pnk}{h}'])
                        Pc, Pn, pck, pnk = Pn, Pc, pnk, pck
                        yield
                    TdT, tdk = Pc, pck
                    Wm = Wt[0]
                    Ut = Wt[1]
                    Uk = kp + 'W1'
                    Xt = t['X']
                    ups = bC[:, 0:128]
                    zps = bB[:, 256:384]
                    for h in range(2):
                        hc_ = slice(h * 64, (h + 1) * 64)
                        mm(ups[:, hc_], TdT[h], Wm[:, hc_], True, True, [kp + f'{tdk}{h}', kp + 'W0'], [bCk])
                    act(Ut, ups, AF.Copy, [bCk], [Uk])
                    yield
                    for it in range(7):
                        for h in range(2):
                            hc_ = slice(h * 64, (h + 1) * 64)
                            mm(zps[:, hc_], Mo[h], Ut[:, hc_], True, True, [kp + f'Mo{h}', Uk], [bBk])
                        tt('dve', Xt, Wm, zps, ALU.add, [kp + 'W0', bBk], [kp + 'X'])
                        for h in range(2):
                            hc_ = slice(h * 64, (h + 1) * 64)
                            mm(ups[:, hc_], TdT[h], Xt[:, hc_], True, True, [kp + f'{tdk}{h}', kp + 'X'], [bCk])
                        act(Ut, ups, AF.Copy, [bCk], [Uk])
                        yield
                    for h in range(2):
                        P.op('pool', lambda e, h=h, Ut=Ut: e.tensor_copy(t[f'Ux{h}'][:, h * 64:(h + 1) * 64], Ut[:, h * 64:(h + 1) * 64]),
                             reads=[Uk], writes=[kp + f'Ux{h}'])
                    yps = bB[:, 384:512]
                    mm(yps, Hx, KR[:, 128:256], True, False, [kp + 'Hx', kp + 'KRr'], [f'ps:{bY}'])
                    for h in range(2):
                        mm(yps, L[f'Vx{h}'], AT[h][:, 128:256], False, False, [lk(f'Vx{h}'), kp + f'AT{h}'], [f'ps:{bY}'])
                    for h in range(2):
                        mm(yps, t[f'Ux{h}'], AT[h][:, 384:512], False, h == 1, [kp + f'Ux{h}', kp + f'AT{h}'], [f'ps:{bY}'])
                    act(t['yt'], yps, AF.Copy, [f'ps:{bY}'], [kp + 'yt'])
                    P.dma('sp', yscr[pr, cs], t['yt'], reads=[kp + 'yt'], writes=[f'Y{d}:{p}:{c}'])
                    yield
                    mm(bA[0][:, 0:128], t['Kp'], ident32, True, True, [kp + 'Kp', 'ident32'], [bAk[0]])
                    mm(bA[0][:, 128:256], t['Bp'], ident32, True, True, [kp + 'Bp', 'ident32'], [bAk[0]])
                    act(t['KTt'], bA[0][:, 0:128], AF.Copy, [bAk[0]], [kp + 'KTt'])
                    act(t['BTt'], bA[0][:, 128:256], AF.Copy, [bAk[0]], [kp + 'BTt'])
                    dps = bA[1][:, 0:128]
                    mm(dps, t['KTt'], L['Vx0'], True, False, [kp + 'KTt', lk('Vx0')], [bAk[1]])
                    mm(dps, t['KTt'], L['Vx1'], False, False, [kp + 'KTt', lk('Vx1')], [bAk[1]])
                    mm(dps, t['BTt'], Ut, False, True, [kp + 'BTt', Uk], [bAk[1]])
                    tt('dve', t['dl'], dps, bones32, ALU.mult, [bAk[1], 'bones'], [kp + 'dl'])
                    stt(Hx, Hx, gam, t['dl'], ALU.mult, ALU.add, [kp + 'Hx', kp + 'E1', kp + 'dl'], [kp + 'Hx'])
                    if ci % 2 == 1:
                        seq = c // 2
                        P.dma('sp', ostate_d[j, seq, d, p], Hx, reads=[kp + 'Hx'], writes=[K('ost')], final=True)
                        if ci < 15:
                            tsc(Hx, Hx, keepf[:, 0:1], ALU.mult, [kp + 'Hx', 'keepf'], [kp + 'Hx'])
                    yield

            PT = [A.f32(256) for _ in range(6)]
            def limited(g):
                n = 0
                for _ in g:
                    n += 1
                    if n >= rstop:
                        return
                    yield

            for p in range(8):
                if rstop < 999:
                    if p > 0:
                        break
                    import os as _os
                    if _os.environ.get('RW_SINGLE') == '0':
                        run_interleaved([limited(chain(p, 0, 0))])
                    elif _os.environ.get('RW_SINGLE') == '1':
                        run_interleaved([limited(chain(p, 1, 1))])
                    else:
                        run_interleaved([limited(chain(p, 0, 0)), limited(chain(p, 1, 1))])
                    continue
                if p % 2 == 0:
                    run_interleaved([chain(p, 0, 0), chain(p, 1, 1), chain(p + 1, 0, 2), chain(p + 1, 1, 3)])
                pr = slice(p * 128, (p + 1) * 128)
                for tb in range(8):
                    blk = slice(tb * 256, (tb + 1) * 256)
                    yf, yb, dd, s2, bn, gg_ = PT
                    ykeys = [f'Y{d}:{p}:{c}' for d in range(2) for c in range(tb * 2, tb * 2 + 2)]
                    P.dma('sp', yf, SCR['YF'][pr, blk], reads=ykeys, writes=['pt:yf'])
                    P.dma('sp', yb, SCR['YB'][pr, blk], reads=ykeys, writes=['pt:yb'])
                    P.dma('sp', bn, SCR['BON'][pr, blk], writes=['pt:bn'])
                    P.dma('sp', gg_, SCR['G'][pr, blk], writes=['pt:gg'])
                    tt('pool', yf, yf, yb, ALU.add, ['pt:yf', 'pt:yb'], ['pt:yf'])
                    mm(ps[0][:, 0:256], bo64, yf, True, True, ['bo64', 'pt:yf'], ['ps:0'])
                    tt('dve', dd, yf, ps[0][:, 0:256], ALU.subtract, ['pt:yf', 'ps:0'], ['pt:dd'])
                    tt('pool', s2, dd, dd, ALU.mult, ['pt:dd'], ['pt:s2'])
                    mm(ps[1][:, 0:256], bo64, s2, True, True, ['bo64', 'pt:s2'], ['ps:1'])
                    act(s2, ps[1][:, 0:256], AF.Sqrt, ['ps:1', 'lneps'], ['pt:s2'], bias=lneps[:, 0:1], scale=1.0)
                    P.op('dve', lambda e, s2=s2: e.reciprocal(s2, s2), reads=['pt:s2'], writes=['pt:s2'])
                    tt('pool', dd, dd, s2, ALU.mult, ['pt:dd', 'pt:s2'], ['pt:dd'])
                    P.op('dve', lambda e, dd=dd, p=p: e.tensor_scalar(dd, dd, vecs[:, 7, p:p + 1], vecs[:, 8, p:p + 1], ALU.mult, ALU.add),
                         reads=['pt:dd', 'vecs'], writes=['pt:dd'])
                    tt('pool', dd, dd, bn, ALU.add, ['pt:dd', 'pt:bn'], ['pt:dd'])
                    tt('pool', dd, dd, gg_, ALU.mult, ['pt:dd', 'pt:gg'], ['pt:dd'])
                    P.dma('sp', SCR['YG'][pr, blk], dd, reads=['pt:dd'], writes=[f'YG:{p}:{tb}'])
            A.release(m_s)
            P.barrier()
            if rstage < 4:
                A.release(m_layer)
                P.barrier()
                return

            m_o = A.mark()
            Wo = A.bf16(8 * 1024).rearrange("p (k n) -> p k n", k=8)
            ygT = A.bf16(8 * 512).rearrange("p (c t) -> p c t", c=8)
            mT = A.f32(8 * 512).rearrange("p (c t) -> p c t", c=8)
            rstd = A.f32(512)
            ntmp = A.f32(512)
            sqb = A.bf16(512)
            P.dma('pool', Wo, wor_d[j].rearrange("(k p) n -> p k n", p=128), writes=['Wo'])
            for tb in range(4):
                blk = slice(tb * 512, (tb + 1) * 512)
                for c in range(8):
                    P.dma('pool', ygT[:, c, :], SCR['YG'][c * 128:(c + 1) * 128, blk], writes=[f'ygT:{c}'])
                for c in range(8):
                    b = c % 2
                    for k in range(8):
                        mm(ps[b], Wo[:, k, c * 128:(c + 1) * 128], ygT[:, k, :], k == 0, k == 7, ['Wo', f'ygT:{k}'], [f'ps:{b}'])
                    act(mT[:, c, :], ps[b], AF.Copy, [f'ps:{b}'], [f'mT:{c}'])
                    act(sqb, ps[b], AF.Square, [f'ps:{b}'], ['sqb'])
                    mm(ps[2], ones_bf, sqb, c == 0, c == 7, ['sqb', 'ones'], ['ps:2'])
                residual_update(l, 8, tb * 512, 512, mT, 'mT', rstd, ntmp)
            A.release(m_layer)
            P.barrier()

        for l in range(n_layers):
            if l % 2 == 0 and stage >= 2:
                attn_layer(l)
            elif l % 2 == 1 and do_rwkv:
                rwkv_layer(l)
            if stage >= 4:
                mlp(l)

        for c in range(8):
            P.dma('sp', yT_d[c * 128:(c + 1) * 128, :], xT[:, c, :], reads=[f'x:{c}:{tb}' for tb in range(4)], writes=[K('yT')], final=True)
        P.emit()
    return nc, P


_PROG = {}


def _rope_tables(rows=32, grid_w=64):
    row = np.repeat(np.arange(rows), grid_w).astype(np.float32)
    col = np.tile(np.arange(grid_w), rows).astype(np.float32)
    inv = (1.0 / (np.float32(10000.0) ** (np.arange(0, 32, 2, dtype=np.float32) / np.float32(32)))).astype(np.float32)
    ar = row[:, None] * inv[None, :]
    ac = col[:, None] * inv[None, :]
    ang = np.concatenate([ar, ar, ac, ac], axis=-1).astype(np.float32)
    return np.cos(ang).astype(np.float32), np.sin(ang).astype(np.float32)


def _tok_major_table(t):
    return np.ascontiguousarray(t.reshape(16, 128, 64).transpose(1, 0, 2).reshape(128, 1024))


def kernel(x_prompt, x_sample, c, cache_k_gqa, cache_v_gqa, cache_k_diff, cache_v_diff, state_rwkv,
           c_ctx, w_ada, b_ada, norm_gains, attn_w_in, attn_w_out, attn_qk_gain, diff_lambda, diff_subln,
           rwkv_mu, rwkv_w_rkv, rwkv_w_o, rwkv_w0, rwkv_w1, rwkv_w2, rwkv_a0, rwkv_a1, rwkv_a2,
           rwkv_g1, rwkv_g2, rwkv_kvec, rwkv_lnx, mlp_w1, mlp_w2):
    f = lambda a: np.ascontiguousarray(np.asarray(a, dtype=np.float32))
    x_prompt, x_sample, c = f(x_prompt), f(x_sample), f(c)
    if 'nc' not in _PROG:
        _PROG['nc'] = build_program()[0]
    nc = _PROG['nc']

    qa_perm = np.concatenate([np.r_[h * 64:(h + 1) * 64, (4 + h) * 64:(5 + h) * 64] for h in range(4)])
    cols = np.concatenate([qa_perm, np.arange(768, 1280), np.arange(512, 640), np.arange(1280, 1792),
                           np.arange(640, 768), np.arange(1792, 2304)])
    w_in = f(np.asarray(attn_w_in)[:, :, cols])
    rows = np.concatenate([qa_perm, np.arange(512, 1024)])
    w_out = f(np.asarray(attn_w_out)[:, rows, :])
    gqk = f(np.concatenate([np.broadcast_to(np.asarray(attn_qk_gain)[:, None, 0, :], (2, 128, 64)),
                            np.broadcast_to(np.asarray(attn_qk_gain)[:, None, 1, :], (2, 128, 64))], axis=2))
    lamv = f(np.broadcast_to(np.asarray(diff_lambda).reshape(2, 1, 256), (2, 128, 256)))
    subln = f(np.broadcast_to(np.asarray(diff_subln).reshape(2, 1, 128), (2, 128, 128)))
    bada = f(np.asarray(b_ada).reshape(4, 48, 128).transpose(0, 2, 1))
    gains = f(np.asarray(norm_gains).reshape(4, 4, 8, 128).transpose(0, 3, 1, 2).reshape(4, 128, 32))
    ident = np.eye(128, dtype=np.float32)
    cos, sin = _rope_tables()
    sinS = sin.reshape(2048, 2, 2, 16).copy()
    sinS[:, :, 0, :] *= -1.0
    sinS = sinS.reshape(2048, 64)
    cos_s, sin_s = _tok_major_table(cos), _tok_major_table(sinS)
    cos_p, sin_p = np.ones((128, 1024), np.float32), np.zeros((128, 1024), np.float32)
    mask_s = np.zeros((128, 8, 20), np.float32)
    mask_p = np.full((128, 8, 20), -25.0, np.float32)
    for i in range(8):
        mask_p[:, i, 4 + 2 * i: 6 + 2 * i] = 0.0
    w_ada_, w1_, w2_ = f(w_ada), f(mlp_w1), f(mlp_w2)

    W1, A1 = np.asarray(rwkv_w1), np.asarray(rwkv_a1)
    w1c = f(np.concatenate([W1[:, 0], W1[:, 1]], axis=-1))
    a1c = f(np.concatenate([A1[:, 0], A1[:, 1]], axis=-1))
    w2x = np.zeros((2, 2, 128, 1024), np.float32)
    a2x = np.zeros((2, 2, 128, 1024), np.float32)
    for d_ in range(2):
        w2x[:, d_, d_ * 64:(d_ + 1) * 64, :] = np.asarray(rwkv_w2)[:, d_]
        a2x[:, d_, d_ * 64:(d_ + 1) * 64, :] = np.asarray(rwkv_a2)[:, d_]
    rmu = f(np.asarray(rwkv_mu).reshape(2, 6, 8, 128).transpose(0, 3, 1, 2).reshape(2, 128, 48))
    vec9 = np.stack([np.asarray(rwkv_w0)[:, 0], np.asarray(rwkv_w0)[:, 1], np.asarray(rwkv_a0)[:, 0], np.asarray(rwkv_a0)[:, 1],
                     np.asarray(rwkv_kvec)[:, 0], np.asarray(rwkv_kvec)[:, 1], np.asarray(rwkv_kvec)[:, 2],
                     np.asarray(rwkv_lnx)[:, 0], np.asarray(rwkv_lnx)[:, 1]], axis=1)
    rvecs = f(vec9.reshape(2, 9, 8, 128).transpose(0, 3, 1, 2).reshape(2, 128, 72))
    bones = np.zeros((128, 128), np.float32)
    bones[0:64, 0:64] = 1.0
    bones[64:128, 64:128] = 1.0
    ii = np.arange(128)
    lt = (ii[:, None] < ii[None, :]).astype(np.float32)
    le = (ii[:, None] <= ii[None, :]).astype(np.float32)
    rmask = np.zeros((2, 128, 640), np.float32)
    rmask[0] = np.concatenate([lt, le, lt, le, lt.T], axis=1)
    rmask[1] = np.concatenate([lt.T, le.T, lt.T, le.T, lt], axis=1)
    b16 = (ii[:, None] // 16 == ii[None, :] // 16).astype(np.float32)
    bd16 = f(np.concatenate([b16, 1.0 - b16], axis=1))
    tpos = np.arange(2048)
    sh_s = np.stack([np.where(tpos == 0, 0.0, 0.5), np.where(tpos == 2047, 0.0, 0.5)]).astype(np.float32)
    sh_p = np.stack([np.where(tpos % 256 == 0, 0.0, 0.5), np.where(tpos % 256 == 255, 0.0, 0.5)]).astype(np.float32)
    sh_s = f(np.broadcast_to(sh_s[:, None, :], (2, 128, 2048)))
    sh_p = f(np.broadcast_to(sh_p[:, None, :], (2, 128, 2048)))
    wrkv_, wor_, g1_, g2_ = f(rwkv_w_rkv), f(rwkv_w_o), f(rwkv_g1), f(rwkv_g2)
    st_all = np.asarray(state_rwkv, dtype=np.float32)

    in_maps = []
    for core in range(8):
        if core < 4:
            b = core
            x = x_sample[b]
            cond = c[b]
            kT = np.concatenate([np.asarray(cache_k_gqa)[b].reshape(2, 512, 128).transpose(0, 2, 1),
                                 np.asarray(cache_k_diff)[b].reshape(2, 512, 512).transpose(0, 2, 1)], axis=1)
            v = np.concatenate([np.asarray(cache_v_gqa)[b].reshape(2, 512, 128),
                                np.asarray(cache_v_diff)[b].reshape(2, 512, 512)], axis=2)
            cs, sn, mk = cos_s, sin_s, mask_s
            st0 = np.zeros((2, 2, 8, 128, 128), np.float32)
            for p_ in range(8):
                for h_ in range(2):
                    st0[:, :, p_, h_ * 64:(h_ + 1) * 64, h_ * 64:(h_ + 1) * 64] = st_all[b][:, :, 2 * p_ + h_].transpose(0, 1, 3, 2)
            keep, shm = np.ones((128, 1), np.float32), sh_s
        else:
            st0 = np.zeros((2, 2, 8, 128, 128), np.float32)
            keep, shm = np.zeros((128, 1), np.float32), sh_p
            p0 = (core - 4) * 8
            x = x_prompt[p0:p0 + 8].reshape(2048, 1024)
            cond = np.asarray(c_ctx)
            kT = np.zeros((2, 640, 512), np.float32)
            v = np.zeros((2, 512, 640), np.float32)
            cs, sn, mk = cos_p, sin_p, mask_p
        in_maps.append({
            "xT": f(x.T), "cond": f(np.asarray(cond).reshape(8, 128).T), "b_ada": bada, "gains": gains,
            "w_ada": w_ada_, "w_in": w_in, "w_out": w_out, "gqk": gqk, "lamv": lamv, "subln": subln,
            "kTc": f(kT), "vc": f(v), "maskb": f(mk.reshape(128, 160)), "cos": f(cs), "sin": f(sn),
            "mlp_w1": w1_, "mlp_w2": w2_, "ident": ident,
            "wrkv": wrkv_, "wo_r": wor_, "w1c": w1c, "a1c": a1c, "g1": g1_, "w2x": w2x, "a2x": a2x, "g2": g2_,
            "rmu": rmu, "rvecs": rvecs, "st0": st0, "keepf": keep, "shiftm": shm, "bones": bones, "rmask": rmask, "bd16": bd16,
        })
    if _PROG.get('dbg_cores'):
        ncd = _PROG['dbg_cores']
        return run_bass_kernel_spmd(nc, [in_maps[i] for i in ncd], core_ids=list(range(len(ncd)))).results
    res = run_bass_kernel_spmd(nc, in_maps, core_ids=list(range(8))).results
    _PROG['last'] = res

    y_sample = np.stack([res[b]["yT"].T for b in range(4)], axis=0).astype(np.float32)
    y_prompt = np.concatenate([res[4 + g]["yT"].T.reshape(8, 256, 1024) for g in range(4)], axis=0).astype(np.float32)
    okv = np.concatenate([res[4 + g]["okv"].reshape(2, 8, 256, 1280).transpose(1, 0, 2, 3) for g in range(4)], axis=0)
    new_k_gqa = np.ascontiguousarray(okv[..., 0:128].reshape(32, 2, 256, 2, 64)).astype(np.float32)
    new_k_diff = np.ascontiguousarray(okv[..., 128:640].reshape(32, 2, 256, 4, 2, 64)).astype(np.float32)
    new_v_gqa = np.ascontiguousarray(okv[..., 640:768].reshape(32, 2, 256, 2, 64)).astype(np.float32)
    new_v_diff = np.ascontiguousarray(okv[..., 768:1280].reshape(32, 2, 256, 4, 128)).astype(np.float32)
    new_state = np.zeros((32, 2, 2, 16, 64, 64), np.float32)
    for g in range(4):
        os_ = res[4 + g]["ostate"]
        for p_ in range(8):
            for h_ in range(2):
                blkv = os_[:, :, :, p_, h_ * 64:(h_ + 1) * 64, h_ * 64:(h_ + 1) * 64]
                new_state[8 * g:8 * g + 8, :, :, 2 * p_ + h_] = blkv.transpose(1, 0, 2, 4, 3)
    return (y_prompt, y_sample, new_k_gqa, new_v_gqa, new_k_diff, new_v_diff, new_state)
```
